# Optimizing a Trainium2 kernel written in Bass

```python
import jax
import jax.numpy as jnp
from jax import lax
import numpy as np

D_MODEL = 2048
BATCH = 2
SEQ = 8192
DEPTH = 4

GRID_W = 64
CTX_LEN = 256
EPS = 1e-6
LOG_FLOOR = 1e-30
N_EVEN = (DEPTH + 1) // 2
N_ODD = DEPTH // 2
N_MOD = 6

A_HEADS = 8
A_DK = 128
A_DV = D_MODEL // 16
A_KWIDTH = A_HEADS * A_DK
A_WIDTH = A_HEADS * A_DV
SCAN_CHUNK = 64

B_GROUPS = 8
B_CH = D_MODEL // 16
B_WIDTH = B_GROUPS * B_CH
B_CHUNK = 128

Q_OFF = 0
FF_OFF = Q_OFF + A_KWIDTH
FB_OFF = FF_OFF + A_KWIDTH
I_OFF = FB_OFF + A_KWIDTH
G_OFF = I_OFF + A_WIDTH
U_OFF = G_OFF + A_WIDTH
V_OFF = U_OFF + B_WIDTH
IN_COLS = V_OFF + B_WIDTH
MIX_WIDTH = A_WIDTH + B_WIDTH

C_GROUPS = 4
C_CH = D_MODEL // C_GROUPS
POOL_WINDOWS = (2, 4, 8, 16)

D_FF = 4 * D_MODEL

kernel_name = 'hybrid_hgrn2_chunkmlp_pool_dit'


def rms_norm(x, g):
    xf = x.astype(jnp.float32)
    y = xf * lax.rsqrt(jnp.mean(xf * xf, axis=-1, keepdims=True) + EPS)
    return (y * g.astype(jnp.float32)).astype(x.dtype)


def modulate(h, shift, scale):
    return h * (1.0 + scale[:, None, :]) + shift[:, None, :]


def ada_mods(cvec, w, b):
    m = jax.nn.silu(cvec) @ w + b
    return jnp.split(m, N_MOD, axis=-1)


def split_heads(t, dh):
    return t.reshape(t.shape[:-1] + (t.shape[-1] // dh, dh))


def flip_seq(t):
    return jnp.flip(t, axis=1)


def lower_bounds(lb_logits):
    p = jax.nn.softmax(lb_logits.astype(jnp.float32), axis=1)
    return jnp.cumsum(p, axis=1) - p[:, :1]


def hgrn_gates(f_pre, lb):
    f = f_pre.astype(jnp.float32)
    fg = lb + (1.0 - lb) * jax.nn.sigmoid(f)
    log_f = jnp.log(jnp.maximum(fg, LOG_FLOOR))
    k = (1.0 - lb) * jax.nn.sigmoid(-f)
    return split_heads(k, A_DK), split_heads(log_f, A_DK)


def gla_chunk_scan(q, k, v, log_f, s0):
    bsz, seq_len, n_heads, _ = q.shape
    dv = v.shape[-1]
    n_chunks = seq_len // SCAN_CHUNK

    def to_chunks(t):
        return t.reshape(bsz, n_chunks, SCAN_CHUNK, n_heads, t.shape[-1]).transpose(1, 0, 3, 2, 4)

    lower = jnp.tril(jnp.ones((SCAN_CHUNK, SCAN_CHUNK), dtype=bool))[None, None, :, :, None]

    def step(state, blk):
        qb, kb, vb, gb = blk
        b = jnp.cumsum(gb, axis=2)
        o_inter = jnp.einsum('bhtk,bhkv->bhtv', qb * jnp.exp(b), state)
        diff = b[:, :, :, None, :] - b[:, :, None, :, :]
        decay = jnp.where(lower, jnp.exp(jnp.minimum(diff, 0.0)), 0.0)
        scores = jnp.einsum('bhtk,bhsk,bhtsk->bhts', qb, kb, decay)
        o_intra = jnp.einsum('bhts,bhsv->bhtv', scores, vb)
        b_last = b[:, :, -1:, :]
        new_state = (jnp.exp(b_last[:, :, 0, :])[..., None] * state
                     + jnp.einsum('bhsk,bhsv->bhkv', kb * jnp.exp(b_last - b), vb))
        return new_state, o_inter + o_intra

    s_fin, o = lax.scan(step, s0, (to_chunks(q), to_chunks(k), to_chunks(v), to_chunks(log_f)))
    o = o.transpose(1, 0, 3, 2, 4).reshape(bsz, seq_len, n_heads, dv)
    return o, s_fin


def gla_final_state(k, v, log_f):
    b = jnp.cumsum(log_f, axis=1)
    w = jnp.exp(b[:, -1:] - b)
    return jnp.einsum('blhk,blhv->bhkv', k * w, v)


def hgrn_stream(q, f_fwd, f_bwd, i, lb_f, lb_b, s0_f, s0_b):
    kf, gf = hgrn_gates(f_fwd, lb_f)
    kb, gb = hgrn_gates(f_bwd, lb_b)
    qh = split_heads(q.astype(jnp.float32), A_DK)
    vh = split_heads(i.astype(jnp.float32), A_DV)
    o_f, s_f = gla_chunk_scan(qh, kf, vh, gf, s0_f)
    o_b, s_b = gla_chunk_scan(flip_seq(qh), flip_seq(kb), flip_seq(vh), flip_seq(gb), s0_b)
    return o_f + flip_seq(o_b), s_f, s_b


def hgrn_final_states(f_fwd, f_bwd, i, lb_f, lb_b):
    kf, gf = hgrn_gates(f_fwd, lb_f)
    kb, gb = hgrn_gates(f_bwd, lb_b)
    vh = split_heads(i.astype(jnp.float32), A_DV)
    s_f = gla_final_state(kf, vh, gf)
    s_b = gla_final_state(flip_seq(kb), flip_seq(vh), flip_seq(gb))
    return s_f, s_b


def hgrn_readout(o, g, norm_g):
    o = o * lax.rsqrt(jnp.mean(o * o, axis=-1, keepdims=True) + EPS)
    o = o.reshape(o.shape[:2] + (A_WIDTH,)) * norm_g.astype(jnp.float32)
    return (o * jax.nn.silu(g.astype(jnp.float32))).astype(g.dtype)


def chunk_token_mlp(u, v, ws, bias, vnorm_g):
    bsz, seq_len, _ = u.shape
    u32 = jax.nn.gelu(u.astype(jnp.float32), approximate=False)
    vg = split_heads(jax.nn.gelu(v.astype(jnp.float32), approximate=False), B_CH)
    mu = jnp.mean(vg, axis=-1, keepdims=True)
    var = jnp.mean(jnp.square(vg - mu), axis=-1, keepdims=True)
    vn = (vg - mu) * lax.rsqrt(var + EPS) * split_heads(vnorm_g.astype(jnp.float32), B_CH)
    vc = vn.reshape(bsz, seq_len // B_CHUNK, B_CHUNK, B_GROUPS, B_CH)
    mixed = (jnp.einsum('gts,bnsgc->bntgc', ws.astype(jnp.float32), vc)
             + bias.astype(jnp.float32).T[:, :, None])
    return (u32 * mixed.reshape(bsz, seq_len, B_WIDTH)).astype(v.dtype)


def even_mixer(h_lat, h_ctx, w_in, w_out, lb_f, lb_b, a_norm_g, b_ws, b_bias, b_vnorm_g, ctx_out_needed):
    bsz = h_ctx.shape[0]
    if ctx_out_needed:
        zero = jnp.zeros((bsz, A_HEADS, A_DK, A_DV), jnp.float32)
        pc = h_ctx @ w_in
        oc, s_f, s_b = hgrn_stream(pc[..., Q_OFF:FF_OFF], pc[..., FF_OFF:FB_OFF], pc[..., FB_OFF:I_OFF],
                                   pc[..., I_OFF:G_OFF], lb_f, lb_b, zero, zero)
        yc = jnp.concatenate([hgrn_readout(oc, pc[..., G_OFF:U_OFF], a_norm_g),
                              chunk_token_mlp(pc[..., U_OFF:V_OFF], pc[..., V_OFF:IN_COLS], b_ws, b_bias, b_vnorm_g)],
                             axis=-1) @ w_out
    else:
        pc = h_ctx @ w_in[:, FF_OFF:G_OFF]
        s_f, s_b = hgrn_final_states(pc[..., :A_KWIDTH], pc[..., A_KWIDTH:2 * A_KWIDTH],
                                     pc[..., 2 * A_KWIDTH:], lb_f, lb_b)
        yc = None
    pl = h_lat @ w_in
    ol, _, _ = hgrn_stream(pl[..., Q_OFF:FF_OFF], pl[..., FF_OFF:FB_OFF], pl[..., FB_OFF:I_OFF],
                           pl[..., I_OFF:G_OFF], lb_f, lb_b, s_f, s_b)
    yl = jnp.concatenate([hgrn_readout(ol, pl[..., G_OFF:U_OFF], a_norm_g),
                          chunk_token_mlp(pl[..., U_OFF:V_OFF], pl[..., V_OFF:IN_COLS], b_ws, b_bias, b_vnorm_g)],
                         axis=-1) @ w_out
    return yl, yc


def window_bounds(n, k):
    t = np.arange(n)
    return np.clip(t - k // 2, 0, n), np.clip(t - k // 2 + k, 0, n)


def pool_minus_self_2d(x, k):
    bsz, seq_len, ch = x.shape
    rows = seq_len // GRID_W
    g = x.reshape(bsz, rows, GRID_W, ch)
    sat = jnp.pad(jnp.cumsum(jnp.cumsum(g, axis=1), axis=2), ((0, 0), (1, 0), (1, 0), (0, 0)))
    rlo, rhi = window_bounds(rows, k)
    clo, chi = window_bounds(GRID_W, k)

    def corner(r, cidx):
        return sat[:, r][:, :, cidx]

    s = corner(rhi, chi) - corner(rlo, chi) - corner(rhi, clo) + corner(rlo, clo)
    cnt = ((rhi - rlo)[:, None] * (chi - clo)[None, :]).astype(np.float32)
    return (s / cnt[None, :, :, None] - g).reshape(bsz, seq_len, ch)


def pool_minus_self_1d(x, k):
    seq_len = x.shape[1]
    cs = jnp.pad(jnp.cumsum(x, axis=1), ((0, 0), (1, 0), (0, 0)))
    lo, hi = window_bounds(seq_len, k)
    return (cs[:, hi] - cs[:, lo]) / (hi - lo).astype(np.float32)[None, :, None] - x


def pool_mixer(h, w_in, w_grp, b_grp, scale, on_grid):
    p = (h @ w_in).astype(jnp.float32)
    pool = pool_minus_self_2d if on_grid else pool_minus_self_1d
    z = jnp.stack([pool(p[..., gi * C_CH:(gi + 1) * C_CH], k) for gi, k in enumerate(POOL_WINDOWS)], axis=2)
    y = jnp.einsum('blgc,gcd->blgd', z, w_grp.astype(jnp.float32)) + b_grp.astype(jnp.float32)
    return (y.reshape(h.shape[:2] + (D_MODEL,)) * scale.astype(jnp.float32)).astype(h.dtype)


def sq_relu_mlp(h, w1, w2):
    return jnp.square(jax.nn.relu(h @ w1)) @ w2


def setup_inputs(seed: int = 0) -> dict:
    key = jax.random.key(seed)
    ks = jax.random.split(key, 22)

    def nrm(k, shape, s):
        return jax.random.normal(k, shape, jnp.float32) * s

    d = D_MODEL
    return {
        'x': nrm(ks[0], (BATCH, SEQ, d), 1.0),
        'c': nrm(ks[1], (BATCH, d), 1.0),
        'ctx': nrm(ks[2], (BATCH, CTX_LEN, d), 1.0),
        'c_ctx': nrm(ks[3], (d,), 1.0),
        'w_ada': nrm(ks[4], (DEPTH, d, N_MOD * d), 0.5 * d ** -0.5),
        'b_ada': nrm(ks[5], (DEPTH, N_MOD * d), 0.01),
        'g_norm_mix': 1.0 + nrm(ks[6], (DEPTH, d), 0.05),
        'g_norm_ffn': 1.0 + nrm(ks[7], (DEPTH, d), 0.05),
        'w_in_even': nrm(ks[8], (N_EVEN, d, IN_COLS), d ** -0.5),
        'w_out_even': nrm(ks[9], (N_EVEN, MIX_WIDTH, d), MIX_WIDTH ** -0.5),
        'lb_logits': nrm(ks[10], (2, N_EVEN, A_KWIDTH), 0.5),
        'g_hgrn_out': 1.0 + nrm(ks[11], (N_EVEN, A_WIDTH), 0.05),
        'w_spatial': nrm(ks[12], (N_EVEN, B_GROUPS, B_CHUNK, B_CHUNK), B_CHUNK ** -0.5),
        'b_spatial': 1.0 + nrm(ks[13], (N_EVEN, B_GROUPS, B_CHUNK), 0.02),
        'g_spatial_v': 1.0 + nrm(ks[14], (N_EVEN, B_WIDTH), 0.05),
        'w_in_pool': nrm(ks[15], (N_ODD, d, d), d ** -0.5),
        'w_grp_pool': nrm(ks[16], (N_ODD, C_GROUPS, C_CH, C_CH), C_CH ** -0.5),
        'b_grp_pool': nrm(ks[17], (N_ODD, C_GROUPS, C_CH), 0.01),
        'scale_pool': 1.0 + nrm(ks[18], (N_ODD, d), 0.05),
        'w_ffn_up': nrm(ks[19], (DEPTH, d, D_FF), d ** -0.5),
        'w_ffn_down': nrm(ks[20], (DEPTH, D_FF, d), D_FF ** -0.5),
        'g_norm_final': 1.0 + nrm(ks[21], (d,), 0.05),
    }


def reference(x, c, ctx, c_ctx, w_ada, b_ada, g_norm_mix, g_norm_ffn, w_in_even, w_out_even, lb_logits,
              g_hgrn_out, w_spatial, b_spatial, g_spatial_v, w_in_pool, w_grp_pool, b_grp_pool, scale_pool,
              w_ffn_up, w_ffn_down, g_norm_final):
    lbs = lower_bounds(lb_logits)
    last_even = DEPTH - 1 if (DEPTH - 1) % 2 == 0 else DEPTH - 2
    c_ctx_row = c_ctx[None, :]
    ctx_s = ctx
    for layer in range(DEPTH):
        ctx_out_needed = layer < last_even
        ctx_read = layer <= last_even
        sh1, sc1, gt1, sh2, sc2, gt2 = ada_mods(c, w_ada[layer], b_ada[layer])
        hl = modulate(rms_norm(x, g_norm_mix[layer]), sh1, sc1)
        hc = None
        if ctx_read:
            csh1, csc1, cgt1, csh2, csc2, cgt2 = ada_mods(c_ctx_row, w_ada[layer], b_ada[layer])
            hc = modulate(rms_norm(ctx_s, g_norm_mix[layer]), csh1, csc1)
        if layer % 2 == 0:
            e = layer // 2
            yl, yc = even_mixer(hl, hc, w_in_even[e], w_out_even[e], lbs[0, e], lbs[1, e], g_hgrn_out[e],
                                w_spatial[e], b_spatial[e], g_spatial_v[e], ctx_out_needed)
        else:
            o = layer // 2
            yl = pool_mixer(hl, w_in_pool[o], w_grp_pool[o], b_grp_pool[o], scale_pool[o], True)
            yc = pool_mixer(hc, w_in_pool[o], w_grp_pool[o], b_grp_pool[o], scale_pool[o], False) if ctx_out_needed else None
        x = x + gt1[:, None, :] * yl
        x = x + gt2[:, None, :] * sq_relu_mlp(modulate(rms_norm(x, g_norm_ffn[layer]), sh2, sc2),
                                              w_ffn_up[layer], w_ffn_down[layer])
        if ctx_out_needed:
            ctx_s = ctx_s + cgt1[:, None, :] * yc
            ctx_s = ctx_s + cgt2[:, None, :] * sq_relu_mlp(modulate(rms_norm(ctx_s, g_norm_ffn[layer]), csh2, csc2),
                                                          w_ffn_up[layer], w_ffn_down[layer])
    return rms_norm(x, g_norm_final)
```

```python
import contextlib
import numpy as np
import concourse.bass as bass
import concourse.mybir as mybir
from concourse.bass_utils import run_bass_kernel_spmd

F32 = mybir.dt.float32
BF16 = mybir.dt.bfloat16
AF = mybir.ActivationFunctionType
ALU = mybir.AluOpType

D = 2048
KC = 16
NT = 2048
NCX = 256
DEPTH = 4
DFF = 8192
EPS = 1e-6
NCORES = 8

SEM_ROT = 16000
N_DMA_SEM = 12
SAME_ENGINE_SYNC = True
DEBUG_CUT = 0
FFN_PRECAST = True
DEBUG_DUMP = False


class Prog:
    ENG = ("pe", "dve", "act", "pool", "sp")

    def __init__(self, nc):
        self.nc = nc
        self.stream = {e: [] for e in self.ENG}
        self.count = {e: 0 for e in self.ENG}
        self.seen = {e: {} for e in self.ENG}
        self.last_w = {}
        self.readers = {}
        self.dma_n = {"sp": 0, "pool": 0, "act": 0}
        self.dma_val = {}
        self.sem_keys = set()
        self.cc_n = 0
        self.last_ticket = {}

    def _deps(self, eng, reads, writes):
        deps = []
        for r in reads:
            t = self.last_w.get(r)
            if t is not None:
                deps.append(t)
        for w in writes:
            t = self.last_w.get(w)
            if t is not None:
                deps.append(t)
            deps.extend(self.readers.get(w, ()))
        best = {}
        for (sk, val, e) in deps:
            if e == eng and (eng == "pe" or not SAME_ENGINE_SYNC) and sk[0] == "c":
                continue
            if best.get(sk, 0) < val:
                best[sk] = val
        out = []
        for sk, val in best.items():
            if self.seen[eng].get(sk, 0) >= val:
                continue
            self.seen[eng][sk] = val
            out.append((sk, val))
        return out

    def _commit(self, ticket, reads, writes):
        for r in reads:
            self.readers.setdefault(r, []).append(ticket)
        for w in writes:
            self.last_w[w] = ticket
            self.readers[w] = []
        self.last_ticket[ticket[0]] = ticket

    def op(self, eng, fn, reads=(), writes=()):
        waits = self._deps(eng, reads, writes)
        self.count[eng] += 1
        n = self.count[eng]
        sk = ("c", eng, (n - 1) // SEM_ROT)
        val = (n - 1) % SEM_ROT + 1
        self.sem_keys.add(sk)
        ticket = (sk, val, eng)
        for w in waits:
            self.stream[eng].append(("wait", w))
        self.stream[eng].append(("op", fn, (sk, 1)))
        self._commit(ticket, reads, writes)
        return ticket

    def dma(self, q, fn, reads=(), writes=()):
        waits = self._deps(q, reads, writes)
        i = self.dma_n[q]
        self.dma_n[q] += 1
        sk = ("d", q, i % N_DMA_SEM)
        self.sem_keys.add(sk)
        prev = self.dma_val.get(sk, 0)
        if prev > 0 and self.seen[q].get(sk, 0) < prev:
            self.seen[q][sk] = prev
            waits.append((sk, prev))
        val = prev + 16
        self.dma_val[sk] = val
        ticket = (sk, val, "dma_" + q)
        for w in waits:
            self.stream[q].append(("wait", w))
        self.stream[q].append(("op", fn, (sk, 16)))
        self._commit(ticket, reads, writes)
        return ticket

    def cc(self, fn, reads=(), writes=()):
        q = "pool"
        waits = self._deps(q, reads, writes)
        self.cc_n += 1
        sk = ("k", "cc", 0)
        self.sem_keys.add(sk)
        ticket = (sk, self.cc_n, "cc")
        for w in waits:
            self.stream[q].append(("wait", w))
        self.stream[q].append(("cc", fn, sk))
        self.seen[q][sk] = self.cc_n
        self.stream[q].append(("wait", (sk, self.cc_n)))
        self._commit(ticket, reads, writes)
        return ticket

    def barrier(self):
        tickets = list(self.last_ticket.values())
        for eng in self.ENG:
            for (sk, val, e) in tickets:
                if e == eng and sk[0] == "c":
                    continue
                if self.seen[eng].get(sk, 0) < val:
                    self.seen[eng][sk] = val
                    self.stream[eng].append(("wait", (sk, val)))

    def emit(self):
        nc = self.nc
        with contextlib.ExitStack() as st:
            sems = {}
            for sk in sorted(self.sem_keys):
                sems[sk] = st.enter_context(nc.semaphore("s_" + "_".join(str(x) for x in sk)))
            block = st.enter_context(nc.Block())

            def run(engname):
                def body(e):
                    for item in self.stream[engname]:
                        if item[0] == "wait":
                            sk, val = item[1]
                            e.wait_ge(sems[sk], val)
                        elif item[0] == "cc":
                            item[1](e).then_inc(sems[item[2]])
                        else:
                            _, fn, (sk, inc) = item
                            fn(e).then_inc(sems[sk], inc)
                return body

            block.sync(run("sp"))
            block.tensor(run("pe"))
            block.vector(run("dve"))
            block.scalar(run("act"))
            block.gpsimd(run("pool"))


C_IDENT = 0
C_MDF, C_MQF, C_MDB, C_MQB = 1, 2, 3, 4
C_MCF, C_MCB = 5, 6
C_MASKF, C_MASKB = 7, 8
C_W2D = 9
C_W1D = 28
N_CMAT = 40
POOL_WINDOWS = (2, 4, 8, 16)
W2D_DELTAS = {2: (-1, 0), 4: (-1, 0, 1), 8: (-2, -1, 0, 1, 2), 16: (-4, -3, -2, -1, 0, 1, 2, 3, 4)}
W1D_DELTAS = (-1, 0, 1)
REF_F = 31
REF_B = 32


def _w2d_index():
    idx = {}
    n = C_W2D
    for k in POOL_WINDOWS:
        for dl in W2D_DELTAS[k]:
            idx[(k, dl)] = n
            n += 1
    return idx


def _w1d_index():
    idx = {}
    n = C_W1D
    for k in POOL_WINDOWS:
        for dl in W1D_DELTAS:
            idx[(k, dl)] = n
            n += 1
    return idx


W2D_IDX = _w2d_index()
W1D_IDX = _w1d_index()


def make_cmat():
    cm = np.zeros((N_CMAT, 128, 128), np.float32)
    cm[C_IDENT] = np.eye(128, dtype=np.float32)
    s = np.arange(128)[:, None]
    t = np.arange(128)[None, :]
    same = (s // 64) == (t // 64)
    js, jt = s % 64, t % 64
    mdf = ((js <= REF_F).astype(np.float32) - (js <= jt).astype(np.float32)) * same
    cm[C_MDF] = mdf
    cm[C_MQF] = -mdf
    mdb = ((js >= REF_B).astype(np.float32) - (js >= jt).astype(np.float32)) * same
    cm[C_MDB] = mdb
    cm[C_MQB] = -mdb
    for c in range(2):
        inc = (np.arange(128) // 64 == c)
        j = np.arange(128) % 64
        cm[C_MCF][:, 3 * c + 0] = inc * (j <= REF_F)
        cm[C_MCF][:, 3 * c + 1] = inc * (j > REF_F)
        cm[C_MCF][:, 3 * c + 2] = inc
        cm[C_MCB][:, 3 * c + 0] = inc * (j >= REF_B)
        cm[C_MCB][:, 3 * c + 1] = inc * (j < REF_B)
        cm[C_MCB][:, 3 * c + 2] = inc
    s64 = np.arange(64)[:, None]
    t64 = np.arange(64)[None, :]
    cm[C_MASKF][:64, 0:64] = (s64 <= t64)
    cm[C_MASKF][:64, 64:128] = (s64 <= t64)
    cm[C_MASKB][:64, 0:64] = (s64 >= t64)
    cm[C_MASKB][:64, 64:128] = (s64 >= t64)
    for k in POOL_WINDOWS:
        for dl in W2D_DELTAS[k]:
            rs = 2 * dl + (np.arange(128) // 64)[:, None]
            cs = (np.arange(128) % 64)[:, None]
            rt = (np.arange(128) // 64)[None, :]
            ct = (np.arange(128) % 64)[None, :]
            inr = (rs >= rt - k // 2) & (rs < rt - k // 2 + k)
            inc_ = (cs >= ct - k // 2) & (cs < ct - k // 2 + k)
            cm[W2D_IDX[(k, dl)]] = (inr & inc_).astype(np.float32)
        for dl in W1D_DELTAS:
            ps = 128 * dl + np.arange(128)[:, None]
            pt = np.arange(128)[None, :]
            cm[W1D_IDX[(k, dl)]] = ((ps >= pt - k // 2) & (ps < pt - k // 2 + k)).astype(np.float32)
    return cm


def window_bounds(n, k):
    t = np.arange(n)
    return np.clip(t - k // 2, 0, n), np.clip(t - k // 2 + k, 0, n)


def make_invcnt(seg):
    rows = 128
    inv2 = np.zeros((NT, 4), np.float32)
    inv1 = np.zeros((NCX, 4), np.float32)
    for gi, k in enumerate(POOL_WINDOWS):
        rlo, rhi = window_bounds(rows, k)
        clo, chi = window_bounds(64, k)
        cnt = ((rhi - rlo)[:, None] * (chi - clo)[None, :]).astype(np.float32)
        inv2[:, gi] = (1.0 / cnt[seg * 32:(seg + 1) * 32]).reshape(-1)
        lo, hi = window_bounds(NCX, k)
        inv1[:, gi] = 1.0 / (hi - lo).astype(np.float32)
    return inv2, inv1


class Builder:
    def __init__(self, stages):
        self.stages = list(stages)
        nc = self.nc = bass.Bass("TRN2", target_bir_lowering=False)
        self.P = Prog(nc)
        dt = nc.dram_tensor
        need = needed_inputs(self.stages)
        self.need = need

        def inp(name, shape, dtype=F32):
            if name not in need:
                return None
            return dt(name, list(shape), dtype, kind="ExternalInput").ap()

        def outp(name, shape, dtype=F32):
            if name not in need:
                return None
            return dt(name, list(shape), dtype, kind="ExternalOutput").ap()

        self.xT = inp("xT", [D, NT])
        self.ctxT = inp("ctxT", [D, NCX])
        self.cvec = inp("cvec", [128, KC, 2])
        self.w_ada = inp("w_ada", [DEPTH, D, 6 * D])
        self.b_ada = inp("b_ada", [DEPTH, 128, 96])
        self.mods_in = inp("mods_in", [128, DEPTH, 96, 2])
        self.gmix = inp("gmix", [128, DEPTH, KC])
        self.gffn = inp("gffn", [128, DEPTH, KC])
        self.gfin = inp("gfin", [128, KC])
        self.w_up = inp("w_up", [D, DFF])
        self.w_down = inp("w_down", [DFF, D])
        self.w_in_even = inp("w_in_even", [D, 7168])
        self.w_out_even = inp("w_out_even", [D, D])
        self.lb_logits = inp("lb_logits", [2, 2, 1024])
        self.g_hgrn = inp("g_hgrn", [128, 2, 8])
        self.wsT = inp("wsT", [8, 128, 128])
        self.b_sp = inp("b_sp", [8, 128])
        self.g_spv = inp("g_spv", [1024])
        self.w_in_pool = inp("w_in_pool", [D, D])
        self.w_grp = inp("w_grp", [4, 512, 512])
        self.b_grp = inp("b_grp", [128, 2, KC])
        self.sc_pool = inp("sc_pool", [128, 2, KC])
        self.cmat = inp("cmat", [N_CMAT, 128, 128])
        self.inv2 = inp("inv2", [128, 16, 4])
        self.inv1 = inp("inv1", [128, 2, 4])
        self.coef = inp("coef", [128, 4, NCORES])
        self.sc_in = inp("sc_in", [128, 16, 128])
        self.pack_in = inp("pack_in", [NCORES * 2048, 132])
        self.halo_in = inp("halo_in", [NCORES * 1024, D], BF16)
        self.yT = outp("yT", [D, NT])
        self.cres_out = outp("cres_out", [D, NCX])
        self.mods_out = outp("mods_out", [128, DEPTH, 96, 2])
        self.sc_out = outp("sc_out", [128, 16, 128])
        self.pack_out = outp("pack_out", [2048, 132])
        self.halo_out = outp("halo_out", [1024, D], BF16)
        self.p_out = outp("p_out", [NT, D], BF16)
        self.p_in = inp("p_in", [NT, D], BF16)
        self.csc_out = outp("csc_out", [128, 2 * 8 * (NT // 64) * 3])
        self.csc_in = inp("csc_in", [128, 2 * 8 * (NT // 64) * 3])
        self.xres = dt("xres", [D, NT], F32).ap()
        self.cres = dt("cres", [D, NCX], F32).ap()
        self.st = contextlib.ExitStack()
        self.ps_rr = 0

    _uid = 0

    def sb(self, stack, name, shape, dtype):
        Builder._uid += 1
        return stack.enter_context(self.nc.sbuf_tensor(f"{name}_u{Builder._uid}", list(shape), dtype))

    def next_ps(self):
        i = self.ps_rr % 7
        self.ps_rr += 1
        return self.ps[i], f"ps{i}"

    def build(self):
        nc, P = self.nc, self.P
        with self.st as st:
            self.ps = [st.enter_context(nc.psum_tensor(f"ps{i}", [128, 512], F32)) for i in range(7)]
            self.pst = st.enter_context(nc.psum_tensor("pst", [128, 1024], BF16))
            self.ident = self.sb(st, "ident", [128, 128], BF16)
            self.ones = self.sb(st, "ones", [128, 128], BF16)
            self.mods = self.sb(st, "mods", [128, DEPTH, 96, 2], F32)
            self.A1 = self.sb(st, "A1", [128, DEPTH, KC, 2], F32)
            self.A2 = self.sb(st, "A2", [128, DEPTH, KC, 2], F32)
            self.gm = self.sb(st, "gm", [128, DEPTH, KC], F32)
            self.gf = self.sb(st, "gf", [128, DEPTH, KC], F32)
            self.gfin_sb = self.sb(st, "gfin_sb", [128, KC], F32)
            P.dma("pool", lambda e: e.dma_start(out=self.ident[:], in_=self.cmat[C_IDENT]), writes=["ident"])
            P.op("dve", lambda e: e.memset(self.ones[:], 1.0), writes=["ones"])
            self.epsb = self.sb(st, "epsb", [128, 1], F32)
            P.op("dve", lambda e: e.memset(self.epsb[:], EPS), writes=["epsb"])
            P.dma("sp", lambda e: e.dma_start(out=self.gm[:], in_=self.gmix), writes=["gm"])
            P.dma("sp", lambda e: e.dma_start(out=self.gf[:], in_=self.gffn), writes=["gf"])
            P.dma("sp", lambda e: e.dma_start(out=self.gfin_sb[:], in_=self.gfin), writes=["gfin"])
            if "adaln" in self.stages:
                self.phase_adaln()
                P.dma("sp", lambda e: e.dma_start(out=self.mods_out, in_=self.mods[:]), reads=["mods"], writes=["dram_mods"])
            else:
                P.dma("sp", lambda e: e.dma_start(out=self.mods[:], in_=self.mods_in), writes=["mods"])
            self.mods_derived()
            self.cur_x, self.cur_c = self.xT, self.ctxT
            final = False
            if FFN_PRECAST and "ffn" in [t[0] for t in self.stages if not isinstance(t, str)] and self.stages[0][0] == "even_b":
                self.ffn_precast()
            for tok in self.stages:
                if tok == "adaln":
                    continue
                if tok == "final":
                    self.phase_final(self.cur_x)
                    final = True
                    continue
                kind, L = tok
                if kind == "even_a":
                    self.phase_even_a(L)
                elif kind == "even_b":
                    self.phase_even_b(L)
                elif kind == "odd_a":
                    self.phase_odd(L, part="a")
                elif kind == "odd_b":
                    self.phase_odd(L, part="b")
                elif kind == "ffn":
                    self.phase_ffn(L, self.cur_x, self.xres, NT, 0)
                    self.cur_x = self.xres
                    if L < 2:
                        self.phase_ffn(L, self.cur_c, self.cres, NCX, 1)
                        self.cur_c = self.cres
            if not final:
                self.phase_dump(self.cur_x, self.yT, NT)
                if self.cres_out is not None:
                    self.phase_dump(self.cur_c, self.cres_out, NCX)
            P.barrier()
            P.emit()
        return nc

    def ffn_precast(self):
        P = self.P
        self.wbf_up = self.nc.dram_tensor("wbf_up", [16, 128, KC, 512], BF16).ap()
        self.wbf_dn = self.nc.dram_tensor("wbf_dn", [16, 128, 4, D], BF16).ap()
        for hb in range(16):
            P.dma("pool", lambda e, hb=hb: e.dma_start(out=self.wbf_up[hb], in_=self.w_up[:, hb * 512:(hb + 1) * 512].rearrange("(kc p) m -> p kc m", p=128)),
                  writes=[f"wbf_up{hb}"])
            P.dma("pool", lambda e, hb=hb: e.dma_start(out=self.wbf_dn[hb], in_=self.w_down[hb * 512:(hb + 1) * 512, :].rearrange("(hc p) m -> p hc m", p=128)),
                  writes=[f"wbf_dn{hb}"])

    def mods_derived(self):
        P = self.P
        for L in range(DEPTH):
            for j in range(2):
                P.op("dve", lambda e, L=L, j=j: e.scalar_tensor_tensor(
                    out=self.A1[:, L, :, j], in0=self.mods[:, L, 16:32, j], scalar=1.0, in1=self.gm[:, L, :],
                    op0=ALU.add, op1=ALU.mult), reads=["mods", "gm"], writes=["A1"])
                P.op("dve", lambda e, L=L, j=j: e.scalar_tensor_tensor(
                    out=self.A2[:, L, :, j], in0=self.mods[:, L, 64:80, j], scalar=1.0, in1=self.gf[:, L, :],
                    op0=ALU.add, op1=ALU.mult), reads=["mods", "gf"], writes=["A2"])

    def phase_adaln(self):
        nc, P = self.nc, self.P
        with contextlib.ExitStack() as ph:
            cv = self.sb(ph, "cv", [128, KC, 2], F32)
            scv = self.sb(ph, "scv", [128, KC, 2], BF16)
            wb = [self.sb(ph, f"adw{i}", [128, KC, 512], BF16) for i in range(2)]
            bias = self.sb(ph, "adb", [128, DEPTH, 96], F32)
            P.dma("sp", lambda e: e.dma_start(out=cv[:], in_=self.cvec), writes=["cv"])
            P.dma("sp", lambda e: e.dma_start(out=bias[:], in_=self.b_ada.rearrange("l p c -> p l c")), writes=["adb"])
            P.op("act", lambda e: e.activation(out=scv[:], in_=cv[:], func=AF.Silu), reads=["cv"], writes=["scv"])
            n = 0
            for L in range(DEPTH):
                pt, pn = self.next_ps()
                for cb in range(24):
                    slot = n % 2
                    n += 1
                    w = wb[slot]
                    src = self.w_ada[L][:, cb * 512:(cb + 1) * 512].rearrange("(kc p) m -> p kc m", p=128)
                    P.dma("pool", lambda e, w=w, src=src: e.dma_start(out=w[:], in_=src), writes=[f"adw{slot}"])
                    for mc in range(4):
                        c = cb * 4 + mc
                        for kc in range(KC):
                            P.op("pe", lambda e, w=w, kc=kc, mc=mc, c=c, pt=pt: e.matmul(
                                pt[:, 2 * c:2 * c + 2], lhsT=w[:, kc, mc * 128:(mc + 1) * 128], rhs=scv[:, kc, :],
                                start=(kc == 0), stop=(kc == KC - 1)), reads=[f"adw{slot}", "scv"], writes=[pn])
                for j in range(2):
                    P.op("dve", lambda e, L=L, j=j, pt=pt: e.tensor_tensor(
                        out=self.mods[:, L, :, j], in0=pt[:, 0:192].rearrange("p (c j) -> p c j", j=2)[:, :, j],
                        in1=bias[:, L, :], op=ALU.add), reads=[pn, "adb"], writes=["mods"])
            P.barrier()

    def norm_mod(self, xt, xname, tt, A, B, hn, hname, sq, sqname, tmp, tmpname, rs, rsname):
        P = self.P
        for h in range(0, tt, 512):
            w = min(512, tt - h)
            P.op("act", lambda e, h=h, w=w: e.activation(out=sq[:, :, :w], in_=xt[:, :, h:h + w], func=AF.Square),
                 reads=[xname], writes=[sqname])
            pt, pn = self.next_ps()
            for kc in range(KC):
                P.op("pe", lambda e, kc=kc, w=w, pt=pt: e.matmul(pt[:, :w], lhsT=self.ones[:], rhs=sq[:, kc, :w],
                                                                 start=(kc == 0), stop=(kc == KC - 1)),
                     reads=["ones", sqname], writes=[pn])
            P.op("act", lambda e, w=w, pt=pt: e.activation(out=rs[:, :w], in_=pt[:, :w], func=AF.Sqrt, bias=self.epsb[:, 0:1],
                                                           scale=1.0 / D), reads=[pn], writes=[rsname])
            P.op("dve", lambda e, w=w: e.reciprocal(out=rs[:, :w], in_=rs[:, :w]), reads=[rsname], writes=[rsname])
            for kc in range(KC):
                tn = f"{tmpname}{kc % 2}"
                tb = tmp[kc % 2]
                P.op("dve", lambda e, kc=kc, tb=tb, h=h, w=w: e.scalar_tensor_tensor(
                    out=tb[:, :w], in0=xt[:, kc, h:h + w], scalar=A[:, kc:kc + 1], in1=rs[:, :w], op0=ALU.mult, op1=ALU.mult),
                    reads=[xname, rsname, "A1", "A2"], writes=[tn])
                P.op("act", lambda e, kc=kc, tb=tb, h=h, w=w: e.activation(
                    out=hn[:, kc, h:h + w], in_=tb[:, :w], func=AF.Identity, bias=B[:, kc:kc + 1], scale=1.0),
                    reads=[tn, "mods"], writes=[hname])

    def phase_ffn(self, L, src, dst, ntok, j):
        nc, P = self.nc, self.P
        TT = min(1024, ntok)
        with contextlib.ExitStack() as ph:
            xt = self.sb(ph, "f_x", [128, KC, TT], F32)
            hn = self.sb(ph, "f_hn", [128, KC, TT], BF16)
            w1 = [self.sb(ph, f"f_w1{i}", [128, KC, 512], BF16) for i in range(2)]
            w2 = [self.sb(ph, f"f_w2{i}", [128, 4, D], BF16) for i in range(2)]
            hid = [self.sb(ph, "f_hid0", [128, 4, TT], BF16)]
            sq = self.sb(ph, "f_sq", [128, KC, 512], BF16)
            rl = [self.sb(ph, f"f_rl{i}", [128, 512], F32) for i in range(2)]
            tmp = [self.sb(ph, f"f_tmp{i}", [128, 512], F32) for i in range(2)]
            rs = self.sb(ph, "f_rs", [128, 512], F32)
            A = self.A2[:, L, :, j]
            B = self.mods[:, L, 48:64, j]
            G = self.mods[:, L, 80:96, j]
            nblk = 0
            for t0 in range(0, ntok, TT):
                P.dma("sp", lambda e, t0=t0: e.dma_start(out=xt[:], in_=src[:, t0:t0 + TT].rearrange("(kc p) t -> p kc t", p=128)),
                      reads=["dram_x"], writes=["f_x"])
                self.norm_mod(xt, "f_x", TT, A, B, hn, "f_hn", sq, "f_sq", tmp, "f_tmp", rs, "f_rs")
                for hb in range(DFF // 512):
                    slot = nblk % 2
                    nblk += 1
                    a, b2, hd = w1[slot], w2[slot], hid[0]
                    if hasattr(self, "wbf_up"):
                        P.dma("sp", lambda e, a=a, hb=hb: e.dma_start(out=a[:], in_=self.wbf_up[hb]), reads=[f"wbf_up{hb}"], writes=[f"f_w1{slot}"])
                        P.dma("sp", lambda e, b2=b2, hb=hb: e.dma_start(out=b2[:], in_=self.wbf_dn[hb]), reads=[f"wbf_dn{hb}"], writes=[f"f_w2{slot}"])
                    else:
                        s1 = self.w_up[:, hb * 512:(hb + 1) * 512].rearrange("(kc p) m -> p kc m", p=128)
                        s2 = self.w_down[hb * 512:(hb + 1) * 512, :].rearrange("(hc p) m -> p hc m", p=128)
                        P.dma("pool", lambda e, a=a, s1=s1: e.dma_start(out=a[:], in_=s1), writes=[f"f_w1{slot}"])
                        P.dma("pool", lambda e, b2=b2, s2=s2: e.dma_start(out=b2[:], in_=s2), writes=[f"f_w2{slot}"])
                    k = 0
                    for hc in range(4):
                        for h in range(0, TT, 512):
                            w = min(512, TT - h)
                            pt, pn = self.next_ps()
                            for kc in range(KC):
                                P.op("pe", lambda e, a=a, kc=kc, hc=hc, h=h, w=w, pt=pt: e.matmul(
                                    pt[:, :w], lhsT=a[:, kc, hc * 128:(hc + 1) * 128], rhs=hn[:, kc, h:h + w],
                                    start=(kc == 0), stop=(kc == KC - 1)), reads=[f"f_w1{slot}", "f_hn"], writes=[pn])
                            r = rl[k % 2]
                            rn = f"f_rl{k % 2}"
                            k += 1
                            P.op("act", lambda e, r=r, w=w, pt=pt: e.activation(out=r[:, :w], in_=pt[:, :w], func=AF.Relu),
                                 reads=[pn], writes=[rn])
                            P.op("pool", lambda e, r=r, w=w, hd=hd, hc=hc, h=h: e.tensor_tensor(
                                out=hd[:, hc, h:h + w], in0=r[:, :w], in1=r[:, :w], op=ALU.mult), reads=[rn], writes=["f_hid0"])
                    for m in range(KC):
                        for h in range(0, TT, 512):
                            w = min(512, TT - h)
                            pt, pn = self.next_ps()
                            for hc in range(4):
                                P.op("pe", lambda e, b2=b2, hc=hc, m=m, h=h, w=w, hd=hd, pt=pt: e.matmul(
                                    pt[:, :w], lhsT=b2[:, hc, m * 128:(m + 1) * 128], rhs=hd[:, hc, h:h + w],
                                    start=(hc == 0), stop=(hc == 3)), reads=[f"f_w2{slot}", "f_hid0"], writes=[pn])
                            P.op("dve", lambda e, m=m, h=h, w=w, pt=pt: e.scalar_tensor_tensor(
                                out=xt[:, m, h:h + w], in0=pt[:, :w], scalar=G[:, m:m + 1], in1=xt[:, m, h:h + w],
                                op0=ALU.mult, op1=ALU.add), reads=[pn, "mods", "f_x"], writes=["f_x"])
                P.dma("sp", lambda e, t0=t0: e.dma_start(out=dst[:, t0:t0 + TT].rearrange("(kc p) t -> p kc t", p=128), in_=xt[:]),
                      reads=["f_x"], writes=["dram_x"])
            P.barrier()

    def phase_final(self, src):
        nc, P = self.nc, self.P
        TT = 512
        with contextlib.ExitStack() as ph:
            xt = [self.sb(ph, f"o_x{i}", [128, KC, TT], F32) for i in range(2)]
            sq = self.sb(ph, "o_sq", [128, KC, TT], BF16)
            rs = self.sb(ph, "o_rs", [128, TT], F32)
            for i, t0 in enumerate(range(0, NT, TT)):
                x = xt[i % 2]
                xn = f"o_x{i % 2}"
                P.dma("sp", lambda e, x=x, t0=t0: e.dma_start(out=x[:], in_=src[:, t0:t0 + TT].rearrange("(kc p) t -> p kc t", p=128)),
                      reads=["dram_x"], writes=[xn])
                P.op("act", lambda e, x=x: e.activation(out=sq[:], in_=x[:], func=AF.Square), reads=[xn], writes=["o_sq"])
                pt, pn = self.next_ps()
                for kc in range(KC):
                    P.op("pe", lambda e, kc=kc, pt=pt: e.matmul(pt[:, :], lhsT=self.ones[:], rhs=sq[:, kc, :], start=(kc == 0), stop=(kc == KC - 1)),
                         reads=["ones", "o_sq"], writes=[pn])
                P.op("act", lambda e, pt=pt: e.activation(out=rs[:], in_=pt[:], func=AF.Sqrt, bias=self.epsb[:, 0:1], scale=1.0 / D),
                     reads=[pn], writes=["o_rs"])
                P.op("dve", lambda e: e.reciprocal(out=rs[:], in_=rs[:]), reads=["o_rs"], writes=["o_rs"])
                for kc in range(KC):
                    P.op("dve", lambda e, x=x, kc=kc: e.scalar_tensor_tensor(out=x[:, kc, :], in0=x[:, kc, :], scalar=self.gfin_sb[:, kc:kc + 1],
                                                                             in1=rs[:], op0=ALU.mult, op1=ALU.mult),
                         reads=[xn, "o_rs", "gfin"], writes=[xn])
                P.dma("sp", lambda e, x=x, t0=t0: e.dma_start(out=self.yT[:, t0:t0 + TT].rearrange("(kc p) t -> p kc t", p=128), in_=x[:]),
                      reads=[xn], writes=["dram_y"])
            P.barrier()

    def phase_dump(self, src, dst, ntok):
        P = self.P
        with contextlib.ExitStack() as ph:
            W = min(512, ntok)
            xt = self.sb(ph, "d_x", [128, KC, W], F32)
            for t0 in range(0, ntok, W):
                P.dma("sp", lambda e, t0=t0: e.dma_start(out=xt[:], in_=src[:, t0:t0 + W].rearrange("(kc p) t -> p kc t", p=128)),
                      reads=["dram_x"], writes=["d_x"])
                P.dma("sp", lambda e, t0=t0: e.dma_start(out=dst[:, t0:t0 + W].rearrange("(kc p) t -> p kc t", p=128), in_=xt[:]),
                      reads=["d_x"], writes=["dram_y"])
            P.barrier()

    SCR_KEYS = ["QpT0", "QpT1", "KpT0", "KpT1", "Kp0", "Kp1", "V", "SGT", "YBT"]

    def scratch(self, n, tag, kind="Internal"):
        if not hasattr(self, "_scr"):
            self._scr = {}
        if tag not in self._scr:
            def mk(name, shape):
                if kind == "Internal":
                    return self.nc.dram_tensor(f"{name}_{tag}", shape, BF16).ap()
                return self.nc.dram_tensor(f"scr_{name}", shape, BF16, kind=kind).ap()
            fm, tm = [1024, n], [n, 1024]
            self._scr[tag] = dict(
                QpT=[mk("QpT0", fm), mk("QpT1", fm)], KpT=[mk("KpT0", fm), mk("KpT1", fm)],
                Kp=[mk("Kp0", tm), mk("Kp1", tm)], V=mk("V", tm), SGT=mk("SGT", fm), YBT=mk("YBT", fm))
        return self._scr[tag]

    def phase_even_a(self, L):
        nc, P = self.nc, self.P
        e = L // 2
        ctx_out = (L == 0)
        scr_l, scr_c = self.scratch(NT, "l", "ExternalOutput"), self.scratch(NCX, "c")
        with contextlib.ExitStack() as lay:
            csc_l = self.sb(lay, "csc_l", [128, 2, 8, NT // 64, 3], F32)
            csl_l = self.sb(lay, "csl_l", [128, 2, 8, NT // 64, 3], F32)
            csc_c = self.sb(lay, "csc_c", [128, 2, 8, NCX // 64, 3], F32)
            csl_c = self.sb(lay, "csl_c", [128, 2, 8, NCX // 64, 3], F32)
            self.even_local(L, e, self.cur_c, NCX, 1, scr_c, csc_c, csl_c, full=ctx_out)
            if DEBUG_CUT == 1 or DEBUG_CUT >= 11:
                return
            self.even_local(L, e, self.cur_x, NT, 0, scr_l, csc_l, csl_l, full=True)
            P.dma("sp", lambda e_: e_.dma_start(out=self.csc_out, in_=csc_l[:].rearrange("p d h c x -> p (d h c x)")), reads=["csc0"], writes=["dram_csc"])
            if DEBUG_CUT == 2:
                return
            with contextlib.ExitStack() as lay2:
                S_c = self.sb(lay2, "S_c", [128, 16, 128], F32)
                S_l = self.sb(lay2, "S_l", [128, 16, 128], F32)
                P.op("dve", lambda e_: e_.memset(S_c[:], 0.0), writes=[f"S_c_{i}" for i in range(16)])
                self.even_scan(L, e, scr_c, NCX, 1, csc_c, S_c, "S_c", ctx_out, self.cur_c, self.cres)
                if ctx_out:
                    self.cur_c = self.cres
                P.dma("sp", lambda e_: e_.dma_start(out=self.sc_out, in_=S_c[:]), reads=[f"S_c_{i}" for i in range(16)], writes=["dram_sc"])
                if DEBUG_CUT == 3:
                    P.barrier()
                    if DEBUG_DUMP:
                        self.debug_dump(scr_c, csc_c)
                    return
                P.op("dve", lambda e_: e_.memset(S_l[:], 0.0), writes=[f"S_l_{i}" for i in range(16)])
                self.even_scan(L, e, scr_l, NT, 0, csc_l, S_l, "S_l", False, None, None)
                self.even_pack(csl_l, S_l)
                P.barrier()

    def debug_dump(self, scr, csc):
        nc, P = self.nc, self.P
        names = [("QpT0", scr["QpT"][0]), ("QpT1", scr["QpT"][1]), ("KpT0", scr["KpT"][0]), ("KpT1", scr["KpT"][1]), ("SGT", scr["SGT"]), ("YBT", scr["YBT"]),
                 ("Kp0", scr["Kp"][0]), ("Kp1", scr["Kp"][1]), ("V", scr["V"])]
        with contextlib.ExitStack() as ph:
            for nm, ap in names:
                shp = list(ap.shape)
                o = nc.dram_tensor("dbg_" + nm, shp, F32, kind="ExternalOutput").ap()
                t = self.sb(ph, "dbgt_" + nm, [128, shp[0] // 128, shp[1]], F32)
                P.dma("pool", lambda e_, t=t, ap=ap: e_.dma_start(out=t[:], in_=ap.rearrange("(a p) c -> p a c", p=128)), writes=["dbg" + nm])
                P.dma("sp", lambda e_, t=t, o=o: e_.dma_start(out=o.rearrange("(a p) c -> p a c", p=128), in_=t[:]), reads=["dbg" + nm], writes=["dbgo" + nm])
            o = nc.dram_tensor("dbg_csc", [128, 2 * 8 * 4 * 3], F32, kind="ExternalOutput").ap()
            P.dma("sp", lambda e_: e_.dma_start(out=o, in_=csc[:].rearrange("p d h c x -> p (d h c x)")), reads=["csc1"], writes=["dbgocsc"])
            P.barrier()

    def phase_even_b(self, L):
        nc, P = self.nc, self.P
        e = L // 2
        scr_l = self.scratch(NT, "l", "ExternalInput")
        with contextlib.ExitStack() as lay:
            csc_l = self.sb(lay, "csc_l", [128, 2, 8, NT // 64, 3], F32)
            P.dma("sp", lambda e_: e_.dma_start(out=csc_l[:].rearrange("p d h c x -> p (d h c x)"), in_=self.csc_in), writes=["csc0"])
            with contextlib.ExitStack() as lay2:
                S_l = self.sb(lay2, "S_l", [128, 16, 128], F32)
                self.even_combine(S_l)
                self.even_scan(L, e, scr_l, NT, 0, csc_l, S_l, "S_l", True, self.cur_x, self.xres)
                self.cur_x = self.xres
                P.barrier()

    def even_local(self, L, e, src, ntok, j, scr, csc, csl, full):
        nc, P = self.nc, self.P
        TT = 256
        NB = TT // 128
        W_IN = self.w_in_even
        cscn, csln = f"csc{j}", f"csl{j}"
        with contextlib.ExitStack() as ph:
            sb = lambda n, shp, d: self.sb(ph, n, shp, d)
            xt = sb("e_x", [128, KC, TT], F32)
            hn = sb("e_hn", [128, KC, TT], BF16)
            sq = sb("e_sq", [128, KC, TT], BF16)
            tmp = [sb(f"e_tmp{i}", [128, TT], F32) for i in range(2)]
            rs = sb("e_rs", [128, TT], F32)
            wb = [sb(f"e_w{i}", [128, KC, 512], BF16) for i in range(2)]
            lbt = sb("e_lb", [128, 2, 1024], F32)
            oml = sb("e_oml", [128, 2, 1024], F32)
            gt = [sb(f"e_g{b}", [128, 1024], F32) for b in range(NB)]
            kk = [sb(f"e_kk{b}", [128, 1024], F32) for b in range(NB)]
            sg = [sb(f"e_sg{i}", [128, 512], F32) for i in range(2)]
            eq = sb("e_eq", [128, 2, 8, TT], F32)
            kpt = sb("e_kpt", [128, 2, 8, TT], BF16)
            qpt = sb("e_qpt", [128, 2, 8, TT], BF16)
            kp = [sb(f"e_kp{i}", [128, 1024], BF16) for i in range(2)]
            vt = [sb(f"e_v{i}", [128, 1024], BF16) for i in range(2)]
            sgt = sb("e_sgt", [128, 8, TT], BF16)
            ut = sb("e_ut", [128, 8, TT], BF16)
            vn = [sb(f"e_vn{b}", [128, 1024], BF16) for b in range(NB)]
            ybt = sgt
            md = sb("e_md", [128, 6, 128], F32)
            wst = sb("e_wst", [128, 8, 128], BF16)
            brow = sb("e_brow", [128, 8, 128], F32)
            grow = sb("e_grow", [128, 1024], F32)
            st4 = sb("e_st4", [128, 8, 4], F32)
            exs = [sb(f"e_ex{i}", [128, 512], F32) for i in range(2)]
            P.dma("sp", lambda e_: e_.dma_start(out=md[:], in_=self.cmat[C_MDF:C_MDF + 6].rearrange("c p t -> p c t")), writes=["e_md"])
            P.dma("pool", lambda e_: e_.dma_start(out=wst[:], in_=self.wsT.rearrange("g s t -> s g t")), writes=["e_wst"])
            P.dma("sp", lambda e_: e_.dma_start(out=brow[:].rearrange("p g t -> p (g t)"),
                                                in_=self.b_sp.rearrange("g t -> (g t)").partition_broadcast(128)), writes=["e_brow"])
            P.dma("sp", lambda e_: e_.dma_start(out=grow[:], in_=self.g_spv.partition_broadcast(128)), writes=["e_grow"])
            if e == 0:
                P.op("dve", lambda e_: e_.memset(lbt[:], 0.0), writes=["e_lb"])
                P.op("dve", lambda e_: e_.memset(oml[:], 1.0), writes=["e_oml"])
            else:
                for d in range(2):
                    P.dma("sp", lambda e_, d=d: e_.dma_start(out=lbt[:, d, :], in_=self.lb_logits[d, 0].partition_broadcast(128)), writes=["e_lb"])
                    P.dma("sp", lambda e_, d=d: e_.dma_start(out=oml[:, d, :], in_=self.lb_logits[d, 1].partition_broadcast(128)), writes=["e_oml"])
                P.op("dve", lambda e_: e_.tensor_tensor(out=oml[:], in0=oml[:], in1=lbt[:], op=ALU.subtract), reads=["e_oml", "e_lb"], writes=["e_oml"])
                P.op("act", lambda e_: e_.activation(out=lbt[:], in_=oml[:], func=AF.Sigmoid), reads=["e_oml"], writes=["e_lb"])
                P.op("dve", lambda e_: e_.tensor_scalar(out=oml[:], in0=lbt[:], scalar1=-1.0, scalar2=1.0, op0=ALU.mult, op1=ALU.add),
                     reads=["e_lb"], writes=["e_oml"])
            A = self.A1[:, L, :, j]
            B = self.mods[:, L, 0:16, j]
            nw = [0]

            if not hasattr(self, "wbf_in"):
                self.wbf_in = self.nc.dram_tensor("wbf_in", [14, 128, KC, 512], BF16).ap()
                for blk in range(14):
                    P.dma("pool", lambda e_, blk=blk: e_.dma_start(out=self.wbf_in[blk], in_=W_IN[:, blk * 512:(blk + 1) * 512].rearrange("(kc p) m -> p kc m", p=128)),
                          writes=[f"wbf_in{blk}"])

            def load_w(c0):
                slot = nw[0] % 2
                nw[0] += 1
                w = wb[slot]
                blk = c0 // 512
                P.dma("sp", lambda e_, w=w, blk=blk: e_.dma_start(out=w[:], in_=self.wbf_in[blk]), reads=[f"wbf_in{blk}"], writes=[f"e_w{slot}"])
                return w, f"e_w{slot}"

            def tok_major(w, wn, b):
                pt, pn = self.next_ps()
                for kc in range(KC):
                    P.op("pe", lambda e_, kc=kc, pt=pt: e_.matmul(pt[:, :], lhsT=hn[:, kc, b * 128:(b + 1) * 128], rhs=w[:, kc, :],
                                                                  start=(kc == 0), stop=(kc == KC - 1)), reads=["e_hn", wn], writes=[pn])
                return pt, pn

            def feat_major(w, wn, mc):
                pt, pn = self.next_ps()
                for kc in range(KC):
                    P.op("pe", lambda e_, kc=kc, pt=pt: e_.matmul(pt[:, :TT], lhsT=w[:, kc, mc * 128:(mc + 1) * 128], rhs=hn[:, kc, :],
                                                                  start=(kc == 0), stop=(kc == KC - 1)), reads=["e_hn", wn], writes=[pn])
                return pt, pn

            nk = [0]
            for t0 in range(0, ntok, TT):
                P.dma("sp", lambda e_, t0=t0: e_.dma_start(out=xt[:], in_=src[:, t0:t0 + TT].rearrange("(kc p) t -> p kc t", p=128)),
                      reads=["dram_x"], writes=["e_x"])
                self.norm_mod(xt, "e_x", TT, A, B, hn, "e_hn", sq, "e_sq", tmp, "e_tmp", rs, "e_rs")
                if DEBUG_CUT == 11:
                    P.barrier()
                    return
                for d in range(2):
                    for cbk in range(2):
                        c0 = 1024 * (1 + d) + cbk * 512
                        w, wn = load_w(c0)
                        cs = slice(cbk * 512, (cbk + 1) * 512)
                        for b in range(NB):
                            pt, pn = tok_major(w, wn, b)
                            s_ = sg[b % 2]
                            sn = f"e_sg{b % 2}"
                            P.op("act", lambda e_, s_=s_, pt=pt: e_.activation(out=s_[:], in_=pt[:], func=AF.Sigmoid), reads=[pn], writes=[sn])
                            P.op("dve", lambda e_, s_=s_, d=d, cs=cs: e_.tensor_tensor(out=s_[:], in0=s_[:], in1=oml[:, d, cs], op=ALU.mult),
                                 reads=[sn, "e_oml"], writes=[sn])
                            P.op("pool", lambda e_, s_=s_, d=d, cs=cs, b=b: e_.tensor_tensor(out=kk[b][:, cs], in0=oml[:, d, cs], in1=s_[:], op=ALU.subtract),
                                 reads=[sn, "e_oml"], writes=[f"e_kk{b}"])
                            P.op("dve", lambda e_, s_=s_, d=d, cs=cs: e_.tensor_tensor(out=s_[:], in0=s_[:], in1=lbt[:, d, cs], op=ALU.add),
                                 reads=[sn, "e_lb"], writes=[sn])
                            P.op("dve", lambda e_, s_=s_: e_.tensor_single_scalar(out=s_[:], in_=s_[:], scalar=1e-30, op=ALU.max), reads=[sn], writes=[sn])
                            P.op("act", lambda e_, s_=s_, b=b, cs=cs: e_.activation(out=gt[b][:, cs], in_=s_[:], func=AF.Ln), reads=[sn], writes=[f"e_g{b}"])
                    if DEBUG_CUT == 12:
                        P.barrier()
                        return
                    for b in range(NB):
                        g_, gn = gt[b], f"e_g{b}"
                        tb0 = t0 + b * 128
                        kpb = kp[nk[0] % 2]
                        kpn = f"e_kp{nk[0] % 2}"
                        nk[0] += 1
                        for hf in range(2):
                            cs = slice(hf * 512, (hf + 1) * 512)
                            pt, pn = self.next_ps()
                            P.op("pe", lambda e_, pt=pt, d=d, g_=g_, cs=cs: e_.matmul(pt[:, :], lhsT=md[:, 2 * d, :], rhs=g_[:, cs], start=True, stop=True),
                                 reads=["e_md", gn], writes=[pn])
                            ex = exs[hf]
                            P.op("act", lambda e_, ex=ex, pt=pt: e_.activation(out=ex[:], in_=pt[:], func=AF.Exp), reads=[pn], writes=[f"e_ex{hf}"])
                            P.op("dve", lambda e_, ex=ex, b=b, cs=cs, kpb=kpb: e_.tensor_tensor(out=kpb[:, cs], in0=kk[b][:, cs], in1=ex[:], op=ALU.mult),
                                 reads=[f"e_ex{hf}", f"e_kk{b}"], writes=[kpn])
                        if DEBUG_CUT == 121:
                            P.barrier()
                            return
                        P.dma("sp", lambda e_, kpb=kpb, d=d, tb0=tb0: e_.dma_start(out=scr["Kp"][d][tb0:tb0 + 128, :], in_=kpb[:]),
                              reads=[kpn], writes=["dram_kp"])
                        if DEBUG_CUT == 122:
                            P.barrier()
                            return
                        for h in range(8):
                            P.op("pe", lambda e_, h=h, kpb=kpb: e_.transpose(out=self.pst[:, h * 128:(h + 1) * 128], in_=kpb[:, h * 128:(h + 1) * 128],
                                                                             identity=self.ident[:]), reads=[kpn, "ident"], writes=["pst"])
                        P.op("dve", lambda e_, d=d, b=b: e_.tensor_copy(out=kpt[:, d, :, b * 128:(b + 1) * 128],
                                                                      in_=self.pst[:, :].rearrange("p (h t) -> p h t", h=8)), reads=["pst"], writes=["e_kpt"])
                        if DEBUG_CUT == 13:
                            P.barrier()
                            return
                        for hh in range(2):
                            pt, pn = self.next_ps()
                            for h4 in range(4):
                                h = hh * 4 + h4
                                P.op("pe", lambda e_, pt=pt, h=h, h4=h4, d=d, g_=g_: e_.matmul(pt[:, h4 * 128:(h4 + 1) * 128], lhsT=g_[:, h * 128:(h + 1) * 128],
                                                                                            rhs=md[:, 2 * d + 1, :], start=True, stop=True),
                                     reads=["e_md", gn], writes=[pn])
                            for h4 in range(4):
                                P.op("act", lambda e_, pt=pt, hh=hh, h4=h4, d=d, b=b: e_.activation(
                                    out=eq[:, d, hh * 4 + h4, b * 128:(b + 1) * 128], in_=pt[:, h4 * 128:(h4 + 1) * 128], func=AF.Exp),
                                    reads=[pn], writes=["e_eq"])
                        if DEBUG_CUT == 14:
                            P.barrier()
                            return
                        pt, pn = self.next_ps()
                        for h in range(8):
                            P.op("pe", lambda e_, pt=pt, h=h, d=d, g_=g_: e_.matmul(pt[:, h * 6:(h + 1) * 6], lhsT=g_[:, h * 128:(h + 1) * 128],
                                                                                 rhs=md[:, 4 + d, 0:6], start=True, stop=True), reads=["e_md", gn], writes=[pn])
                        ci0 = tb0 // 64
                        for h in range(8):
                            P.op("dve", lambda e_, pt=pt, d=d, ci0=ci0, h=h: e_.tensor_copy(
                                out=csl[:, d, h, ci0:ci0 + 2, :], in_=pt[:, h * 6:(h + 1) * 6].rearrange("p (c x) -> p c x", c=2)), reads=[pn], writes=[csln])
                        for h in range(8):
                            P.op("act", lambda e_, d=d, ci0=ci0, h=h: e_.activation(
                                out=csc[:, d, h, ci0:ci0 + 2, :], in_=csl[:, d, h, ci0:ci0 + 2, :], func=AF.Exp), reads=[csln], writes=[cscn])
                    P.dma("sp", lambda e_, d=d, t0=t0: e_.dma_start(out=scr["KpT"][d][:, t0:t0 + TT].rearrange("(h p) t -> p h t", p=128), in_=kpt[:, d]),
                          reads=["e_kpt"], writes=["dram_kpt"])
                if DEBUG_CUT == 15:
                    P.barrier()
                    return
                for cbk in range(2):
                    w, wn = load_w(cbk * 512)
                    for mc in range(4):
                        h = cbk * 4 + mc
                        pt, pn = feat_major(w, wn, mc)
                        for d in range(2):
                            P.op("dve", lambda e_, pt=pt, d=d, h=h: e_.tensor_tensor(out=qpt[:, d, h, :], in0=pt[:, :TT], in1=eq[:, d, h, :], op=ALU.mult),
                                 reads=[pn, "e_eq"], writes=["e_qpt"])
                for d in range(2):
                    P.dma("sp", lambda e_, d=d, t0=t0: e_.dma_start(out=scr["QpT"][d][:, t0:t0 + TT].rearrange("(h p) t -> p h t", p=128), in_=qpt[:, d]),
                          reads=["e_qpt"], writes=["dram_qpt"])
                for cbk in range(2):
                    w, wn = load_w(3072 + cbk * 512)
                    for b in range(NB):
                        pt, pn = tok_major(w, wn, b)
                        P.op("act", lambda e_, pt=pt, b=b, cbk=cbk: e_.copy(out=vt[b][:, cbk * 512:(cbk + 1) * 512], in_=pt[:]), reads=[pn], writes=[f"e_v{b}"])
                for b in range(NB):
                    P.dma("sp", lambda e_, b=b, t0=t0: e_.dma_start(out=scr["V"][t0 + b * 128:t0 + (b + 1) * 128, :], in_=vt[b][:]),
                          reads=[f"e_v{b}"], writes=["dram_v"])
                if DEBUG_CUT == 16:
                    P.barrier()
                    return
                if not full:
                    continue
                for cbk in range(2):
                    w, wn = load_w(4096 + cbk * 512)
                    for mc in range(4):
                        pt, pn = feat_major(w, wn, mc)
                        P.op("act", lambda e_, pt=pt, h=cbk * 4 + mc: e_.activation(out=sgt[:, h, :], in_=pt[:, :TT], func=AF.Silu), reads=[pn], writes=["e_sgt"])
                P.dma("sp", lambda e_, t0=t0: e_.dma_start(out=scr["SGT"][:, t0:t0 + TT].rearrange("(h p) t -> p h t", p=128), in_=sgt[:]),
                      reads=["e_sgt"], writes=["dram_sgt"])
                for cbk in range(2):
                    w, wn = load_w(5120 + cbk * 512)
                    for mc in range(4):
                        pt, pn = feat_major(w, wn, mc)
                        P.op("act", lambda e_, pt=pt, h=cbk * 4 + mc: e_.activation(out=ut[:, h, :], in_=pt[:, :TT], func=AF.Gelu), reads=[pn], writes=["e_ut"])
                if DEBUG_CUT == 17:
                    P.barrier()
                    return
                for cbk in range(2):
                    w, wn = load_w(6144 + cbk * 512)
                    for b in range(NB):
                        pt, pn = tok_major(w, wn, b)
                        s_ = sg[b % 2]
                        sn = f"e_sg{b % 2}"
                        x2 = exs[b % 2]
                        xn2 = f"e_ex{b % 2}"
                        P.op("act", lambda e_, s_=s_, pt=pt: e_.activation(out=s_[:], in_=pt[:], func=AF.Gelu), reads=[pn], writes=[sn])
                        P.op("act", lambda e_, s_=s_, x2=x2: e_.activation(out=x2[:], in_=s_[:], func=AF.Square), reads=[sn], writes=[xn2])
                        g0 = cbk * 4
                        P.op("dve", lambda e_, s_=s_, g0=g0: e_.tensor_reduce(out=st4[:, g0:g0 + 4, 0], in_=s_[:].rearrange("p (g c) -> p g c", g=4),
                                                                             axis=mybir.AxisListType.X, op=ALU.add), reads=[sn], writes=["e_st4"])
                        P.op("dve", lambda e_, x2=x2, g0=g0: e_.tensor_reduce(out=st4[:, g0:g0 + 4, 1], in_=x2[:].rearrange("p (g c) -> p g c", g=4),
                                                                             axis=mybir.AxisListType.X, op=ALU.add), reads=[xn2], writes=["e_st4"])
                        P.op("dve", lambda e_, g0=g0: e_.tensor_single_scalar(out=st4[:, g0:g0 + 4, 0], in_=st4[:, g0:g0 + 4, 0], scalar=1.0 / 128, op=ALU.mult),
                             reads=["e_st4"], writes=["e_st4"])
                        P.op("dve", lambda e_, g0=g0: e_.tensor_tensor(out=st4[:, g0:g0 + 4, 2], in0=st4[:, g0:g0 + 4, 0], in1=st4[:, g0:g0 + 4, 0], op=ALU.mult),
                             reads=["e_st4"], writes=["e_st4"])
                        P.op("dve", lambda e_, g0=g0: e_.scalar_tensor_tensor(out=st4[:, g0:g0 + 4, 1], in0=st4[:, g0:g0 + 4, 1], scalar=1.0 / 128,
                                                                            in1=st4[:, g0:g0 + 4, 2], op0=ALU.mult, op1=ALU.subtract),
                             reads=["e_st4"], writes=["e_st4"])
                        P.op("act", lambda e_, g0=g0: e_.activation(out=st4[:, g0:g0 + 4, 1], in_=st4[:, g0:g0 + 4, 1], func=AF.Sqrt, bias=self.epsb[:, 0:1], scale=1.0),
                             reads=["e_st4", "epsb"], writes=["e_st4"])
                        P.op("dve", lambda e_, g0=g0: e_.reciprocal(out=st4[:, g0:g0 + 4, 1], in_=st4[:, g0:g0 + 4, 1]), reads=["e_st4"], writes=["e_st4"])
                        for g4 in range(4):
                            g = g0 + g4
                            P.op("dve", lambda e_, s_=s_, g=g, g4=g4: e_.tensor_scalar(
                                out=s_[:, g4 * 128:(g4 + 1) * 128], in0=s_[:, g4 * 128:(g4 + 1) * 128], scalar1=st4[:, g, 0:1], scalar2=st4[:, g, 1:2],
                                op0=ALU.subtract, op1=ALU.mult), reads=[sn, "e_st4"], writes=[sn])
                        P.op("pool", lambda e_, s_=s_, b=b, cbk=cbk: e_.tensor_tensor(out=vn[b][:, cbk * 512:(cbk + 1) * 512], in0=s_[:],
                                                                                     in1=grow[:, cbk * 512:(cbk + 1) * 512], op=ALU.mult),
                             reads=[sn, "e_grow"], writes=[f"e_vn{b}"])
                if DEBUG_CUT == 18:
                    P.barrier()
                    return
                for b in range(NB):
                    for hh in range(2):
                        pt, pn = self.next_ps()
                        for g4 in range(4):
                            g = hh * 4 + g4
                            P.op("pe", lambda e_, pt=pt, g=g, g4=g4, b=b: e_.matmul(pt[:, g4 * 128:(g4 + 1) * 128], lhsT=vn[b][:, g * 128:(g + 1) * 128],
                                                                                   rhs=wst[:, g, :], start=True, stop=True), reads=[f"e_vn{b}", "e_wst"], writes=[pn])
                        x2 = exs[hh]
                        xn2 = f"e_ex{hh}"
                        P.op("dve", lambda e_, pt=pt, hh=hh, x2=x2: e_.tensor_tensor(out=x2[:], in0=pt[:], in1=brow[:, hh * 4:(hh + 1) * 4, :].rearrange("p g t -> p (g t)"),
                                                                                   op=ALU.add), reads=[pn, "e_brow"], writes=[xn2])
                        P.op("pool", lambda e_, hh=hh, x2=x2, b=b: e_.tensor_tensor(out=ybt[:, hh * 4:(hh + 1) * 4, b * 128:(b + 1) * 128],
                                                                                   in0=x2[:].rearrange("p (g t) -> p g t", g=4),
                                                                                   in1=ut[:, hh * 4:(hh + 1) * 4, b * 128:(b + 1) * 128], op=ALU.mult),
                             reads=[xn2, "e_ut"], writes=["e_sgt"])
                P.dma("sp", lambda e_, t0=t0: e_.dma_start(out=scr["YBT"][:, t0:t0 + TT].rearrange("(h p) t -> p h t", p=128), in_=ybt[:]),
                      reads=["e_sgt"], writes=["dram_ybt"])
            P.barrier()

    def even_scan(self, L, e, scr, ntok, j, csc, S, Sname, compute_out, src, dst):
        nc, P = self.nc, self.P
        nblk = ntok // 128
        cscn = f"csc{j}"
        with contextlib.ExitStack() as ph:
            sb = lambda n, shp, d: self.sb(ph, n, shp, d)
            Sb = [sb(f"s_Sb{i}", [128, 8, 128], BF16) for i in range(2)]
            tmpU = sb("s_tu", [128, 8, 128], F32)
            qt = [sb(f"s_q{i}", [128, 8, 128], BF16) for i in range(2)]
            kt = [sb(f"s_k{i}", [128, 8, 128], BF16) for i in range(2)]
            kpb = [sb(f"s_kp{i}", [64, 2, 1024], BF16) for i in range(2)]
            vb = [sb(f"s_v{i}", [64, 2, 1024], BF16) for i in range(2)]
            sct = [sb(f"s_sc{i}", [64, 8, 64], BF16) for i in range(2)]
            mask = sb("s_mask", [64, 2, 8, 64], F32)
            masku = sb("s_masku", [64, 2, 8, 64], mybir.dt.uint32)
            sc32 = [sb(f"s_sc32_{i}", [64, 8, 64], F32) for i in range(2)]
            m64 = sb("s_m64", [64, 2, 128], F32)
            oT = sb("s_oT", [128, 8, ntok], F32) if compute_out else None
            if compute_out:
                for d in range(2):
                    P.dma("sp", lambda e_, d=d: e_.dma_start(out=m64[:, d, :], in_=self.cmat[C_MASKF + d][0:64, :]), writes=["s_m64"])
                    for h in range(8):
                        P.op("dve", lambda e_, d=d, h=h: e_.tensor_copy(out=mask[:, d, h, :], in_=m64[:, d, 0:64]), reads=["s_m64"], writes=["s_mask"])
                    P.op("dve", lambda e_, d=d: e_.tensor_single_scalar(out=masku[:, d], in_=mask[:, d], scalar=0.5, op=ALU.is_gt), reads=["s_mask"], writes=["s_masku"])
                    P.op("dve", lambda e_, d=d: e_.memset(sc32[d][:], 0.0), writes=[f"s_sc32_{d}"])
            n = 0
            nchunk = 0
            for d in range(2):
                blocks = list(range(nblk)) if d == 0 else list(reversed(range(nblk)))
                for blk in blocks:
                    slot = n % 2
                    n += 1
                    t0 = blk * 128
                    if compute_out:
                        P.dma("sp", lambda e_, slot=slot, d=d, t0=t0: e_.dma_start(out=qt[slot][:], in_=scr["QpT"][d][:, t0:t0 + 128].rearrange("(h p) t -> p h t", p=128)),
                              reads=["dram_qpt"], writes=[f"s_q{slot}"])
                        P.dma("sp", lambda e_, slot=slot, d=d, t0=t0: e_.dma_start(out=kt[slot][:], in_=scr["KpT"][d][:, t0:t0 + 128].rearrange("(h p) t -> p h t", p=128)),
                              reads=["dram_kpt"], writes=[f"s_k{slot}"])
                    P.dma("sp", lambda e_, slot=slot, d=d, t0=t0: e_.dma_start(out=kpb[slot][:], in_=scr["Kp"][d][t0:t0 + 128, :].rearrange("(c p) k -> p c k", p=64)),
                          reads=["dram_kp"], writes=[f"s_kp{slot}"])
                    P.dma("sp", lambda e_, slot=slot, t0=t0: e_.dma_start(out=vb[slot][:], in_=scr["V"][t0:t0 + 128, :].rearrange("(c p) k -> p c k", p=64)),
                          reads=["dram_v"], writes=[f"s_v{slot}"])
                    for c in ((0, 1) if d == 0 else (1, 0)):
                        ci = blk * 2 + c
                        cslc = slice(c * 64, (c + 1) * 64)
                        if compute_out:
                            sbt = Sb[nchunk % 2]
                            sbn = f"s_Sb{nchunk % 2}"
                            sc_ = sct[nchunk % 2]
                            scn = f"s_sc{nchunk % 2}"
                            nchunk += 1
                            for h in range(8):
                                P.op("act", lambda e_, h=h, d=d, ci=ci, sbt=sbt: e_.activation(out=sbt[:, h, :], in_=S[:, d * 8 + h, :], func=AF.Identity,
                                                                                            scale=csc[:, d, h, ci, 0:1]),
                                     reads=[Sname + f"_{d * 8 + h}", cscn], writes=[sbn])
                            psc, pscn = self.next_ps()
                            for h in range(8):
                                P.op("pe", lambda e_, h=h, psc=psc, slot=slot, cslc=cslc: e_.matmul(psc[0:64, h * 64:(h + 1) * 64], lhsT=kt[slot][:, h, cslc],
                                                                                              rhs=qt[slot][:, h, cslc], start=True, stop=True),
                                     reads=[f"s_k{slot}", f"s_q{slot}"], writes=[pscn])
                            P.op("dve", lambda e_, psc=psc, d=d: e_.copy_predicated(sc32[d][:], masku[:, d], psc[0:64, :].rearrange("p (h t) -> p h t", h=8)),
                                 reads=[pscn, "s_masku", f"s_sc32_{d}"], writes=[f"s_sc32_{d}"])
                            P.op("pool", lambda e_, sc_=sc_, d=d: e_.tensor_copy(out=sc_[:], in_=sc32[d][:]), reads=[f"s_sc32_{d}"], writes=[scn])
                            po, pon = self.next_ps()
                            for h in range(8):
                                P.op("pe", lambda e_, h=h, po=po, slot=slot, c=c, sc_=sc_: e_.matmul(po[:, h * 64:(h + 1) * 64], lhsT=vb[slot][:, c, h * 128:(h + 1) * 128],
                                                                                               rhs=sc_[:, h, :], start=True, stop=False),
                                     reads=[f"s_v{slot}", scn], writes=[pon])
                                P.op("pe", lambda e_, h=h, po=po, slot=slot, cslc=cslc, sbt=sbt: e_.matmul(po[:, h * 64:(h + 1) * 64], lhsT=sbt[:, h, :],
                                                                                                     rhs=qt[slot][:, h, cslc], start=False, stop=True),
                                     reads=[sbn, f"s_q{slot}"], writes=[pon])
                            osl = oT[:, :, ci * 64:(ci + 1) * 64]
                            if d == 0:
                                P.op("dve", lambda e_, po=po, osl=osl: e_.tensor_copy(out=osl, in_=po[:, :].rearrange("p (h t) -> p h t", h=8)), reads=[pon], writes=["s_oT"])
                            else:
                                P.op("dve", lambda e_, po=po, osl=osl: e_.tensor_tensor(out=osl, in0=po[:, :].rearrange("p (h t) -> p h t", h=8), in1=osl, op=ALU.add),
                                     reads=[pon, "s_oT"], writes=["s_oT"])
                        for hh in range(2):
                            pk, pkn = self.next_ps()
                            for h4 in range(4):
                                h = hh * 4 + h4
                                P.op("pe", lambda e_, h=h, h4=h4, pk=pk, slot=slot, c=c: e_.matmul(pk[:, h4 * 128:(h4 + 1) * 128], lhsT=kpb[slot][:, c, h * 128:(h + 1) * 128],
                                                                                             rhs=vb[slot][:, c, h * 128:(h + 1) * 128], start=True, stop=True),
                                     reads=[f"s_kp{slot}", f"s_v{slot}"], writes=[pkn])
                            for h4 in range(4):
                                h = hh * 4 + h4
                                idx = d * 8 + h
                                P.op("act", lambda e_, h=h, h4=h4, pk=pk, d=d, ci=ci: e_.activation(out=tmpU[:, h, :], in_=pk[:, h4 * 128:(h4 + 1) * 128], func=AF.Identity,
                                                                                                 scale=csc[:, d, h, ci, 1:2]), reads=[pkn, cscn], writes=[f"s_tu{h}"])
                                P.op("dve", lambda e_, h=h, idx=idx, d=d, ci=ci: e_.scalar_tensor_tensor(out=S[:, idx, :], in0=S[:, idx, :], scalar=csc[:, d, h, ci, 2:3],
                                                                                                      in1=tmpU[:, h, :], op0=ALU.mult, op1=ALU.add),
                                     reads=[f"s_tu{h}", cscn, Sname + f"_{idx}"], writes=[Sname + f"_{idx}"])
            if compute_out:
                self.even_readout(L, e, scr, ntok, j, oT, src, dst)
            P.barrier()

    def even_readout(self, L, e, scr, ntok, j, oT, src, dst):
        nc, P = self.nc, self.P
        TT = 256
        with contextlib.ExitStack() as ph:
            sb = lambda n, shp, d: self.sb(ph, n, shp, d)
            xt = sb("r_x", [128, KC, TT], F32)
            yt = sb("r_y", [128, KC, TT], BF16)
            sg = sb("r_sg", [128, 8, TT], BF16)
            wo = [sb(f"r_w{i}", [128, KC, 256], BF16) for i in range(2)]
            sqh = [sb(f"r_sq{i}", [128, TT], BF16) for i in range(2)]
            rs = [sb(f"r_rs{i}", [128, TT], F32) for i in range(2)]
            tt_ = [sb(f"r_t{i}", [128, TT], F32) for i in range(2)]
            ghg = sb("r_ghg", [128, 2, 8], F32)
            P.dma("sp", lambda e_: e_.dma_start(out=ghg[:], in_=self.g_hgrn), writes=["r_ghg"])
            G1 = self.mods[:, L, 32:48, j]
            nw = 0
            if not hasattr(self, "wbf_out"):
                self.wbf_out = self.nc.dram_tensor("wbf_out", [8, 128, KC, 256], BF16).ap()
                for mb in range(8):
                    P.dma("pool", lambda e_, mb=mb: e_.dma_start(out=self.wbf_out[mb], in_=self.w_out_even[:, mb * 256:(mb + 1) * 256].rearrange("(kc p) m -> p kc m", p=128)),
                          writes=[f"wbf_out{mb}"])
            for t0 in range(0, ntok, TT):
                P.dma("sp", lambda e_, t0=t0: e_.dma_start(out=xt[:], in_=src[:, t0:t0 + TT].rearrange("(kc p) t -> p kc t", p=128)), reads=["dram_x"], writes=["r_x"])
                P.dma("sp", lambda e_, t0=t0: e_.dma_start(out=sg[:], in_=scr["SGT"][:, t0:t0 + TT].rearrange("(h p) t -> p h t", p=128)), reads=["dram_sgt"], writes=["r_sg"])
                P.dma("sp", lambda e_, t0=t0: e_.dma_start(out=yt[:, 8:16, :], in_=scr["YBT"][:, t0:t0 + TT].rearrange("(h p) t -> p h t", p=128)), reads=["dram_ybt"], writes=["r_yb"])
                for h in range(8):
                    i2 = h % 2
                    P.op("act", lambda e_, h=h, i2=i2, t0=t0: e_.activation(out=sqh[i2][:], in_=oT[:, h, t0:t0 + TT], func=AF.Square), reads=["s_oT"], writes=[f"r_sq{i2}"])
                    pt, pn = self.next_ps()
                    P.op("pe", lambda e_, pt=pt, i2=i2: e_.matmul(pt[:, :TT], lhsT=self.ones[:], rhs=sqh[i2][:], start=True, stop=True), reads=["ones", f"r_sq{i2}"], writes=[pn])
                    P.op("act", lambda e_, pt=pt, i2=i2: e_.activation(out=rs[i2][:], in_=pt[:, :TT], func=AF.Sqrt, bias=self.epsb[:, 0:1], scale=1.0 / 128),
                         reads=[pn, "epsb"], writes=[f"r_rs{i2}"])
                    P.op("dve", lambda e_, i2=i2: e_.reciprocal(out=rs[i2][:], in_=rs[i2][:]), reads=[f"r_rs{i2}"], writes=[f"r_rs{i2}"])
                    P.op("dve", lambda e_, h=h, i2=i2, t0=t0: e_.scalar_tensor_tensor(out=tt_[i2][:], in0=oT[:, h, t0:t0 + TT], scalar=ghg[:, e, h:h + 1], in1=rs[i2][:],
                                                                                    op0=ALU.mult, op1=ALU.mult), reads=["s_oT", "r_ghg", f"r_rs{i2}"], writes=[f"r_t{i2}"])
                    P.op("pool", lambda e_, h=h, i2=i2: e_.tensor_tensor(out=yt[:, h, :], in0=tt_[i2][:], in1=sg[:, h, :], op=ALU.mult),
                         reads=[f"r_t{i2}", "r_sg"], writes=["r_ya"])
                for mb in range(8):
                    slot = nw % 2
                    nw += 1
                    w = wo[slot]
                    P.dma("sp", lambda e_, w=w, mb=mb: e_.dma_start(out=w[:], in_=self.wbf_out[mb]), reads=[f"wbf_out{mb}"], writes=[f"r_w{slot}"])
                    for mc in range(2):
                        m = mb * 2 + mc
                        pt, pn = self.next_ps()
                        for kc in range(KC):
                            P.op("pe", lambda e_, pt=pt, w=w, kc=kc, mc=mc: e_.matmul(pt[:, :TT], lhsT=w[:, kc, mc * 128:(mc + 1) * 128], rhs=yt[:, kc, :],
                                                                                    start=(kc == 0), stop=(kc == KC - 1)), reads=[f"r_w{slot}", "r_ya", "r_yb"], writes=[pn])
                        P.op("dve", lambda e_, pt=pt, m=m: e_.scalar_tensor_tensor(out=xt[:, m, :], in0=pt[:, :TT], scalar=G1[:, m:m + 1], in1=xt[:, m, :],
                                                                                 op0=ALU.mult, op1=ALU.add), reads=[pn, "mods", "r_x"], writes=["r_x"])
                P.dma("sp", lambda e_, t0=t0: e_.dma_start(out=dst[:, t0:t0 + TT].rearrange("(kc p) t -> p kc t", p=128), in_=xt[:]), reads=["r_x"], writes=["dram_x"])

    def even_pack(self, csl, S_l):
        P = self.P
        with contextlib.ExitStack() as ph:
            sb = lambda n, shp, d: self.sb(ph, n, shp, d)
            dt_ = sb("x_dt", [128, 16], F32)
            pk_ = sb("x_pk", [128, 16, 132], F32)
            P.op("dve", lambda e_: e_.tensor_reduce(out=dt_[:], in_=csl[:, :, :, :, 2].rearrange("p d h c -> p (d h) c"), axis=mybir.AxisListType.X, op=ALU.add),
                 reads=["csl0"], writes=["x_dt"])
            P.op("act", lambda e_: e_.activation(out=dt_[:], in_=dt_[:], func=AF.Exp), reads=["x_dt"], writes=["x_dt"])
            allS = [f"S_l_{i}" for i in range(16)]
            P.op("dve", lambda e_: e_.memset(pk_[:], 0.0), writes=["x_pk"])
            P.op("dve", lambda e_: e_.tensor_copy(out=pk_[:, :, 0:128], in_=S_l[:]), reads=allS, writes=["x_pk"])
            P.op("dve", lambda e_: e_.tensor_copy(out=pk_[:, :, 128], in_=dt_[:]), reads=["x_dt"], writes=["x_pk"])
            P.dma("sp", lambda e_: e_.dma_start(out=self.pack_out.rearrange("(i p) c -> p i c", p=128), in_=pk_[:]), reads=["x_pk"], writes=["dram_xs"])
            P.barrier()

    def even_combine(self, S_l):
        P = self.P
        xg = self.pack_in
        with contextlib.ExitStack() as ph:
            sb = lambda n, shp, d: self.sb(ph, n, shp, d)
            cf = sb("x_cf", [128, 4, NCORES], F32)
            oma = sb("x_oma", [128, 4, NCORES], F32)
            G = [sb(f"x_G{i}", [128, 8, 132], F32) for i in range(2)]
            mm = sb("x_m", [128, 8], F32)
            tx = [sb(f"x_t{i}", [128, 128], F32) for i in range(2)]
            P.dma("sp", lambda e_: e_.dma_start(out=cf[:], in_=self.coef), writes=["x_cf"])
            P.op("dve", lambda e_: e_.tensor_scalar(out=oma[:], in0=cf[:], scalar1=-1.0, scalar2=1.0, op0=ALU.mult, op1=ALU.add), reads=["x_cf"], writes=["x_oma"])
            P.dma("sp", lambda e_: e_.dma_start(out=S_l[:], in_=self.sc_in), writes=[f"S_l_{i}" for i in range(16)])
            n = 0
            for dsel in range(2):
                order = list(range(NCORES)) if dsel == 0 else list(reversed(range(NCORES)))
                for jc in order:
                    g_ = G[n % 2]
                    gn = f"x_G{n % 2}"
                    n += 1
                    r0 = jc * 2048 + dsel * 1024
                    P.dma("sp", lambda e_, g_=g_, r0=r0: e_.dma_start(out=g_[:], in_=xg[r0:r0 + 1024, :].rearrange("(h p) c -> p h c", p=128)), writes=[gn])
                    P.op("dve", lambda e_, g_=g_, dsel=dsel, jc=jc: e_.tensor_scalar(out=mm[:], in0=g_[:, :, 128], scalar1=cf[:, dsel, jc:jc + 1], scalar2=oma[:, dsel, jc:jc + 1],
                                                                                   op0=ALU.mult, op1=ALU.add), reads=[gn, "x_cf", "x_oma"], writes=["x_m"])
                    for h in range(8):
                        idx = dsel * 8 + h
                        t_ = tx[h % 2]
                        P.op("pool", lambda e_, g_=g_, h=h, t_=t_, dsel=dsel, jc=jc: e_.tensor_single_scalar(out=t_[:], in_=g_[:, h, 0:128], scalar=cf[:, dsel, jc:jc + 1], op=ALU.mult),
                             reads=[gn, "x_cf"], writes=[f"x_t{h % 2}"])
                        P.op("dve", lambda e_, idx=idx, h=h, t_=t_: e_.scalar_tensor_tensor(out=S_l[:, idx, :], in0=S_l[:, idx, :], scalar=mm[:, h:h + 1], in1=t_[:],
                                                                                          op0=ALU.mult, op1=ALU.add), reads=[f"x_t{h % 2}", "x_m", f"S_l_{idx}"], writes=[f"S_l_{idx}"])
            P.barrier()

    def phase_odd(self, L, part):
        ctx_out = (L == 1)
        src_x, src_c = self.cur_x, self.cur_c
        self.odd_segment(L, part, src_x, self.xres, NT, 0, True)
        if ctx_out and part == "b":
            self.odd_segment(L, part, src_c, self.cres, NCX, 1, False)
        if part == "b":
            self.cur_x = self.xres
            if ctx_out:
                self.cur_c = self.cres

    def odd_segment(self, L, part, src, dst, ntok, j, grid):
        nc, P = self.nc, self.P
        o = L // 2
        nblk = ntok // 128
        with contextlib.ExitStack() as lay:
            p_own = self.sb(lay, f"p_own{j}", [128, nblk, D], BF16)
            halo = [self.sb(lay, f"p_halo{i}", [128, 4, D], BF16) for i in range(2)] if (grid and part == "b") else None
            if part == "b" and grid:
                for blk in range(nblk):
                    P.dma("sp", lambda e_, blk=blk: e_.dma_start(out=p_own[:, blk, :], in_=self.p_in[blk * 128:(blk + 1) * 128, :]), writes=[f"p_own_{blk}"])
            else:
                self.odd_stage1(L, o, src, ntok, j, p_own)
            if part == "a":
                P.dma("sp", lambda e_: e_.dma_start(out=self.p_out.rearrange("(b p) c -> p b c", p=128), in_=p_own[:]),
                      reads=[f"p_own_{b}" for b in range(nblk)], writes=["dram_pout"])
                hs = self.halo_out
                P.dma("sp", lambda e_: e_.dma_start(out=hs[0:512, :].rearrange("(b p) c -> p b c", p=128), in_=p_own[:, 0:4, :]),
                      reads=[f"p_own_{b}" for b in range(4)], writes=["dram_hs"])
                P.dma("sp", lambda e_: e_.dma_start(out=hs[512:1024, :].rearrange("(b p) c -> p b c", p=128), in_=p_own[:, 12:16, :]),
                      reads=[f"p_own_{b}" for b in range(12, 16)], writes=["dram_hs"])
                P.barrier()
                return
            if grid:
                self.odd_stage2(halo)
            self.odd_stage3(L, o, src, dst, ntok, j, grid, p_own, halo)

    def odd_stage1(self, L, o, src, ntok, j, p_own):
        P = self.P
        with contextlib.ExitStack() as ph:
            TT = 256
            NB = TT // 128
            xt = self.sb(ph, "p_x", [128, KC, TT], F32)
            hn = self.sb(ph, "p_hn", [128, KC, TT], BF16)
            sq = self.sb(ph, "p_sq", [128, KC, TT], BF16)
            tmp = [self.sb(ph, f"p_tmp{i}", [128, TT], F32) for i in range(2)]
            rs = self.sb(ph, "p_rs", [128, TT], F32)
            wb = [self.sb(ph, f"p_w{i}", [128, KC, 512], BF16) for i in range(2)]
            A = self.A1[:, L, :, j]
            B = self.mods[:, L, 0:16, j]
            nw = 0
            for t0 in range(0, ntok, TT):
                P.dma("sp", lambda e_, t0=t0: e_.dma_start(out=xt[:], in_=src[:, t0:t0 + TT].rearrange("(kc p) t -> p kc t", p=128)),
                      reads=["dram_x"], writes=["p_x"])
                self.norm_mod(xt, "p_x", TT, A, B, hn, "p_hn", sq, "p_sq", tmp, "p_tmp", rs, "p_rs")
                for cb in range(4):
                    slot = nw % 2
                    nw += 1
                    w = wb[slot]
                    srcw = self.w_in_pool[:, cb * 512:(cb + 1) * 512].rearrange("(kc p) m -> p kc m", p=128)
                    P.dma("pool", lambda e_, w=w, srcw=srcw: e_.dma_start(out=w[:], in_=srcw), writes=[f"p_w{slot}"])
                    for b in range(NB):
                        blk = t0 // 128 + b
                        pt, pn = self.next_ps()
                        for kc in range(KC):
                            P.op("pe", lambda e_, kc=kc, pt=pt, b=b, w=w: e_.matmul(pt[:, :], lhsT=hn[:, kc, b * 128:(b + 1) * 128], rhs=w[:, kc, :],
                                                                                  start=(kc == 0), stop=(kc == KC - 1)), reads=["p_hn", f"p_w{slot}"], writes=[pn])
                        P.op("act", lambda e_, pt=pt, blk=blk, cb=cb: e_.copy(out=p_own[:, blk, cb * 512:(cb + 1) * 512], in_=pt[:]),
                             reads=[pn], writes=[f"p_own_{blk}"])
            P.barrier()

    def odd_stage2(self, halo):
        P = self.P
        hg = self.halo_in
        with contextlib.ExitStack() as ph:
            cand = [self.sb(ph, f"p_cand{i}", [128, 4, D], BF16) for i in range(2)]
            cf = self.sb(ph, "p_cf", [128, 4, NCORES], F32)
            P.dma("sp", lambda e_: e_.dma_start(out=cf[:], in_=self.coef), writes=["p_cf"])
            n = 0
            for side in range(2):
                for jc in range(NCORES):
                    c_ = cand[n % 2]
                    cn = f"p_cand{n % 2}"
                    n += 1
                    r0 = jc * 1024 + (512 if side == 0 else 0)
                    P.dma("sp", lambda e_, c_=c_, r0=r0: e_.dma_start(out=c_[:], in_=hg[r0:r0 + 512, :].rearrange("(b p) c -> p b c", p=128)), writes=[cn])
                    if jc == 0:
                        P.op("dve", lambda e_, c_=c_, side=side, jc=jc: e_.tensor_single_scalar(out=halo[side][:], in_=c_[:], scalar=cf[:, 2 + side, jc:jc + 1], op=ALU.mult),
                             reads=[cn, "p_cf"], writes=[f"p_halo{side}"])
                    else:
                        P.op("dve", lambda e_, c_=c_, side=side, jc=jc: e_.scalar_tensor_tensor(out=halo[side][:], in0=c_[:], scalar=cf[:, 2 + side, jc:jc + 1],
                                                                                             in1=halo[side][:], op0=ALU.mult, op1=ALU.add),
                             reads=[cn, "p_cf", f"p_halo{side}"], writes=[f"p_halo{side}"])
            P.barrier()

    def odd_stage3(self, L, o, src, dst, ntok, j, grid, p_own, halo):
        P = self.P
        nblk = ntok // 128
        with contextlib.ExitStack() as ph:
            deltas = (lambda k: W2D_DELTAS[k]) if grid else (lambda k: W1D_DELTAS)
            keys = [(k, dl) for k in POOL_WINDOWS for dl in deltas(k)]
            wmat = self.sb(ph, "p_wm", [128, len(keys), 128], BF16)
            c0 = C_W2D if grid else C_W1D
            P.dma("pool", lambda e_: e_.dma_start(out=wmat[:], in_=self.cmat[c0:c0 + len(keys)].rearrange("c p t -> p c t")), writes=["p_wm"])
            wpos = {kd: i for i, kd in enumerate(keys)}
            invt = self.sb(ph, "p_inv", [128, nblk, 4], F32)
            P.dma("sp", lambda e_: e_.dma_start(out=invt[:], in_=(self.inv2 if grid else self.inv1)), writes=["p_inv"])
            wg = self.sb(ph, "p_wg", [128, 4, 4, 512], BF16)
            P.dma("pool", lambda e_: e_.dma_start(out=wg[:], in_=self.w_grp.rearrange("g (cc p) d -> p g cc d", p=128)), writes=["p_wg"])
            bg = self.sb(ph, "p_bg", [128, 2, KC], F32)
            scp = self.sb(ph, "p_scp", [128, 2, KC], F32)
            sg1 = self.sb(ph, "p_sg1", [128, KC], F32)
            P.dma("sp", lambda e_: e_.dma_start(out=bg[:], in_=self.b_grp), writes=["p_bg"])
            P.dma("sp", lambda e_: e_.dma_start(out=scp[:], in_=self.sc_pool), writes=["p_scp"])
            P.op("dve", lambda e_: e_.tensor_tensor(out=sg1[:], in0=scp[:, o, :], in1=self.mods[:, L, 32:48, j], op=ALU.mult), reads=["p_scp", "mods"], writes=["p_sg1"])
            TT = min(512, ntok)
            NB = TT // 128
            ztm = [self.sb(ph, f"p_z{i}", [128, D], BF16) for i in range(2)]
            zT = self.sb(ph, "p_zT", [128, KC, TT], BF16)
            xt = self.sb(ph, "p_x3", [128, KC, TT], F32)
            tt_ = [self.sb(ph, f"p_t{i}", [128, TT], F32) for i in range(2)]
            nz = 0
            for t0 in range(0, ntok, TT):
                P.dma("sp", lambda e_, t0=t0: e_.dma_start(out=xt[:], in_=src[:, t0:t0 + TT].rearrange("(kc p) t -> p kc t", p=128)),
                      reads=["dram_x"], writes=["p_x3"])
                for b in range(NB):
                    blk = t0 // 128 + b
                    z_ = ztm[nz % 2]
                    zn = f"p_z{nz % 2}"
                    nz += 1
                    for gi, k in enumerate(POOL_WINDOWS):
                        cols = slice(gi * 512, (gi + 1) * 512)
                        srcs = []
                        for dl in deltas(k):
                            i = blk + dl
                            if 0 <= i < nblk:
                                srcs.append((dl, p_own[:, i, cols], f"p_own_{i}"))
                            elif grid and i < 0:
                                srcs.append((dl, halo[0][:, 4 + i, cols], "p_halo0"))
                            elif grid and i >= nblk:
                                srcs.append((dl, halo[1][:, i - nblk, cols], "p_halo1"))
                        pt, pn = self.next_ps()
                        for si, (dl, ap_, rn) in enumerate(srcs):
                            P.op("pe", lambda e_, pt=pt, ap_=ap_, k=k, dl=dl, si=si, ns=len(srcs): e_.matmul(pt[:, :], lhsT=wmat[:, wpos[(k, dl)], :], rhs=ap_,
                                                                                                     start=(si == 0), stop=(si == ns - 1)),
                                 reads=["p_wm", rn], writes=[pn])
                        P.op("dve", lambda e_, pt=pt, z_=z_, blk=blk, gi=gi, cols=cols: e_.scalar_tensor_tensor(out=z_[:, cols], in0=pt[:, :], scalar=invt[:, blk, gi:gi + 1],
                                                                                                       in1=p_own[:, blk, cols], op0=ALU.mult, op1=ALU.subtract),
                             reads=[pn, "p_inv", f"p_own_{blk}"], writes=[zn])
                    for half in range(2):
                        for c8 in range(8):
                            cc = half * 8 + c8
                            P.op("pe", lambda e_, c8=c8, cc=cc, z_=z_: e_.transpose(out=self.pst[:, c8 * 128:(c8 + 1) * 128], in_=z_[:, cc * 128:(cc + 1) * 128],
                                                                                   identity=self.ident[:]), reads=[zn, "ident"], writes=["pst"])
                        P.op("dve", lambda e_, half=half, b=b: e_.tensor_copy(out=zT[:, half * 8:(half + 1) * 8, b * 128:(b + 1) * 128],
                                                                            in_=self.pst[:, :].rearrange("p (c t) -> p c t", c=8)), reads=["pst"], writes=["p_zT"])
                for gi in range(4):
                    for dc in range(4):
                        m = gi * 4 + dc
                        pt, pn = self.next_ps()
                        for cc in range(4):
                            P.op("pe", lambda e_, pt=pt, gi=gi, cc=cc, dc=dc: e_.matmul(pt[:, :TT], lhsT=wg[:, gi, cc, dc * 128:(dc + 1) * 128], rhs=zT[:, gi * 4 + cc, :],
                                                                                      start=(cc == 0), stop=(cc == 3)), reads=["p_wg", "p_zT"], writes=[pn])
                        t_ = tt_[m % 2]
                        P.op("dve", lambda e_, pt=pt, t_=t_, m=m: e_.tensor_scalar(out=t_[:], in0=pt[:, :TT], scalar1=bg[:, o, m:m + 1], scalar2=sg1[:, m:m + 1],
                                                                                 op0=ALU.add, op1=ALU.mult), reads=[pn, "p_bg", "p_sg1"], writes=[f"p_t{m % 2}"])
                        P.op("pool", lambda e_, t_=t_, m=m: e_.tensor_tensor(out=xt[:, m, :], in0=xt[:, m, :], in1=t_[:], op=ALU.add), reads=[f"p_t{m % 2}", "p_x3"], writes=["p_x3"])
                P.dma("sp", lambda e_, t0=t0: e_.dma_start(out=dst[:, t0:t0 + TT].rearrange("(kc p) t -> p kc t", p=128), in_=xt[:]), reads=["p_x3"], writes=["dram_x"])
            P.barrier()


def _fm(v):
    v = np.asarray(v, np.float32)
    lead = v.shape[:-1]
    return np.ascontiguousarray(np.moveaxis(v.reshape(lead + (KC, 128)), -1, 0))


LAUNCHES = [
    ["adaln", ("even_a", 0)],
    [("even_b", 0), ("ffn", 0), ("odd_a", 1)],
    [("odd_b", 1), ("ffn", 1), ("even_a", 2)],
    [("even_b", 2), ("ffn", 2), ("odd_a", 3)],
    [("odd_b", 3), ("ffn", 3), "final"],
]


def needed_inputs(stages):
    need = {"xT", "ctxT", "gmix", "gffn", "gfin", "cmat", "yT"}
    kinds = [t if isinstance(t, str) else t[0] for t in stages]
    if "adaln" in kinds:
        need |= {"cvec", "w_ada", "b_ada", "mods_out"}
    else:
        need |= {"mods_in"}
    if "final" not in kinds:
        need |= {"cres_out"}
    if "ffn" in kinds:
        need |= {"w_up", "w_down"}
    if "even_a" in kinds:
        need |= {"w_in_even", "w_out_even", "lb_logits", "g_hgrn", "wsT", "b_sp", "g_spv", "sc_out", "pack_out", "csc_out"}
    if "even_b" in kinds:
        need |= {"w_out_even", "g_hgrn", "sc_in", "pack_in", "coef", "csc_in"}
    if "odd_a" in kinds:
        need |= {"w_in_pool", "halo_out", "p_out"}
    if "odd_b" in kinds:
        need |= {"halo_in", "p_in", "coef", "w_grp", "b_grp", "sc_pool", "inv2", "inv1"}
        if stage_layer(stages, ("odd_b",)) == 1:
            need |= {"w_in_pool"}
    return need


def stage_layer(stages, kind):
    for t in stages:
        if not isinstance(t, str) and t[0] in kind:
            return t[1]
    return None


def make_in_maps(inputs, stages, state):
    f = lambda k: np.ascontiguousarray(np.asarray(inputs[k], np.float32))
    need = needed_inputs(stages)
    c, c_ctx = f("c"), f("c_ctx")
    shared = {"gmix": _fm(f("g_norm_mix")), "gffn": _fm(f("g_norm_ffn")), "gfin": _fm(f("g_norm_final")), "cmat": make_cmat()}
    if "w_ada" in need:
        shared["w_ada"] = f("w_ada")
        shared["b_ada"] = np.ascontiguousarray(f("b_ada").reshape(DEPTH, 96, 128).transpose(0, 2, 1))
    Lf = stage_layer(stages, ("ffn",))
    if Lf is not None:
        shared["w_up"] = f("w_ffn_up")[Lf]
        shared["w_down"] = f("w_ffn_down")[Lf]
    Le = stage_layer(stages, ("even_a", "even_b"))
    if Le is not None:
        e = Le // 2
        shared["w_out_even"] = f("w_out_even")[e]
        shared["g_hgrn"] = np.ascontiguousarray(f("g_hgrn_out").reshape(2, 8, 128).transpose(2, 0, 1))
        if "w_in_even" in need:
            shared["w_in_even"] = f("w_in_even")[e]
            shared["lb_logits"] = f("lb_logits")
            shared["wsT"] = np.ascontiguousarray(f("w_spatial")[e].transpose(0, 2, 1))
            shared["b_sp"] = f("b_spatial")[e]
            shared["g_spv"] = f("g_spatial_v")[e]
    Lo = stage_layer(stages, ("odd_a", "odd_b"))
    if Lo is not None:
        o = Lo // 2
        if "w_in_pool" in need:
            shared["w_in_pool"] = f("w_in_pool")[o]
        if "w_grp" in need:
            shared["w_grp"] = f("w_grp_pool")[o]
            shared["b_grp"] = _fm(f("b_grp_pool").reshape(2, D))
            shared["sc_pool"] = _fm(f("scale_pool"))
    if "pack_in" in need:
        shared["pack_in"] = state["pack"]
    if "halo_in" in need:
        shared["halo_in"] = state["halo"]
    maps = []
    for core in range(NCORES):
        b, seg = core // 4, core % 4
        m = dict(shared)
        m["xT"] = state["xT"][core]
        m["ctxT"] = state["ctxT"][core]
        if "cvec" in need:
            m["cvec"] = np.ascontiguousarray(np.stack([c[b].reshape(KC, 128).T, c_ctx.reshape(KC, 128).T], axis=-1))
        if "mods_in" in need:
            m["mods_in"] = state["mods"][core]
        if "sc_in" in need:
            m["sc_in"] = state["sc"][core]
            m["csc_in"] = state["csc"][core]
            for k in Builder.SCR_KEYS:
                m["scr_" + k] = state["scr"][core][k]
        if "coef" in need:
            coef = np.zeros((128, 4, NCORES), np.float32)
            for j in range(NCORES):
                if j // 4 == b:
                    coef[:, 0, j] = 1.0 if (j % 4) < seg else 0.0
                    coef[:, 1, j] = 1.0 if (j % 4) > seg else 0.0
                    coef[:, 2, j] = 1.0 if (j % 4) == seg - 1 else 0.0
                    coef[:, 3, j] = 1.0 if (j % 4) == seg + 1 else 0.0
            m["coef"] = coef
        if "p_in" in need:
            m["p_in"] = state["p"][core]
        if "inv2" in need:
            inv2, inv1 = make_invcnt(seg)
            m["inv2"] = np.ascontiguousarray(inv2.reshape(16, 128, 4).transpose(1, 0, 2))
            m["inv1"] = np.ascontiguousarray(inv1.reshape(2, 128, 4).transpose(1, 0, 2))
        maps.append({k: v for k, v in m.items() if k in need or k.startswith("scr_")})
    return maps


def init_state(inputs, x_override=None, ctx_override=None):
    x = np.asarray(inputs["x"], np.float32) if x_override is None else x_override
    ctx = np.asarray(inputs["ctx"], np.float32) if ctx_override is None else ctx_override
    st = {"xT": [], "ctxT": []}
    for core in range(NCORES):
        b, seg = core // 4, core % 4
        st["xT"].append(np.ascontiguousarray(x[b, seg * NT:(seg + 1) * NT, :].T))
        st["ctxT"].append(np.ascontiguousarray(ctx[b].T))
    return st


def run_launch(inputs, stages, state):
    import time
    t0 = time.time()
    nc = Builder(stages).build()
    maps = make_in_maps(inputs, stages, state)
    print("launch", stages, "build+maps s", round(time.time() - t0, 1), flush=True)
    res = run_bass_kernel_spmd(nc, maps, core_ids=list(range(NCORES)))
    print("launch done s", round(time.time() - t0, 1), flush=True)
    r = res.results
    state["dbg"] = {k: np.asarray(v) for k, v in r[0].items() if k.startswith("dbg_")}
    state["xT"] = [np.asarray(r[c]["yT"]) for c in range(NCORES)]
    if "cres_out" in r[0]:
        state["ctxT"] = [np.asarray(r[c]["cres_out"]) for c in range(NCORES)]
    if "mods_out" in r[0]:
        state["mods"] = [np.asarray(r[c]["mods_out"]) for c in range(NCORES)]
    if "sc_out" in r[0]:
        state["sc"] = [np.asarray(r[c]["sc_out"]) for c in range(NCORES)]
        state["pack"] = np.concatenate([np.asarray(r[c]["pack_out"]) for c in range(NCORES)], axis=0)
        state["csc"] = [np.asarray(r[c]["csc_out"]) for c in range(NCORES)]
        state["scr"] = [{k: np.asarray(r[c]["scr_" + k]) for k in Builder.SCR_KEYS} for c in range(NCORES)]
    if "p_out" in r[0]:
        state["p"] = [np.asarray(r[c]["p_out"]) for c in range(NCORES)]
    if "halo_out" in r[0]:
        state["halo"] = np.concatenate([np.asarray(r[c]["halo_out"]) for c in range(NCORES)], axis=0)
    return res


def assemble(state):
    out = np.zeros((2, 4 * NT, D), np.float32)
    for core in range(NCORES):
        b, seg = core // 4, core % 4
        out[b, seg * NT:(seg + 1) * NT, :] = state["xT"][core].T
    return out


def kernel(**inputs):
    state = init_state(inputs)
    for stages in LAUNCHES:
        run_launch(inputs, stages, state)
    return assemble(state)
```

```python
import contextlib
import numpy as np
import concourse.bass as bass
import concourse.mybir as mybir
from concourse.bass_utils import run_bass_kernel_spmd

F32 = mybir.dt.float32
BF16 = mybir.dt.bfloat16
AF = mybir.ActivationFunctionType
ALU = mybir.AluOpType

D = 2048
KC = 16
NT = 2048
NCX = 256
DEPTH = 4
DFF = 8192
EPS = 1e-6
NCORES = 8

SEM_ROT = 16000
N_DMA_SEM = 12
SAME_ENGINE_SYNC = True
DEBUG_CUT = 0
FFN_PRECAST = False
DEBUG_DUMP = False


class Prog:
    ENG = ("pe", "dve", "act", "pool", "sp")

    def __init__(self, nc):
        self.nc = nc
        self.stream = {e: [] for e in self.ENG}
        self.count = {e: 0 for e in self.ENG}
        self.seen = {e: {} for e in self.ENG}
        self.last_w = {}
        self.readers = {}
        self.dma_n = {"sp": 0, "pool": 0, "act": 0}
        self.dma_val = {}
        self.sem_keys = set()
        self.cc_n = 0
        self.last_ticket = {}

    def _deps(self, eng, reads, writes):
        deps = []
        for r in reads:
            t = self.last_w.get(r)
            if t is not None:
                deps.append(t)
        for w in writes:
            t = self.last_w.get(w)
            if t is not None:
                deps.append(t)
            deps.extend(self.readers.get(w, ()))
        best = {}
        for (sk, val, e) in deps:
            if e == eng and (eng == "pe" or not SAME_ENGINE_SYNC) and sk[0] == "c":
                continue
            if best.get(sk, 0) < val:
                best[sk] = val
        out = []
        for sk, val in best.items():
            if self.seen[eng].get(sk, 0) >= val:
                continue
            self.seen[eng][sk] = val
            out.append((sk, val))
        return out

    def _commit(self, ticket, reads, writes):
        for r in reads:
            self.readers.setdefault(r, []).append(ticket)
        for w in writes:
            self.last_w[w] = ticket
            self.readers[w] = []
        self.last_ticket[ticket[0]] = ticket

    def op(self, eng, fn, reads=(), writes=()):
        waits = self._deps(eng, reads, writes)
        self.count[eng] += 1
        n = self.count[eng]
        sk = ("c", eng, (n - 1) // SEM_ROT)
        val = (n - 1) % SEM_ROT + 1
        self.sem_keys.add(sk)
        ticket = (sk, val, eng)
        for w in waits:
            self.stream[eng].append(("wait", w))
        self.stream[eng].append(("op", fn, (sk, 1)))
        self._commit(ticket, reads, writes)
        return ticket

    def dma(self, q, fn, reads=(), writes=()):
        waits = self._deps(q, reads, writes)
        i = self.dma_n[q]
        self.dma_n[q] += 1
        sk = ("d", q, i % N_DMA_SEM)
        self.sem_keys.add(sk)
        prev = self.dma_val.get(sk, 0)
        if prev > 0 and self.seen[q].get(sk, 0) < prev:
            self.seen[q][sk] = prev
            waits.append((sk, prev))
        val = prev + 16
        self.dma_val[sk] = val
        ticket = (sk, val, "dma_" + q)
        for w in waits:
            self.stream[q].append(("wait", w))
        self.stream[q].append(("op", fn, (sk, 16)))
        self._commit(ticket, reads, writes)
        return ticket

    def cc(self, fn, reads=(), writes=()):
        q = "pool"
        waits = self._deps(q, reads, writes)
        self.cc_n += 1
        sk = ("k", "cc", 0)
        self.sem_keys.add(sk)
        ticket = (sk, self.cc_n, "cc")
        for w in waits:
            self.stream[q].append(("wait", w))
        self.stream[q].append(("cc", fn, sk))
        self.seen[q][sk] = self.cc_n
        self.stream[q].append(("wait", (sk, self.cc_n)))
        self._commit(ticket, reads, writes)
        return ticket

    def barrier(self):
        tickets = list(self.last_ticket.values())
        for eng in self.ENG:
            for (sk, val, e) in tickets:
                if e == eng and sk[0] == "c":
                    continue
                if self.seen[eng].get(sk, 0) < val:
                    self.seen[eng][sk] = val
                    self.stream[eng].append(("wait", (sk, val)))

    def emit(self):
        nc = self.nc
        with contextlib.ExitStack() as st:
            sems = {}
            for sk in sorted(self.sem_keys):
                sems[sk] = st.enter_context(nc.semaphore("s_" + "_".join(str(x) for x in sk)))
            block = st.enter_context(nc.Block())

            def run(engname):
                def body(e):
                    for item in self.stream[engname]:
                        if item[0] == "wait":
                            sk, val = item[1]
                            e.wait_ge(sems[sk], val)
                        elif item[0] == "cc":
                            item[1](e).then_inc(sems[item[2]])
                        else:
                            _, fn, (sk, inc) = item
                            fn(e).then_inc(sems[sk], inc)
                return body

            block.sync(run("sp"))
            block.tensor(run("pe"))
            block.vector(run("dve"))
            block.scalar(run("act"))
            block.gpsimd(run("pool"))


C_IDENT = 0
C_MDF, C_MQF, C_MDB, C_MQB = 1, 2, 3, 4
C_MCF, C_MCB = 5, 6
C_MASKF, C_MASKB = 7, 8
C_W2D = 9
C_W1D = 28
N_CMAT = 40
POOL_WINDOWS = (2, 4, 8, 16)
W2D_DELTAS = {2: (-1, 0), 4: (-1, 0, 1), 8: (-2, -1, 0, 1, 2), 16: (-4, -3, -2, -1, 0, 1, 2, 3, 4)}
W1D_DELTAS = (-1, 0, 1)
REF_F = 31
REF_B = 32


def _w2d_index():
    idx = {}
    n = C_W2D
    for k in POOL_WINDOWS:
        for dl in W2D_DELTAS[k]:
            idx[(k, dl)] = n
            n += 1
    return idx


def _w1d_index():
    idx = {}
    n = C_W1D
    for k in POOL_WINDOWS:
        for dl in W1D_DELTAS:
            idx[(k, dl)] = n
            n += 1
    return idx


W2D_IDX = _w2d_index()
W1D_IDX = _w1d_index()


def make_cmat():
    cm = np.zeros((N_CMAT, 128, 128), np.float32)
    cm[C_IDENT] = np.eye(128, dtype=np.float32)
    s = np.arange(128)[:, None]
    t = np.arange(128)[None, :]
    same = (s // 64) == (t // 64)
    js, jt = s % 64, t % 64
    mdf = ((js <= REF_F).astype(np.float32) - (js <= jt).astype(np.float32)) * same
    cm[C_MDF] = mdf
    cm[C_MQF] = -mdf
    mdb = ((js >= REF_B).astype(np.float32) - (js >= jt).astype(np.float32)) * same
    cm[C_MDB] = mdb
    cm[C_MQB] = -mdb
    for c in range(2):
        inc = (np.arange(128) // 64 == c)
        j = np.arange(128) % 64
        cm[C_MCF][:, 3 * c + 0] = inc * (j <= REF_F)
        cm[C_MCF][:, 3 * c + 1] = inc * (j > REF_F)
        cm[C_MCF][:, 3 * c + 2] = inc
        cm[C_MCB][:, 3 * c + 0] = inc * (j >= REF_B)
        cm[C_MCB][:, 3 * c + 1] = inc * (j < REF_B)
        cm[C_MCB][:, 3 * c + 2] = inc
    s64 = np.arange(64)[:, None]
    t64 = np.arange(64)[None, :]
    cm[C_MASKF][:64, 0:64] = (s64 <= t64)
    cm[C_MASKF][:64, 64:128] = (s64 <= t64)
    cm[C_MASKB][:64, 0:64] = (s64 >= t64)
    cm[C_MASKB][:64, 64:128] = (s64 >= t64)
    for k in POOL_WINDOWS:
        for dl in W2D_DELTAS[k]:
            rs = 2 * dl + (np.arange(128) // 64)[:, None]
            cs = (np.arange(128) % 64)[:, None]
            rt = (np.arange(128) // 64)[None, :]
            ct = (np.arange(128) % 64)[None, :]
            inr = (rs >= rt - k // 2) & (rs < rt - k // 2 + k)
            inc_ = (cs >= ct - k // 2) & (cs < ct - k // 2 + k)
            cm[W2D_IDX[(k, dl)]] = (inr & inc_).astype(np.float32)
        for dl in W1D_DELTAS:
            ps = 128 * dl + np.arange(128)[:, None]
            pt = np.arange(128)[None, :]
            cm[W1D_IDX[(k, dl)]] = ((ps >= pt - k // 2) & (ps < pt - k // 2 + k)).astype(np.float32)
    return cm


def window_bounds(n, k):
    t = np.arange(n)
    return np.clip(t - k // 2, 0, n), np.clip(t - k // 2 + k, 0, n)


def make_invcnt(seg):
    rows = 128
    inv2 = np.zeros((NT, 4), np.float32)
    inv1 = np.zeros((NCX, 4), np.float32)
    for gi, k in enumerate(POOL_WINDOWS):
        rlo, rhi = window_bounds(rows, k)
        clo, chi = window_bounds(64, k)
        cnt = ((rhi - rlo)[:, None] * (chi - clo)[None, :]).astype(np.float32)
        inv2[:, gi] = (1.0 / cnt[seg * 32:(seg + 1) * 32]).reshape(-1)
        lo, hi = window_bounds(NCX, k)
        inv1[:, gi] = 1.0 / (hi - lo).astype(np.float32)
    return inv2, inv1


class Builder:
    def __init__(self, stages):
        self.stages = list(stages)
        nc = self.nc = bass.Bass("TRN2", target_bir_lowering=False)
        self.P = Prog(nc)
        dt = nc.dram_tensor
        need = needed_inputs(self.stages)
        self.need = need

        def inp(name, shape, dtype=F32):
            if name not in need:
                return None
            return dt(name, list(shape), dtype, kind="ExternalInput").ap()

        def outp(name, shape, dtype=F32):
            if name not in need:
                return None
            return dt(name, list(shape), dtype, kind="ExternalOutput").ap()

        self.cvec3 = inp("cvec3", [128, KC, 3])
        self.w_ada_sh = inp("w_ada_sh", [DEPTH, D, 1536])
        self.b_ada_sh = inp("b_ada_sh", [128, DEPTH, 12])
        self.mods_part = outp("mods_part", [128, DEPTH, 12, 3])
        self.xT = inp("xT", [D, NT])
        self.ctxT = inp("ctxT", [D, NCX])
        self.cvec = inp("cvec", [128, KC, 2])
        self.w_ada = inp("w_ada", [DEPTH, D, 6 * D])
        self.b_ada = inp("b_ada", [DEPTH, 128, 96])
        self.mods_in = inp("mods_in", [128, DEPTH, 96, 2])
        self.gmix = inp("gmix", [128, DEPTH, KC])
        self.gffn = inp("gffn", [128, DEPTH, KC])
        self.gfin = inp("gfin", [128, KC])
        self.w_up = inp("w_up", [D, DFF])
        self.w_down = inp("w_down", [DFF, D])
        self.w_in_even = inp("w_in_even", [D, 7168])
        self.w_out_even = inp("w_out_even", [D, D])
        self.lb_logits = inp("lb_logits", [2, 2, 1024])
        self.g_hgrn = inp("g_hgrn", [128, 2, 8])
        self.wsT = inp("wsT", [8, 128, 128])
        self.b_sp = inp("b_sp", [8, 128])
        self.g_spv = inp("g_spv", [1024])
        self.w_in_pool = inp("w_in_pool", [D, D])
        self.w_grp = inp("w_grp", [4, 512, 512])
        self.b_grp = inp("b_grp", [128, 2, KC])
        self.sc_pool = inp("sc_pool", [128, 2, KC])
        self.cmat = inp("cmat", [N_CMAT, 128, 128])
        self.inv2 = inp("inv2", [128, 16, 4])
        self.inv1 = inp("inv1", [128, 2, 4])
        self.coef = inp("coef", [128, 4, NCORES])
        self.sc_in = inp("sc_in", [128, 16, 128])
        self.pack_in = inp("pack_in", [NCORES * 2048, 132])
        self.halo_in = inp("halo_in", [1024, D], BF16)
        self.yT = outp("yT", [D, NT])
        self.cres_out = outp("cres_out", [D, NCX])
        self.mods_out = outp("mods_out", [128, DEPTH, 96, 2])
        self.sc_out = outp("sc_out", [128, 16, 128])
        self.pack_out = outp("pack_out", [2048, 132])
        self.halo_out = outp("halo_out", [1024, D], BF16)
        self.p_out = outp("p_out", [NT, D], BF16)
        self.p_in = inp("p_in", [NT, D], BF16)
        self.csc_out = outp("csc_out", [128, 2 * 8 * (NT // 64) * 3])
        self.csc_in = inp("csc_in", [128, 2 * 8 * (NT // 64) * 3])
        self.xres = dt("xres", [D, NT], F32).ap()
        self.cres = dt("cres", [D, NCX], F32).ap()
        self.st = contextlib.ExitStack()
        self.ps_rr = 0

    _uid = 0

    def sb(self, stack, name, shape, dtype):
        Builder._uid += 1
        return stack.enter_context(self.nc.sbuf_tensor(f"{name}_u{Builder._uid}", list(shape), dtype))

    def next_ps(self):
        i = self.ps_rr % 7
        self.ps_rr += 1
        return self.ps[i], f"ps{i}"

    def build_adaln_shard(self):
        nc, P = self.nc, self.P
        with self.st as st:
            pt = st.enter_context(nc.psum_tensor("ps_ad", [128, 512], F32))
            cv = self.sb(st, "cv", [128, KC, 3], F32)
            scv = self.sb(st, "scv", [128, KC, 3], BF16)
            wb = [self.sb(st, f"adw{i}", [128, KC, 512], BF16) for i in range(3)]
            bias = self.sb(st, "adb", [128, DEPTH, 12], F32)
            mp = self.sb(st, "mp", [128, DEPTH, 12, 3], F32)
            P.dma("sp", lambda e: e.dma_start(out=cv[:], in_=self.cvec3), writes=["cv"])
            P.dma("sp", lambda e: e.dma_start(out=bias[:], in_=self.b_ada_sh), writes=["adb"])
            P.op("act", lambda e: e.activation(out=scv[:], in_=cv[:], func=AF.Silu), reads=["cv"], writes=["scv"])
            n = 0
            for L in range(DEPTH):
                for cb in range(3):
                    slot = n % 3
                    n += 1
                    w = wb[slot]
                    src = self.w_ada_sh[L][:, cb * 512:(cb + 1) * 512].rearrange("(kc p) m -> p kc m", p=128)
                    P.dma("pool", lambda e, w=w, src=src: e.dma_start(out=w[:], in_=src), writes=[f"adw{slot}"])
                    for mc in range(4):
                        c = L * 12 + cb * 4 + mc
                        for kc in range(KC):
                            P.op("pe", lambda e, w=w, kc=kc, mc=mc, c=c: e.matmul(
                                pt[:, 3 * c:3 * c + 3], lhsT=w[:, kc, mc * 128:(mc + 1) * 128], rhs=scv[:, kc, :],
                                start=(kc == 0), stop=(kc == KC - 1)), reads=[f"adw{slot}", "scv"], writes=["ps_ad"])
            for v in range(3):
                P.op("dve", lambda e, v=v: e.tensor_tensor(
                    out=mp[:, :, :, v], in0=pt[:, 0:144].rearrange("p (l c v) -> p l c v", l=DEPTH, v=3)[:, :, :, v],
                    in1=bias[:], op=ALU.add), reads=["ps_ad", "adb"], writes=["mp"])
            P.dma("sp", lambda e: e.dma_start(out=self.mods_part, in_=mp[:]), reads=["mp"], writes=["dram_mp"])
            P.barrier()
            P.emit()
        return nc

    def build(self):
        nc, P = self.nc, self.P
        if self.stages == ["adaln_s"]:
            return self.build_adaln_shard()
        with self.st as st:
            self.ps = [st.enter_context(nc.psum_tensor(f"ps{i}", [128, 512], F32)) for i in range(7)]
            self.pst = st.enter_context(nc.psum_tensor("pst", [128, 1024], BF16))
            self.ident = self.sb(st, "ident", [128, 128], BF16)
            self.ones = self.sb(st, "ones", [128, 128], BF16)
            self.mods = self.sb(st, "mods", [128, DEPTH, 96, 2], F32)
            self.A1 = self.sb(st, "A1", [128, DEPTH, KC, 2], F32)
            self.A2 = self.sb(st, "A2", [128, DEPTH, KC, 2], F32)
            self.gm = self.sb(st, "gm", [128, DEPTH, KC], F32)
            self.gf = self.sb(st, "gf", [128, DEPTH, KC], F32)
            self.gfin_sb = self.sb(st, "gfin_sb", [128, KC], F32)
            P.dma("pool", lambda e: e.dma_start(out=self.ident[:], in_=self.cmat[C_IDENT]), writes=["ident"])
            P.op("dve", lambda e: e.memset(self.ones[:], 1.0), writes=["ones"])
            self.epsb = self.sb(st, "epsb", [128, 1], F32)
            P.op("dve", lambda e: e.memset(self.epsb[:], EPS), writes=["epsb"])
            P.dma("sp", lambda e: e.dma_start(out=self.gm[:], in_=self.gmix), writes=["gm"])
            P.dma("sp", lambda e: e.dma_start(out=self.gf[:], in_=self.gffn), writes=["gf"])
            P.dma("sp", lambda e: e.dma_start(out=self.gfin_sb[:], in_=self.gfin), writes=["gfin"])
            if "adaln" in self.stages:
                self.phase_adaln()
                P.dma("sp", lambda e: e.dma_start(out=self.mods_out, in_=self.mods[:]), reads=["mods"], writes=["dram_mods"])
            else:
                P.dma("sp", lambda e: e.dma_start(out=self.mods[:], in_=self.mods_in), writes=["mods"])
            self.mods_derived()
            self.cur_x, self.cur_c = self.xT, self.ctxT
            final = False
            if FFN_PRECAST and "ffn" in [t[0] for t in self.stages if not isinstance(t, str)] and self.stages[0][0] == "even_b":
                self.ffn_precast()
            for tok in self.stages:
                if tok == "adaln":
                    continue
                if tok == "final":
                    self.phase_final(self.cur_x)
                    final = True
                    continue
                kind, L = tok
                if kind == "even_a":
                    self.phase_even_a(L)
                elif kind == "even_b":
                    self.phase_even_b(L)
                elif kind == "odd_a":
                    self.phase_odd(L, part="a")
                elif kind == "odd_b":
                    self.phase_odd(L, part="b")
                elif kind == "ffn":
                    self.phase_ffn(L, self.cur_x, self.xres, NT, 0)
                    self.cur_x = self.xres
                    if L < 2:
                        self.phase_ffn(L, self.cur_c, self.cres, NCX, 1)
                        self.cur_c = self.cres
            if not final:
                self.phase_dump(self.cur_x, self.yT, NT)
                if self.cres_out is not None:
                    self.phase_dump(self.cur_c, self.cres_out, NCX)
            P.barrier()
            P.emit()
        return nc

    def ffn_precast(self):
        P = self.P
        self.wbf_up = self.nc.dram_tensor("wbf_up", [16, 128, KC, 512], BF16).ap()
        self.wbf_dn = self.nc.dram_tensor("wbf_dn", [16, 128, 4, D], BF16).ap()
        for hb in range(16):
            P.dma("pool", lambda e, hb=hb: e.dma_start(out=self.wbf_up[hb], in_=self.w_up[:, hb * 512:(hb + 1) * 512].rearrange("(kc p) m -> p kc m", p=128)),
                  writes=[f"wbf_up{hb}"])
            P.dma("pool", lambda e, hb=hb: e.dma_start(out=self.wbf_dn[hb], in_=self.w_down[hb * 512:(hb + 1) * 512, :].rearrange("(hc p) m -> p hc m", p=128)),
                  writes=[f"wbf_dn{hb}"])

    def mods_derived(self):
        P = self.P
        for L in range(DEPTH):
            for j in range(2):
                P.op("dve", lambda e, L=L, j=j: e.scalar_tensor_tensor(
                    out=self.A1[:, L, :, j], in0=self.mods[:, L, 16:32, j], scalar=1.0, in1=self.gm[:, L, :],
                    op0=ALU.add, op1=ALU.mult), reads=["mods", "gm"], writes=["A1"])
                P.op("dve", lambda e, L=L, j=j: e.scalar_tensor_tensor(
                    out=self.A2[:, L, :, j], in0=self.mods[:, L, 64:80, j], scalar=1.0, in1=self.gf[:, L, :],
                    op0=ALU.add, op1=ALU.mult), reads=["mods", "gf"], writes=["A2"])

    def phase_adaln(self):
        nc, P = self.nc, self.P
        with contextlib.ExitStack() as ph:
            cv = self.sb(ph, "cv", [128, KC, 2], F32)
            scv = self.sb(ph, "scv", [128, KC, 2], BF16)
            wb = [self.sb(ph, f"adw{i}", [128, KC, 512], BF16) for i in range(2)]
            bias = self.sb(ph, "adb", [128, DEPTH, 96], F32)
            P.dma("sp", lambda e: e.dma_start(out=cv[:], in_=self.cvec), writes=["cv"])
            P.dma("sp", lambda e: e.dma_start(out=bias[:], in_=self.b_ada.rearrange("l p c -> p l c")), writes=["adb"])
            P.op("act", lambda e: e.activation(out=scv[:], in_=cv[:], func=AF.Silu), reads=["cv"], writes=["scv"])
            n = 0
            for L in range(DEPTH):
                pt, pn = self.next_ps()
                for cb in range(24):
                    slot = n % 2
                    n += 1
                    w = wb[slot]
                    src = self.w_ada[L][:, cb * 512:(cb + 1) * 512].rearrange("(kc p) m -> p kc m", p=128)
                    P.dma("pool", lambda e, w=w, src=src: e.dma_start(out=w[:], in_=src), writes=[f"adw{slot}"])
                    for mc in range(4):
                        c = cb * 4 + mc
                        for kc in range(KC):
                            P.op("pe", lambda e, w=w, kc=kc, mc=mc, c=c, pt=pt: e.matmul(
                                pt[:, 2 * c:2 * c + 2], lhsT=w[:, kc, mc * 128:(mc + 1) * 128], rhs=scv[:, kc, :],
                                start=(kc == 0), stop=(kc == KC - 1)), reads=[f"adw{slot}", "scv"], writes=[pn])
                for j in range(2):
                    P.op("dve", lambda e, L=L, j=j, pt=pt: e.tensor_tensor(
                        out=self.mods[:, L, :, j], in0=pt[:, 0:192].rearrange("p (c j) -> p c j", j=2)[:, :, j],
                        in1=bias[:, L, :], op=ALU.add), reads=[pn, "adb"], writes=["mods"])
            P.barrier()

    def norm_mod(self, xt, xname, tt, A, B, hn, hname, sq, sqname, tmp, tmpname, rs, rsname):
        P = self.P
        for h in range(0, tt, 512):
            w = min(512, tt - h)
            P.op("act", lambda e, h=h, w=w: e.activation(out=sq[:, :, :w], in_=xt[:, :, h:h + w], func=AF.Square),
                 reads=[xname], writes=[sqname])
            pt, pn = self.next_ps()
            for kc in range(KC):
                P.op("pe", lambda e, kc=kc, w=w, pt=pt: e.matmul(pt[:, :w], lhsT=self.ones[:], rhs=sq[:, kc, :w],
                                                                 start=(kc == 0), stop=(kc == KC - 1)),
                     reads=["ones", sqname], writes=[pn])
            P.op("act", lambda e, w=w, pt=pt: e.activation(out=rs[:, :w], in_=pt[:, :w], func=AF.Sqrt, bias=self.epsb[:, 0:1],
                                                           scale=1.0 / D), reads=[pn], writes=[rsname])
            P.op("dve", lambda e, w=w: e.reciprocal(out=rs[:, :w], in_=rs[:, :w]), reads=[rsname], writes=[rsname])
            for kc in range(KC):
                tn = f"{tmpname}{kc % 2}"
                tb = tmp[kc % 2]
                P.op("dve", lambda e, kc=kc, tb=tb, h=h, w=w: e.scalar_tensor_tensor(
                    out=tb[:, :w], in0=xt[:, kc, h:h + w], scalar=A[:, kc:kc + 1], in1=rs[:, :w], op0=ALU.mult, op1=ALU.mult),
                    reads=[xname, rsname, "A1", "A2"], writes=[tn])
                P.op("act", lambda e, kc=kc, tb=tb, h=h, w=w: e.activation(
                    out=hn[:, kc, h:h + w], in_=tb[:, :w], func=AF.Identity, bias=B[:, kc:kc + 1], scale=1.0),
                    reads=[tn, "mods"], writes=[hname])

    def phase_ffn(self, L, src, dst, ntok, j):
        nc, P = self.nc, self.P
        TT = min(1024, ntok)
        with contextlib.ExitStack() as ph:
            xt = self.sb(ph, "f_x", [128, KC, TT], F32)
            hn = self.sb(ph, "f_hn", [128, KC, TT], BF16)
            w1 = [self.sb(ph, f"f_w1{i}", [128, KC, 512], BF16) for i in range(2)]
            w2 = [self.sb(ph, f"f_w2{i}", [128, 4, D], BF16) for i in range(2)]
            hid = [self.sb(ph, "f_hid0", [128, 4, TT], BF16)]
            sq = self.sb(ph, "f_sq", [128, KC, 512], BF16)
            rl = [self.sb(ph, f"f_rl{i}", [128, 512], F32) for i in range(2)]
            tmp = [self.sb(ph, f"f_tmp{i}", [128, 512], F32) for i in range(2)]
            rs = self.sb(ph, "f_rs", [128, 512], F32)
            A = self.A2[:, L, :, j]
            B = self.mods[:, L, 48:64, j]
            G = self.mods[:, L, 80:96, j]
            nblk = 0
            for t0 in range(0, ntok, TT):
                P.dma("sp", lambda e, t0=t0: e.dma_start(out=xt[:], in_=src[:, t0:t0 + TT].rearrange("(kc p) t -> p kc t", p=128)),
                      reads=["dram_x"], writes=["f_x"])
                self.norm_mod(xt, "f_x", TT, A, B, hn, "f_hn", sq, "f_sq", tmp, "f_tmp", rs, "f_rs")
                for hb in range(DFF // 512):
                    slot = nblk % 2
                    nblk += 1
                    a, b2, hd = w1[slot], w2[slot], hid[0]
                    if hasattr(self, "wbf_up"):
                        P.dma("sp", lambda e, a=a, hb=hb: e.dma_start(out=a[:], in_=self.wbf_up[hb]), reads=[f"wbf_up{hb}"], writes=[f"f_w1{slot}"])
                        P.dma("sp", lambda e, b2=b2, hb=hb: e.dma_start(out=b2[:], in_=self.wbf_dn[hb]), reads=[f"wbf_dn{hb}"], writes=[f"f_w2{slot}"])
                    else:
                        s1 = self.w_up[:, hb * 512:(hb + 1) * 512].rearrange("(kc p) m -> p kc m", p=128)
                        s2 = self.w_down[hb * 512:(hb + 1) * 512, :].rearrange("(hc p) m -> p hc m", p=128)
                        P.dma("pool", lambda e, a=a, s1=s1: e.dma_start(out=a[:], in_=s1), writes=[f"f_w1{slot}"])
                        P.dma("pool", lambda e, b2=b2, s2=s2: e.dma_start(out=b2[:], in_=s2), writes=[f"f_w2{slot}"])
                    k = 0
                    for hc in range(4):
                        for h in range(0, TT, 512):
                            w = min(512, TT - h)
                            pt, pn = self.next_ps()
                            for kc in range(KC):
                                P.op("pe", lambda e, a=a, kc=kc, hc=hc, h=h, w=w, pt=pt: e.matmul(
                                    pt[:, :w], lhsT=a[:, kc, hc * 128:(hc + 1) * 128], rhs=hn[:, kc, h:h + w],
                                    start=(kc == 0), stop=(kc == KC - 1)), reads=[f"f_w1{slot}", "f_hn"], writes=[pn])
                            r = rl[k % 2]
                            rn = f"f_rl{k % 2}"
                            k += 1
                            P.op("act", lambda e, r=r, w=w, pt=pt: e.activation(out=r[:, :w], in_=pt[:, :w], func=AF.Relu),
                                 reads=[pn], writes=[rn])
                            P.op("pool", lambda e, r=r, w=w, hd=hd, hc=hc, h=h: e.tensor_tensor(
                                out=hd[:, hc, h:h + w], in0=r[:, :w], in1=r[:, :w], op=ALU.mult), reads=[rn], writes=["f_hid0"])
                    for m in range(KC):
                        for h in range(0, TT, 512):
                            w = min(512, TT - h)
                            pt, pn = self.next_ps()
                            for hc in range(4):
                                P.op("pe", lambda e, b2=b2, hc=hc, m=m, h=h, w=w, hd=hd, pt=pt: e.matmul(
                                    pt[:, :w], lhsT=b2[:, hc, m * 128:(m + 1) * 128], rhs=hd[:, hc, h:h + w],
                                    start=(hc == 0), stop=(hc == 3)), reads=[f"f_w2{slot}", "f_hid0"], writes=[pn])
                            P.op("dve", lambda e, m=m, h=h, w=w, pt=pt: e.scalar_tensor_tensor(
                                out=xt[:, m, h:h + w], in0=pt[:, :w], scalar=G[:, m:m + 1], in1=xt[:, m, h:h + w],
                                op0=ALU.mult, op1=ALU.add), reads=[pn, "mods", "f_x"], writes=["f_x"])
                P.dma("sp", lambda e, t0=t0: e.dma_start(out=dst[:, t0:t0 + TT].rearrange("(kc p) t -> p kc t", p=128), in_=xt[:]),
                      reads=["f_x"], writes=["dram_x"])
            P.barrier()

    def phase_final(self, src):
        nc, P = self.nc, self.P
        TT = 512
        with contextlib.ExitStack() as ph:
            xt = [self.sb(ph, f"o_x{i}", [128, KC, TT], F32) for i in range(2)]
            sq = self.sb(ph, "o_sq", [128, KC, TT], BF16)
            rs = self.sb(ph, "o_rs", [128, TT], F32)
            for i, t0 in enumerate(range(0, NT, TT)):
                x = xt[i % 2]
                xn = f"o_x{i % 2}"
                P.dma("sp", lambda e, x=x, t0=t0: e.dma_start(out=x[:], in_=src[:, t0:t0 + TT].rearrange("(kc p) t -> p kc t", p=128)),
                      reads=["dram_x"], writes=[xn])
                P.op("act", lambda e, x=x: e.activation(out=sq[:], in_=x[:], func=AF.Square), reads=[xn], writes=["o_sq"])
                pt, pn = self.next_ps()
                for kc in range(KC):
                    P.op("pe", lambda e, kc=kc, pt=pt: e.matmul(pt[:, :], lhsT=self.ones[:], rhs=sq[:, kc, :], start=(kc == 0), stop=(kc == KC - 1)),
                         reads=["ones", "o_sq"], writes=[pn])
                P.op("act", lambda e, pt=pt: e.activation(out=rs[:], in_=pt[:], func=AF.Sqrt, bias=self.epsb[:, 0:1], scale=1.0 / D),
                     reads=[pn], writes=["o_rs"])
                P.op("dve", lambda e: e.reciprocal(out=rs[:], in_=rs[:]), reads=["o_rs"], writes=["o_rs"])
                for kc in range(KC):
                    P.op("dve", lambda e, x=x, kc=kc: e.scalar_tensor_tensor(out=x[:, kc, :], in0=x[:, kc, :], scalar=self.gfin_sb[:, kc:kc + 1],
                                                                             in1=rs[:], op0=ALU.mult, op1=ALU.mult),
                         reads=[xn, "o_rs", "gfin"], writes=[xn])
                P.dma("sp", lambda e, x=x, t0=t0: e.dma_start(out=self.yT[:, t0:t0 + TT].rearrange("(kc p) t -> p kc t", p=128), in_=x[:]),
                      reads=[xn], writes=["dram_y"])
            P.barrier()

    def phase_dump(self, src, dst, ntok):
        P = self.P
        with contextlib.ExitStack() as ph:
            W = min(512, ntok)
            xt = self.sb(ph, "d_x", [128, KC, W], F32)
            for t0 in range(0, ntok, W):
                P.dma("sp", lambda e, t0=t0: e.dma_start(out=xt[:], in_=src[:, t0:t0 + W].rearrange("(kc p) t -> p kc t", p=128)),
                      reads=["dram_x"], writes=["d_x"])
                P.dma("sp", lambda e, t0=t0: e.dma_start(out=dst[:, t0:t0 + W].rearrange("(kc p) t -> p kc t", p=128), in_=xt[:]),
                      reads=["d_x"], writes=["dram_y"])
            P.barrier()

    SCR_KEYS = ["QpT0", "QpT1", "KpT0", "KpT1", "Kp0", "Kp1", "V", "SGT", "YBT"]

    def scratch(self, n, tag, kind="Internal"):
        if not hasattr(self, "_scr"):
            self._scr = {}
        if tag not in self._scr:
            def mk(name, shape):
                if kind == "Internal":
                    return self.nc.dram_tensor(f"{name}_{tag}", shape, BF16).ap()
                return self.nc.dram_tensor(f"scr_{name}", shape, BF16, kind=kind).ap()
            fm, tm = [1024, n], [n, 1024]
            self._scr[tag] = dict(
                QpT=[mk("QpT0", fm), mk("QpT1", fm)], KpT=[mk("KpT0", fm), mk("KpT1", fm)],
                Kp=[mk("Kp0", tm), mk("Kp1", tm)], V=mk("V", tm), SGT=mk("SGT", fm), YBT=mk("YBT", fm))
        return self._scr[tag]

    def phase_even_a(self, L):
        nc, P = self.nc, self.P
        e = L // 2
        ctx_out = (L == 0)
        scr_l, scr_c = self.scratch(NT, "l", "ExternalOutput"), self.scratch(NCX, "c")
        with contextlib.ExitStack() as lay:
            csc_l = self.sb(lay, "csc_l", [128, 2, 8, NT // 64, 3], F32)
            csl_l = self.sb(lay, "csl_l", [128, 2, 8, NT // 64, 3], F32)
            csc_c = self.sb(lay, "csc_c", [128, 2, 8, NCX // 64, 3], F32)
            csl_c = self.sb(lay, "csl_c", [128, 2, 8, NCX // 64, 3], F32)
            self.even_local(L, e, self.cur_c, NCX, 1, scr_c, csc_c, csl_c, full=ctx_out)
            if DEBUG_CUT == 1 or DEBUG_CUT >= 11:
                return
            self.even_local(L, e, self.cur_x, NT, 0, scr_l, csc_l, csl_l, full=True)
            P.dma("sp", lambda e_: e_.dma_start(out=self.csc_out, in_=csc_l[:].rearrange("p d h c x -> p (d h c x)")), reads=["csc0"], writes=["dram_csc"])
            if DEBUG_CUT == 2:
                return
            with contextlib.ExitStack() as lay2:
                S_c = self.sb(lay2, "S_c", [128, 16, 128], F32)
                S_l = self.sb(lay2, "S_l", [128, 16, 128], F32)
                P.op("dve", lambda e_: e_.memset(S_c[:], 0.0), writes=[f"S_c_{i}" for i in range(16)])
                self.even_scan(L, e, scr_c, NCX, 1, csc_c, S_c, "S_c", ctx_out, self.cur_c, self.cres)
                if ctx_out:
                    self.cur_c = self.cres
                P.dma("sp", lambda e_: e_.dma_start(out=self.sc_out, in_=S_c[:]), reads=[f"S_c_{i}" for i in range(16)], writes=["dram_sc"])
                if DEBUG_CUT == 3:
                    P.barrier()
                    if DEBUG_DUMP:
                        self.debug_dump(scr_c, csc_c)
                    return
                P.op("dve", lambda e_: e_.memset(S_l[:], 0.0), writes=[f"S_l_{i}" for i in range(16)])
                self.even_scan(L, e, scr_l, NT, 0, csc_l, S_l, "S_l", False, None, None)
                self.even_pack(csl_l, S_l)
                P.barrier()

    def debug_dump(self, scr, csc):
        nc, P = self.nc, self.P
        names = [("QpT0", scr["QpT"][0]), ("QpT1", scr["QpT"][1]), ("KpT0", scr["KpT"][0]), ("KpT1", scr["KpT"][1]), ("SGT", scr["SGT"]), ("YBT", scr["YBT"]),
                 ("Kp0", scr["Kp"][0]), ("Kp1", scr["Kp"][1]), ("V", scr["V"])]
        with contextlib.ExitStack() as ph:
            for nm, ap in names:
                shp = list(ap.shape)
                o = nc.dram_tensor("dbg_" + nm, shp, F32, kind="ExternalOutput").ap()
                t = self.sb(ph, "dbgt_" + nm, [128, shp[0] // 128, shp[1]], F32)
                P.dma("pool", lambda e_, t=t, ap=ap: e_.dma_start(out=t[:], in_=ap.rearrange("(a p) c -> p a c", p=128)), writes=["dbg" + nm])
                P.dma("sp", lambda e_, t=t, o=o: e_.dma_start(out=o.rearrange("(a p) c -> p a c", p=128), in_=t[:]), reads=["dbg" + nm], writes=["dbgo" + nm])
            o = nc.dram_tensor("dbg_csc", [128, 2 * 8 * 4 * 3], F32, kind="ExternalOutput").ap()
            P.dma("sp", lambda e_: e_.dma_start(out=o, in_=csc[:].rearrange("p d h c x -> p (d h c x)")), reads=["csc1"], writes=["dbgocsc"])
            P.barrier()

    def phase_even_b(self, L):
        nc, P = self.nc, self.P
        e = L // 2
        scr_l = self.scratch(NT, "l", "ExternalInput")
        with contextlib.ExitStack() as lay:
            csc_l = self.sb(lay, "csc_l", [128, 2, 8, NT // 64, 3], F32)
            P.dma("sp", lambda e_: e_.dma_start(out=csc_l[:].rearrange("p d h c x -> p (d h c x)"), in_=self.csc_in), writes=["csc0"])
            with contextlib.ExitStack() as lay2:
                S_l = self.sb(lay2, "S_l", [128, 16, 128], F32)
                self.even_combine(S_l)
                self.even_scan(L, e, scr_l, NT, 0, csc_l, S_l, "S_l", True, self.cur_x, self.xres)
                self.cur_x = self.xres
                P.barrier()

    def even_local(self, L, e, src, ntok, j, scr, csc, csl, full):
        nc, P = self.nc, self.P
        TT = 256
        NB = TT // 128
        W_IN = self.w_in_even
        cscn, csln = f"csc{j}", f"csl{j}"
        with contextlib.ExitStack() as ph:
            sb = lambda n, shp, d: self.sb(ph, n, shp, d)
            xt = sb("e_x", [128, KC, TT], F32)
            hn = sb("e_hn", [128, KC, TT], BF16)
            sq = sb("e_sq", [128, KC, TT], BF16)
            tmp = [sb(f"e_tmp{i}", [128, TT], F32) for i in range(2)]
            rs = sb("e_rs", [128, TT], F32)
            wb = [sb(f"e_w{i}", [128, KC, 512], BF16) for i in range(2)]
            lbt = sb("e_lb", [128, 2, 1024], F32)
            oml = sb("e_oml", [128, 2, 1024], F32)
            gt = [sb(f"e_g{b}", [128, 1024], F32) for b in range(NB)]
            kk = [sb(f"e_kk{b}", [128, 1024], F32) for b in range(NB)]
            sg = [sb(f"e_sg{i}", [128, 512], F32) for i in range(2)]
            eq = sb("e_eq", [128, 2, 8, TT], F32)
            kpt = sb("e_kpt", [128, 2, 8, TT], BF16)
            qpt = sb("e_qpt", [128, 2, 8, TT], BF16)
            kp = [sb(f"e_kp{i}", [128, 1024], BF16) for i in range(2)]
            vt = [sb(f"e_v{i}", [128, 1024], BF16) for i in range(2)]
            sgt = sb("e_sgt", [128, 8, TT], BF16)
            ut = sb("e_ut", [128, 8, TT], BF16)
            vn = [sb(f"e_vn{b}", [128, 1024], BF16) for b in range(NB)]
            ybt = sgt
            md = sb("e_md", [128, 6, 128], F32)
            wst = sb("e_wst", [128, 8, 128], BF16)
            brow = sb("e_brow", [128, 8, 128], F32)
            grow = sb("e_grow", [128, 1024], F32)
            st4 = sb("e_st4", [128, 8, 4], F32)
            exs = [sb(f"e_ex{i}", [128, 512], F32) for i in range(2)]
            P.dma("sp", lambda e_: e_.dma_start(out=md[:], in_=self.cmat[C_MDF:C_MDF + 6].rearrange("c p t -> p c t")), writes=["e_md"])
            P.dma("pool", lambda e_: e_.dma_start(out=wst[:], in_=self.wsT.rearrange("g s t -> s g t")), writes=["e_wst"])
            P.dma("sp", lambda e_: e_.dma_start(out=brow[:].rearrange("p g t -> p (g t)"),
                                                in_=self.b_sp.rearrange("g t -> (g t)").partition_broadcast(128)), writes=["e_brow"])
            P.dma("sp", lambda e_: e_.dma_start(out=grow[:], in_=self.g_spv.partition_broadcast(128)), writes=["e_grow"])
            if e == 0:
                P.op("dve", lambda e_: e_.memset(lbt[:], 0.0), writes=["e_lb"])
                P.op("dve", lambda e_: e_.memset(oml[:], 1.0), writes=["e_oml"])
            else:
                for d in range(2):
                    P.dma("sp", lambda e_, d=d: e_.dma_start(out=lbt[:, d, :], in_=self.lb_logits[d, 0].partition_broadcast(128)), writes=["e_lb"])
                    P.dma("sp", lambda e_, d=d: e_.dma_start(out=oml[:, d, :], in_=self.lb_logits[d, 1].partition_broadcast(128)), writes=["e_oml"])
                P.op("dve", lambda e_: e_.tensor_tensor(out=oml[:], in0=oml[:], in1=lbt[:], op=ALU.subtract), reads=["e_oml", "e_lb"], writes=["e_oml"])
                P.op("act", lambda e_: e_.activation(out=lbt[:], in_=oml[:], func=AF.Sigmoid), reads=["e_oml"], writes=["e_lb"])
                P.op("dve", lambda e_: e_.tensor_scalar(out=oml[:], in0=lbt[:], scalar1=-1.0, scalar2=1.0, op0=ALU.mult, op1=ALU.add),
                     reads=["e_lb"], writes=["e_oml"])
            A = self.A1[:, L, :, j]
            B = self.mods[:, L, 0:16, j]
            nw = [0]

            if not hasattr(self, "wbf_in"):
                self.wbf_in = self.nc.dram_tensor("wbf_in", [14, 128, KC, 512], BF16).ap()
                for blk in range(14):
                    P.dma("pool", lambda e_, blk=blk: e_.dma_start(out=self.wbf_in[blk], in_=W_IN[:, blk * 512:(blk + 1) * 512].rearrange("(kc p) m -> p kc m", p=128)),
                          writes=[f"wbf_in{blk}"])

            def load_w(c0):
                slot = nw[0] % 2
                nw[0] += 1
                w = wb[slot]
                blk = c0 // 512
                P.dma("sp", lambda e_, w=w, blk=blk: e_.dma_start(out=w[:], in_=self.wbf_in[blk]), reads=[f"wbf_in{blk}"], writes=[f"e_w{slot}"])
                return w, f"e_w{slot}"

            def tok_major(w, wn, b):
                pt, pn = self.next_ps()
                for kc in range(KC):
                    P.op("pe", lambda e_, kc=kc, pt=pt: e_.matmul(pt[:, :], lhsT=hn[:, kc, b * 128:(b + 1) * 128], rhs=w[:, kc, :],
                                                                  start=(kc == 0), stop=(kc == KC - 1)), reads=["e_hn", wn], writes=[pn])
                return pt, pn

            def feat_major(w, wn, mc):
                pt, pn = self.next_ps()
                for kc in range(KC):
                    P.op("pe", lambda e_, kc=kc, pt=pt: e_.matmul(pt[:, :TT], lhsT=w[:, kc, mc * 128:(mc + 1) * 128], rhs=hn[:, kc, :],
                                                                  start=(kc == 0), stop=(kc == KC - 1)), reads=["e_hn", wn], writes=[pn])
                return pt, pn

            nk = [0]
            for t0 in range(0, ntok, TT):
                P.dma("sp", lambda e_, t0=t0: e_.dma_start(out=xt[:], in_=src[:, t0:t0 + TT].rearrange("(kc p) t -> p kc t", p=128)),
                      reads=["dram_x"], writes=["e_x"])
                self.norm_mod(xt, "e_x", TT, A, B, hn, "e_hn", sq, "e_sq", tmp, "e_tmp", rs, "e_rs")
                if DEBUG_CUT == 11:
                    P.barrier()
                    return
                for d in range(2):
                    for cbk in range(2):
                        c0 = 1024 * (1 + d) + cbk * 512
                        w, wn = load_w(c0)
                        cs = slice(cbk * 512, (cbk + 1) * 512)
                        for b in range(NB):
                            pt, pn = tok_major(w, wn, b)
                            s_ = sg[b % 2]
                            sn = f"e_sg{b % 2}"
                            P.op("act", lambda e_, s_=s_, pt=pt: e_.activation(out=s_[:], in_=pt[:], func=AF.Sigmoid), reads=[pn], writes=[sn])
                            P.op("dve", lambda e_, s_=s_, d=d, cs=cs: e_.tensor_tensor(out=s_[:], in0=s_[:], in1=oml[:, d, cs], op=ALU.mult),
                                 reads=[sn, "e_oml"], writes=[sn])
                            P.op("pool", lambda e_, s_=s_, d=d, cs=cs, b=b: e_.tensor_tensor(out=kk[b][:, cs], in0=oml[:, d, cs], in1=s_[:], op=ALU.subtract),
                                 reads=[sn, "e_oml"], writes=[f"e_kk{b}"])
                            P.op("dve", lambda e_, s_=s_, d=d, cs=cs: e_.tensor_tensor(out=s_[:], in0=s_[:], in1=lbt[:, d, cs], op=ALU.add),
                                 reads=[sn, "e_lb"], writes=[sn])
                            P.op("dve", lambda e_, s_=s_: e_.tensor_single_scalar(out=s_[:], in_=s_[:], scalar=1e-30, op=ALU.max), reads=[sn], writes=[sn])
                            P.op("act", lambda e_, s_=s_, b=b, cs=cs: e_.activation(out=gt[b][:, cs], in_=s_[:], func=AF.Ln), reads=[sn], writes=[f"e_g{b}"])
                    if DEBUG_CUT == 12:
                        P.barrier()
                        return
                    for b in range(NB):
                        g_, gn = gt[b], f"e_g{b}"
                        tb0 = t0 + b * 128
                        kpb = kp[nk[0] % 2]
                        kpn = f"e_kp{nk[0] % 2}"
                        nk[0] += 1
                        for hf in range(2):
                            cs = slice(hf * 512, (hf + 1) * 512)
                            pt, pn = self.next_ps()
                            P.op("pe", lambda e_, pt=pt, d=d, g_=g_, cs=cs: e_.matmul(pt[:, :], lhsT=md[:, 2 * d, :], rhs=g_[:, cs], start=True, stop=True),
                                 reads=["e_md", gn], writes=[pn])
                            ex = exs[hf]
                            P.op("act", lambda e_, ex=ex, pt=pt: e_.activation(out=ex[:], in_=pt[:], func=AF.Exp), reads=[pn], writes=[f"e_ex{hf}"])
                            P.op("dve", lambda e_, ex=ex, b=b, cs=cs, kpb=kpb: e_.tensor_tensor(out=kpb[:, cs], in0=kk[b][:, cs], in1=ex[:], op=ALU.mult),
                                 reads=[f"e_ex{hf}", f"e_kk{b}"], writes=[kpn])
                        if DEBUG_CUT == 121:
                            P.barrier()
                            return
                        P.dma("sp", lambda e_, kpb=kpb, d=d, tb0=tb0: e_.dma_start(out=scr["Kp"][d][tb0:tb0 + 128, :], in_=kpb[:]),
                              reads=[kpn], writes=["dram_kp"])
                        if DEBUG_CUT == 122:
                            P.barrier()
                            return
                        for h in range(8):
                            P.op("pe", lambda e_, h=h, kpb=kpb: e_.transpose(out=self.pst[:, h * 128:(h + 1) * 128], in_=kpb[:, h * 128:(h + 1) * 128],
                                                                             identity=self.ident[:]), reads=[kpn, "ident"], writes=["pst"])
                        P.op("dve", lambda e_, d=d, b=b: e_.tensor_copy(out=kpt[:, d, :, b * 128:(b + 1) * 128],
                                                                      in_=self.pst[:, :].rearrange("p (h t) -> p h t", h=8)), reads=["pst"], writes=["e_kpt"])
                        if DEBUG_CUT == 13:
                            P.barrier()
                            return
                        for hh in range(2):
                            pt, pn = self.next_ps()
                            for h4 in range(4):
                                h = hh * 4 + h4
                                P.op("pe", lambda e_, pt=pt, h=h, h4=h4, d=d, g_=g_: e_.matmul(pt[:, h4 * 128:(h4 + 1) * 128], lhsT=g_[:, h * 128:(h + 1) * 128],
                                                                                            rhs=md[:, 2 * d + 1, :], start=True, stop=True),
                                     reads=["e_md", gn], writes=[pn])
                            for h4 in range(4):
                                P.op("act", lambda e_, pt=pt, hh=hh, h4=h4, d=d, b=b: e_.activation(
                                    out=eq[:, d, hh * 4 + h4, b * 128:(b + 1) * 128], in_=pt[:, h4 * 128:(h4 + 1) * 128], func=AF.Exp),
                                    reads=[pn], writes=["e_eq"])
                        if DEBUG_CUT == 14:
                            P.barrier()
                            return
                        pt, pn = self.next_ps()
                        for h in range(8):
                            P.op("pe", lambda e_, pt=pt, h=h, d=d, g_=g_: e_.matmul(pt[:, h * 6:(h + 1) * 6], lhsT=g_[:, h * 128:(h + 1) * 128],
                                                                                 rhs=md[:, 4 + d, 0:6], start=True, stop=True), reads=["e_md", gn], writes=[pn])
                        ci0 = tb0 // 64
                        for h in range(8):
                            P.op("dve", lambda e_, pt=pt, d=d, ci0=ci0, h=h: e_.tensor_copy(
                                out=csl[:, d, h, ci0:ci0 + 2, :], in_=pt[:, h * 6:(h + 1) * 6].rearrange("p (c x) -> p c x", c=2)), reads=[pn], writes=[csln])
                        for h in range(8):
                            P.op("act", lambda e_, d=d, ci0=ci0, h=h: e_.activation(
                                out=csc[:, d, h, ci0:ci0 + 2, :], in_=csl[:, d, h, ci0:ci0 + 2, :], func=AF.Exp), reads=[csln], writes=[cscn])
                    P.dma("sp", lambda e_, d=d, t0=t0: e_.dma_start(out=scr["KpT"][d][:, t0:t0 + TT].rearrange("(h p) t -> p h t", p=128), in_=kpt[:, d]),
                          reads=["e_kpt"], writes=["dram_kpt"])
                if DEBUG_CUT == 15:
                    P.barrier()
                    return
                for cbk in range(2):
                    w, wn = load_w(cbk * 512)
                    for mc in range(4):
                        h = cbk * 4 + mc
                        pt, pn = feat_major(w, wn, mc)
                        for d in range(2):
                            P.op("dve", lambda e_, pt=pt, d=d, h=h: e_.tensor_tensor(out=qpt[:, d, h, :], in0=pt[:, :TT], in1=eq[:, d, h, :], op=ALU.mult),
                                 reads=[pn, "e_eq"], writes=["e_qpt"])
                for d in range(2):
                    P.dma("sp", lambda e_, d=d, t0=t0: e_.dma_start(out=scr["QpT"][d][:, t0:t0 + TT].rearrange("(h p) t -> p h t", p=128), in_=qpt[:, d]),
                          reads=["e_qpt"], writes=["dram_qpt"])
                for cbk in range(2):
                    w, wn = load_w(3072 + cbk * 512)
                    for b in range(NB):
                        pt, pn = tok_major(w, wn, b)
                        P.op("act", lambda e_, pt=pt, b=b, cbk=cbk: e_.copy(out=vt[b][:, cbk * 512:(cbk + 1) * 512], in_=pt[:]), reads=[pn], writes=[f"e_v{b}"])
                for b in range(NB):
                    P.dma("sp", lambda e_, b=b, t0=t0: e_.dma_start(out=scr["V"][t0 + b * 128:t0 + (b + 1) * 128, :], in_=vt[b][:]),
                          reads=[f"e_v{b}"], writes=["dram_v"])
                if DEBUG_CUT == 16:
                    P.barrier()
                    return
                if not full:
                    continue
                for cbk in range(2):
                    w, wn = load_w(4096 + cbk * 512)
                    for mc in range(4):
                        pt, pn = feat_major(w, wn, mc)
                        P.op("act", lambda e_, pt=pt, h=cbk * 4 + mc: e_.activation(out=sgt[:, h, :], in_=pt[:, :TT], func=AF.Silu), reads=[pn], writes=["e_sgt"])
                P.dma("sp", lambda e_, t0=t0: e_.dma_start(out=scr["SGT"][:, t0:t0 + TT].rearrange("(h p) t -> p h t", p=128), in_=sgt[:]),
                      reads=["e_sgt"], writes=["dram_sgt"])
                for cbk in range(2):
                    w, wn = load_w(5120 + cbk * 512)
                    for mc in range(4):
                        pt, pn = feat_major(w, wn, mc)
                        P.op("act", lambda e_, pt=pt, h=cbk * 4 + mc: e_.activation(out=ut[:, h, :], in_=pt[:, :TT], func=AF.Gelu), reads=[pn], writes=["e_ut"])
                if DEBUG_CUT == 17:
                    P.barrier()
                    return
                for cbk in range(2):
                    w, wn = load_w(6144 + cbk * 512)
                    for b in range(NB):
                        pt, pn = tok_major(w, wn, b)
                        s_ = sg[b % 2]
                        sn = f"e_sg{b % 2}"
                        x2 = exs[b % 2]
                        xn2 = f"e_ex{b % 2}"
                        P.op("act", lambda e_, s_=s_, pt=pt: e_.activation(out=s_[:], in_=pt[:], func=AF.Gelu), reads=[pn], writes=[sn])
                        P.op("act", lambda e_, s_=s_, x2=x2: e_.activation(out=x2[:], in_=s_[:], func=AF.Square), reads=[sn], writes=[xn2])
                        g0 = cbk * 4
                        P.op("dve", lambda e_, s_=s_, g0=g0: e_.tensor_reduce(out=st4[:, g0:g0 + 4, 0], in_=s_[:].rearrange("p (g c) -> p g c", g=4),
                                                                             axis=mybir.AxisListType.X, op=ALU.add), reads=[sn], writes=["e_st4"])
                        P.op("dve", lambda e_, x2=x2, g0=g0: e_.tensor_reduce(out=st4[:, g0:g0 + 4, 1], in_=x2[:].rearrange("p (g c) -> p g c", g=4),
                                                                             axis=mybir.AxisListType.X, op=ALU.add), reads=[xn2], writes=["e_st4"])
                        P.op("dve", lambda e_, g0=g0: e_.tensor_single_scalar(out=st4[:, g0:g0 + 4, 0], in_=st4[:, g0:g0 + 4, 0], scalar=1.0 / 128, op=ALU.mult),
                             reads=["e_st4"], writes=["e_st4"])
                        P.op("dve", lambda e_, g0=g0: e_.tensor_tensor(out=st4[:, g0:g0 + 4, 2], in0=st4[:, g0:g0 + 4, 0], in1=st4[:, g0:g0 + 4, 0], op=ALU.mult),
                             reads=["e_st4"], writes=["e_st4"])
                        P.op("dve", lambda e_, g0=g0: e_.scalar_tensor_tensor(out=st4[:, g0:g0 + 4, 1], in0=st4[:, g0:g0 + 4, 1], scalar=1.0 / 128,
                                                                            in1=st4[:, g0:g0 + 4, 2], op0=ALU.mult, op1=ALU.subtract),
                             reads=["e_st4"], writes=["e_st4"])
                        P.op("act", lambda e_, g0=g0: e_.activation(out=st4[:, g0:g0 + 4, 1], in_=st4[:, g0:g0 + 4, 1], func=AF.Sqrt, bias=self.epsb[:, 0:1], scale=1.0),
                             reads=["e_st4", "epsb"], writes=["e_st4"])
                        P.op("dve", lambda e_, g0=g0: e_.reciprocal(out=st4[:, g0:g0 + 4, 1], in_=st4[:, g0:g0 + 4, 1]), reads=["e_st4"], writes=["e_st4"])
                        for g4 in range(4):
                            g = g0 + g4
                            P.op("dve", lambda e_, s_=s_, g=g, g4=g4: e_.tensor_scalar(
                                out=s_[:, g4 * 128:(g4 + 1) * 128], in0=s_[:, g4 * 128:(g4 + 1) * 128], scalar1=st4[:, g, 0:1], scalar2=st4[:, g, 1:2],
                                op0=ALU.subtract, op1=ALU.mult), reads=[sn, "e_st4"], writes=[sn])
                        P.op("pool", lambda e_, s_=s_, b=b, cbk=cbk: e_.tensor_tensor(out=vn[b][:, cbk * 512:(cbk + 1) * 512], in0=s_[:],
                                                                                     in1=grow[:, cbk * 512:(cbk + 1) * 512], op=ALU.mult),
                             reads=[sn, "e_grow"], writes=[f"e_vn{b}"])
                if DEBUG_CUT == 18:
                    P.barrier()
                    return
                for b in range(NB):
                    for hh in range(2):
                        pt, pn = self.next_ps()
                        for g4 in range(4):
                            g = hh * 4 + g4
                            P.op("pe", lambda e_, pt=pt, g=g, g4=g4, b=b: e_.matmul(pt[:, g4 * 128:(g4 + 1) * 128], lhsT=vn[b][:, g * 128:(g + 1) * 128],
                                                                                   rhs=wst[:, g, :], start=True, stop=True), reads=[f"e_vn{b}", "e_wst"], writes=[pn])
                        x2 = exs[hh]
                        xn2 = f"e_ex{hh}"
                        P.op("dve", lambda e_, pt=pt, hh=hh, x2=x2: e_.tensor_tensor(out=x2[:], in0=pt[:], in1=brow[:, hh * 4:(hh + 1) * 4, :].rearrange("p g t -> p (g t)"),
                                                                                   op=ALU.add), reads=[pn, "e_brow"], writes=[xn2])
                        P.op("pool", lambda e_, hh=hh, x2=x2, b=b: e_.tensor_tensor(out=ybt[:, hh * 4:(hh + 1) * 4, b * 128:(b + 1) * 128],
                                                                                   in0=x2[:].rearrange("p (g t) -> p g t", g=4),
                                                                                   in1=ut[:, hh * 4:(hh + 1) * 4, b * 128:(b + 1) * 128], op=ALU.mult),
                             reads=[xn2, "e_ut"], writes=["e_sgt"])
                P.dma("sp", lambda e_, t0=t0: e_.dma_start(out=scr["YBT"][:, t0:t0 + TT].rearrange("(h p) t -> p h t", p=128), in_=ybt[:]),
                      reads=["e_sgt"], writes=["dram_ybt"])
            P.barrier()

    def even_scan(self, L, e, scr, ntok, j, csc, S, Sname, compute_out, src, dst):
        nc, P = self.nc, self.P
        nblk = ntok // 128
        cscn = f"csc{j}"
        with contextlib.ExitStack() as ph:
            sb = lambda n, shp, d: self.sb(ph, n, shp, d)
            Sb = [sb(f"s_Sb{i}", [128, 8, 128], BF16) for i in range(2)]
            tmpU = sb("s_tu", [128, 8, 128], F32)
            qt = [sb(f"s_q{i}", [128, 8, 128], BF16) for i in range(2)]
            kt = [sb(f"s_k{i}", [128, 8, 128], BF16) for i in range(2)]
            kpb = [sb(f"s_kp{i}", [64, 2, 1024], BF16) for i in range(2)]
            vb = [sb(f"s_v{i}", [64, 2, 1024], BF16) for i in range(2)]
            sct = [sb(f"s_sc{i}", [64, 8, 64], BF16) for i in range(2)]
            mask = sb("s_mask", [64, 2, 8, 64], F32)
            masku = sb("s_masku", [64, 2, 8, 64], mybir.dt.uint32)
            sc32 = [sb(f"s_sc32_{i}", [64, 8, 64], F32) for i in range(2)]
            m64 = sb("s_m64", [64, 2, 128], F32)
            oT = sb("s_oT", [128, 8, ntok], F32) if compute_out else None
            if compute_out:
                for d in range(2):
                    P.dma("sp", lambda e_, d=d: e_.dma_start(out=m64[:, d, :], in_=self.cmat[C_MASKF + d][0:64, :]), writes=["s_m64"])
                    for h in range(8):
                        P.op("dve", lambda e_, d=d, h=h: e_.tensor_copy(out=mask[:, d, h, :], in_=m64[:, d, 0:64]), reads=["s_m64"], writes=["s_mask"])
                    P.op("dve", lambda e_, d=d: e_.tensor_single_scalar(out=masku[:, d], in_=mask[:, d], scalar=0.5, op=ALU.is_gt), reads=["s_mask"], writes=["s_masku"])
                    P.op("dve", lambda e_, d=d: e_.memset(sc32[d][:], 0.0), writes=[f"s_sc32_{d}"])
            n = 0
            nchunk = 0
            for d in range(2):
                blocks = list(range(nblk)) if d == 0 else list(reversed(range(nblk)))
                for blk in blocks:
                    slot = n % 2
                    n += 1
                    t0 = blk * 128
                    if compute_out:
                        P.dma("sp", lambda e_, slot=slot, d=d, t0=t0: e_.dma_start(out=qt[slot][:], in_=scr["QpT"][d][:, t0:t0 + 128].rearrange("(h p) t -> p h t", p=128)),
                              reads=["dram_qpt"], writes=[f"s_q{slot}"])
                        P.dma("sp", lambda e_, slot=slot, d=d, t0=t0: e_.dma_start(out=kt[slot][:], in_=scr["KpT"][d][:, t0:t0 + 128].rearrange("(h p) t -> p h t", p=128)),
                              reads=["dram_kpt"], writes=[f"s_k{slot}"])
                    P.dma("sp", lambda e_, slot=slot, d=d, t0=t0: e_.dma_start(out=kpb[slot][:], in_=scr["Kp"][d][t0:t0 + 128, :].rearrange("(c p) k -> p c k", p=64)),
                          reads=["dram_kp"], writes=[f"s_kp{slot}"])
                    P.dma("sp", lambda e_, slot=slot, t0=t0: e_.dma_start(out=vb[slot][:], in_=scr["V"][t0:t0 + 128, :].rearrange("(c p) k -> p c k", p=64)),
                          reads=["dram_v"], writes=[f"s_v{slot}"])
                    for c in ((0, 1) if d == 0 else (1, 0)):
                        ci = blk * 2 + c
                        cslc = slice(c * 64, (c + 1) * 64)
                        if compute_out:
                            sbt = Sb[nchunk % 2]
                            sbn = f"s_Sb{nchunk % 2}"
                            sc_ = sct[nchunk % 2]
                            scn = f"s_sc{nchunk % 2}"
                            nchunk += 1
                            for h in range(8):
                                P.op("act", lambda e_, h=h, d=d, ci=ci, sbt=sbt: e_.activation(out=sbt[:, h, :], in_=S[:, d * 8 + h, :], func=AF.Identity,
                                                                                            scale=csc[:, d, h, ci, 0:1]),
                                     reads=[Sname + f"_{d * 8 + h}", cscn], writes=[sbn])
                            psc, pscn = self.next_ps()
                            for h in range(8):
                                P.op("pe", lambda e_, h=h, psc=psc, slot=slot, cslc=cslc: e_.matmul(psc[0:64, h * 64:(h + 1) * 64], lhsT=kt[slot][:, h, cslc],
                                                                                              rhs=qt[slot][:, h, cslc], start=True, stop=True),
                                     reads=[f"s_k{slot}", f"s_q{slot}"], writes=[pscn])
                            P.op("dve", lambda e_, psc=psc, d=d: e_.copy_predicated(sc32[d][:], masku[:, d], psc[0:64, :].rearrange("p (h t) -> p h t", h=8)),
                                 reads=[pscn, "s_masku", f"s_sc32_{d}"], writes=[f"s_sc32_{d}"])
                            P.op("pool", lambda e_, sc_=sc_, d=d: e_.tensor_copy(out=sc_[:], in_=sc32[d][:]), reads=[f"s_sc32_{d}"], writes=[scn])
                            po, pon = self.next_ps()
                            for h in range(8):
                                P.op("pe", lambda e_, h=h, po=po, slot=slot, c=c, sc_=sc_: e_.matmul(po[:, h * 64:(h + 1) * 64], lhsT=vb[slot][:, c, h * 128:(h + 1) * 128],
                                                                                               rhs=sc_[:, h, :], start=True, stop=False),
                                     reads=[f"s_v{slot}", scn], writes=[pon])
                                P.op("pe", lambda e_, h=h, po=po, slot=slot, cslc=cslc, sbt=sbt: e_.matmul(po[:, h * 64:(h + 1) * 64], lhsT=sbt[:, h, :],
                                                                                                     rhs=qt[slot][:, h, cslc], start=False, stop=True),
                                     reads=[sbn, f"s_q{slot}"], writes=[pon])
                            osl = oT[:, :, ci * 64:(ci + 1) * 64]
                            if d == 0:
                                P.op("dve", lambda e_, po=po, osl=osl: e_.tensor_copy(out=osl, in_=po[:, :].rearrange("p (h t) -> p h t", h=8)), reads=[pon], writes=["s_oT"])
                            else:
                                P.op("dve", lambda e_, po=po, osl=osl: e_.tensor_tensor(out=osl, in0=po[:, :].rearrange("p (h t) -> p h t", h=8), in1=osl, op=ALU.add),
                                     reads=[pon, "s_oT"], writes=["s_oT"])
                        for hh in range(2):
                            pk, pkn = self.next_ps()
                            for h4 in range(4):
                                h = hh * 4 + h4
                                P.op("pe", lambda e_, h=h, h4=h4, pk=pk, slot=slot, c=c: e_.matmul(pk[:, h4 * 128:(h4 + 1) * 128], lhsT=kpb[slot][:, c, h * 128:(h + 1) * 128],
                                                                                             rhs=vb[slot][:, c, h * 128:(h + 1) * 128], start=True, stop=True),
                                     reads=[f"s_kp{slot}", f"s_v{slot}"], writes=[pkn])
                            for h4 in range(4):
                                h = hh * 4 + h4
                                idx = d * 8 + h
                                P.op("act", lambda e_, h=h, h4=h4, pk=pk, d=d, ci=ci: e_.activation(out=tmpU[:, h, :], in_=pk[:, h4 * 128:(h4 + 1) * 128], func=AF.Identity,
                                                                                                 scale=csc[:, d, h, ci, 1:2]), reads=[pkn, cscn], writes=[f"s_tu{h}"])
                                P.op("dve", lambda e_, h=h, idx=idx, d=d, ci=ci: e_.scalar_tensor_tensor(out=S[:, idx, :], in0=S[:, idx, :], scalar=csc[:, d, h, ci, 2:3],
                                                                                                      in1=tmpU[:, h, :], op0=ALU.mult, op1=ALU.add),
                                     reads=[f"s_tu{h}", cscn, Sname + f"_{idx}"], writes=[Sname + f"_{idx}"])
            if compute_out:
                self.even_readout(L, e, scr, ntok, j, oT, src, dst)
            P.barrier()

    def even_readout(self, L, e, scr, ntok, j, oT, src, dst):
        nc, P = self.nc, self.P
        TT = 256
        with contextlib.ExitStack() as ph:
            sb = lambda n, shp, d: self.sb(ph, n, shp, d)
            xt = sb("r_x", [128, KC, TT], F32)
            yt = sb("r_y", [128, KC, TT], BF16)
            sg = sb("r_sg", [128, 8, TT], BF16)
            wo = [sb(f"r_w{i}", [128, KC, 256], BF16) for i in range(2)]
            sqh = [sb(f"r_sq{i}", [128, TT], BF16) for i in range(2)]
            rs = [sb(f"r_rs{i}", [128, TT], F32) for i in range(2)]
            tt_ = [sb(f"r_t{i}", [128, TT], F32) for i in range(2)]
            ghg = sb("r_ghg", [128, 2, 8], F32)
            P.dma("sp", lambda e_: e_.dma_start(out=ghg[:], in_=self.g_hgrn), writes=["r_ghg"])
            G1 = self.mods[:, L, 32:48, j]
            nw = 0
            if not hasattr(self, "wbf_out"):
                self.wbf_out = self.nc.dram_tensor("wbf_out", [8, 128, KC, 256], BF16).ap()
                for mb in range(8):
                    P.dma("pool", lambda e_, mb=mb: e_.dma_start(out=self.wbf_out[mb], in_=self.w_out_even[:, mb * 256:(mb + 1) * 256].rearrange("(kc p) m -> p kc m", p=128)),
                          writes=[f"wbf_out{mb}"])
            for t0 in range(0, ntok, TT):
                P.dma("sp", lambda e_, t0=t0: e_.dma_start(out=xt[:], in_=src[:, t0:t0 + TT].rearrange("(kc p) t -> p kc t", p=128)), reads=["dram_x"], writes=["r_x"])
                P.dma("sp", lambda e_, t0=t0: e_.dma_start(out=sg[:], in_=scr["SGT"][:, t0:t0 + TT].rearrange("(h p) t -> p h t", p=128)), reads=["dram_sgt"], writes=["r_sg"])
                P.dma("sp", lambda e_, t0=t0: e_.dma_start(out=yt[:, 8:16, :], in_=scr["YBT"][:, t0:t0 + TT].rearrange("(h p) t -> p h t", p=128)), reads=["dram_ybt"], writes=["r_yb"])
                for h in range(8):
                    i2 = h % 2
                    P.op("act", lambda e_, h=h, i2=i2, t0=t0: e_.activation(out=sqh[i2][:], in_=oT[:, h, t0:t0 + TT], func=AF.Square), reads=["s_oT"], writes=[f"r_sq{i2}"])
                    pt, pn = self.next_ps()
                    P.op("pe", lambda e_, pt=pt, i2=i2: e_.matmul(pt[:, :TT], lhsT=self.ones[:], rhs=sqh[i2][:], start=True, stop=True), reads=["ones", f"r_sq{i2}"], writes=[pn])
                    P.op("act", lambda e_, pt=pt, i2=i2: e_.activation(out=rs[i2][:], in_=pt[:, :TT], func=AF.Sqrt, bias=self.epsb[:, 0:1], scale=1.0 / 128),
                         reads=[pn, "epsb"], writes=[f"r_rs{i2}"])
                    P.op("dve", lambda e_, i2=i2: e_.reciprocal(out=rs[i2][:], in_=rs[i2][:]), reads=[f"r_rs{i2}"], writes=[f"r_rs{i2}"])
                    P.op("dve", lambda e_, h=h, i2=i2, t0=t0: e_.scalar_tensor_tensor(out=tt_[i2][:], in0=oT[:, h, t0:t0 + TT], scalar=ghg[:, e, h:h + 1], in1=rs[i2][:],
                                                                                    op0=ALU.mult, op1=ALU.mult), reads=["s_oT", "r_ghg", f"r_rs{i2}"], writes=[f"r_t{i2}"])
                    P.op("pool", lambda e_, h=h, i2=i2: e_.tensor_tensor(out=yt[:, h, :], in0=tt_[i2][:], in1=sg[:, h, :], op=ALU.mult),
                         reads=[f"r_t{i2}", "r_sg"], writes=["r_ya"])
                for mb in range(8):
                    slot = nw % 2
                    nw += 1
                    w = wo[slot]
                    P.dma("sp", lambda e_, w=w, mb=mb: e_.dma_start(out=w[:], in_=self.wbf_out[mb]), reads=[f"wbf_out{mb}"], writes=[f"r_w{slot}"])
                    for mc in range(2):
                        m = mb * 2 + mc
                        pt, pn = self.next_ps()
                        for kc in range(KC):
                            P.op("pe", lambda e_, pt=pt, w=w, kc=kc, mc=mc: e_.matmul(pt[:, :TT], lhsT=w[:, kc, mc * 128:(mc + 1) * 128], rhs=yt[:, kc, :],
                                                                                    start=(kc == 0), stop=(kc == KC - 1)), reads=[f"r_w{slot}", "r_ya", "r_yb"], writes=[pn])
                        P.op("dve", lambda e_, pt=pt, m=m: e_.scalar_tensor_tensor(out=xt[:, m, :], in0=pt[:, :TT], scalar=G1[:, m:m + 1], in1=xt[:, m, :],
                                                                                 op0=ALU.mult, op1=ALU.add), reads=[pn, "mods", "r_x"], writes=["r_x"])
                P.dma("sp", lambda e_, t0=t0: e_.dma_start(out=dst[:, t0:t0 + TT].rearrange("(kc p) t -> p kc t", p=128), in_=xt[:]), reads=["r_x"], writes=["dram_x"])

    def even_pack(self, csl, S_l):
        P = self.P
        with contextlib.ExitStack() as ph:
            sb = lambda n, shp, d: self.sb(ph, n, shp, d)
            dt_ = sb("x_dt", [128, 16], F32)
            pk_ = sb("x_pk", [128, 16, 132], F32)
            P.op("dve", lambda e_: e_.tensor_reduce(out=dt_[:], in_=csl[:, :, :, :, 2].rearrange("p d h c -> p (d h) c"), axis=mybir.AxisListType.X, op=ALU.add),
                 reads=["csl0"], writes=["x_dt"])
            P.op("act", lambda e_: e_.activation(out=dt_[:], in_=dt_[:], func=AF.Exp), reads=["x_dt"], writes=["x_dt"])
            allS = [f"S_l_{i}" for i in range(16)]
            P.op("dve", lambda e_: e_.memset(pk_[:], 0.0), writes=["x_pk"])
            P.op("dve", lambda e_: e_.tensor_copy(out=pk_[:, :, 0:128], in_=S_l[:]), reads=allS, writes=["x_pk"])
            P.op("dve", lambda e_: e_.tensor_copy(out=pk_[:, :, 128], in_=dt_[:]), reads=["x_dt"], writes=["x_pk"])
            P.dma("sp", lambda e_: e_.dma_start(out=self.pack_out.rearrange("(i p) c -> p i c", p=128), in_=pk_[:]), reads=["x_pk"], writes=["dram_xs"])
            P.barrier()

    def even_combine(self, S_l):
        P = self.P
        xg = self.pack_in
        with contextlib.ExitStack() as ph:
            sb = lambda n, shp, d: self.sb(ph, n, shp, d)
            cf = sb("x_cf", [128, 4, NCORES], F32)
            oma = sb("x_oma", [128, 4, NCORES], F32)
            G = [sb(f"x_G{i}", [128, 8, 132], F32) for i in range(2)]
            mm = sb("x_m", [128, 8], F32)
            tx = [sb(f"x_t{i}", [128, 128], F32) for i in range(2)]
            P.dma("sp", lambda e_: e_.dma_start(out=cf[:], in_=self.coef), writes=["x_cf"])
            P.op("dve", lambda e_: e_.tensor_scalar(out=oma[:], in0=cf[:], scalar1=-1.0, scalar2=1.0, op0=ALU.mult, op1=ALU.add), reads=["x_cf"], writes=["x_oma"])
            P.dma("sp", lambda e_: e_.dma_start(out=S_l[:], in_=self.sc_in), writes=[f"S_l_{i}" for i in range(16)])
            n = 0
            for dsel in range(2):
                order = list(range(NCORES)) if dsel == 0 else list(reversed(range(NCORES)))
                for jc in order:
                    g_ = G[n % 2]
                    gn = f"x_G{n % 2}"
                    n += 1
                    r0 = jc * 2048 + dsel * 1024
                    P.dma("sp", lambda e_, g_=g_, r0=r0: e_.dma_start(out=g_[:], in_=xg[r0:r0 + 1024, :].rearrange("(h p) c -> p h c", p=128)), writes=[gn])
                    P.op("dve", lambda e_, g_=g_, dsel=dsel, jc=jc: e_.tensor_scalar(out=mm[:], in0=g_[:, :, 128], scalar1=cf[:, dsel, jc:jc + 1], scalar2=oma[:, dsel, jc:jc + 1],
                                                                                   op0=ALU.mult, op1=ALU.add), reads=[gn, "x_cf", "x_oma"], writes=["x_m"])
                    for h in range(8):
                        idx = dsel * 8 + h
                        t_ = tx[h % 2]
                        P.op("pool", lambda e_, g_=g_, h=h, t_=t_, dsel=dsel, jc=jc: e_.tensor_single_scalar(out=t_[:], in_=g_[:, h, 0:128], scalar=cf[:, dsel, jc:jc + 1], op=ALU.mult),
                             reads=[gn, "x_cf"], writes=[f"x_t{h % 2}"])
                        P.op("dve", lambda e_, idx=idx, h=h, t_=t_: e_.scalar_tensor_tensor(out=S_l[:, idx, :], in0=S_l[:, idx, :], scalar=mm[:, h:h + 1], in1=t_[:],
                                                                                          op0=ALU.mult, op1=ALU.add), reads=[f"x_t{h % 2}", "x_m", f"S_l_{idx}"], writes=[f"S_l_{idx}"])
            P.barrier()

    def phase_odd(self, L, part):
        ctx_out = (L == 1)
        src_x, src_c = self.cur_x, self.cur_c
        self.odd_segment(L, part, src_x, self.xres, NT, 0, True)
        if ctx_out and part == "b":
            self.odd_segment(L, part, src_c, self.cres, NCX, 1, False)
        if part == "b":
            self.cur_x = self.xres
            if ctx_out:
                self.cur_c = self.cres

    def odd_segment(self, L, part, src, dst, ntok, j, grid):
        nc, P = self.nc, self.P
        o = L // 2
        nblk = ntok // 128
        with contextlib.ExitStack() as lay:
            p_own = self.sb(lay, f"p_own{j}", [128, nblk, D], BF16)
            halo = [self.sb(lay, f"p_halo{i}", [128, 4, D], BF16) for i in range(2)] if (grid and part == "b") else None
            if part == "b" and grid:
                for blk in range(nblk):
                    P.dma("sp", lambda e_, blk=blk: e_.dma_start(out=p_own[:, blk, :], in_=self.p_in[blk * 128:(blk + 1) * 128, :]), writes=[f"p_own_{blk}"])
            else:
                self.odd_stage1(L, o, src, ntok, j, p_own)
            if part == "a":
                P.dma("sp", lambda e_: e_.dma_start(out=self.p_out.rearrange("(b p) c -> p b c", p=128), in_=p_own[:]),
                      reads=[f"p_own_{b}" for b in range(nblk)], writes=["dram_pout"])
                hs = self.halo_out
                P.dma("sp", lambda e_: e_.dma_start(out=hs[0:512, :].rearrange("(b p) c -> p b c", p=128), in_=p_own[:, 0:4, :]),
                      reads=[f"p_own_{b}" for b in range(4)], writes=["dram_hs"])
                P.dma("sp", lambda e_: e_.dma_start(out=hs[512:1024, :].rearrange("(b p) c -> p b c", p=128), in_=p_own[:, 12:16, :]),
                      reads=[f"p_own_{b}" for b in range(12, 16)], writes=["dram_hs"])
                P.barrier()
                return
            if grid:
                self.odd_stage2(halo)
            self.odd_stage3(L, o, src, dst, ntok, j, grid, p_own, halo)

    def odd_stage1(self, L, o, src, ntok, j, p_own):
        P = self.P
        with contextlib.ExitStack() as ph:
            TT = 256
            NB = TT // 128
            xt = self.sb(ph, "p_x", [128, KC, TT], F32)
            hn = self.sb(ph, "p_hn", [128, KC, TT], BF16)
            sq = self.sb(ph, "p_sq", [128, KC, TT], BF16)
            tmp = [self.sb(ph, f"p_tmp{i}", [128, TT], F32) for i in range(2)]
            rs = self.sb(ph, "p_rs", [128, TT], F32)
            wb = [self.sb(ph, f"p_w{i}", [128, KC, 512], BF16) for i in range(2)]
            A = self.A1[:, L, :, j]
            B = self.mods[:, L, 0:16, j]
            nw = 0
            for t0 in range(0, ntok, TT):
                P.dma("sp", lambda e_, t0=t0: e_.dma_start(out=xt[:], in_=src[:, t0:t0 + TT].rearrange("(kc p) t -> p kc t", p=128)),
                      reads=["dram_x"], writes=["p_x"])
                self.norm_mod(xt, "p_x", TT, A, B, hn, "p_hn", sq, "p_sq", tmp, "p_tmp", rs, "p_rs")
                for cb in range(4):
                    slot = nw % 2
                    nw += 1
                    w = wb[slot]
                    srcw = self.w_in_pool[:, cb * 512:(cb + 1) * 512].rearrange("(kc p) m -> p kc m", p=128)
                    P.dma("pool", lambda e_, w=w, srcw=srcw: e_.dma_start(out=w[:], in_=srcw), writes=[f"p_w{slot}"])
                    for b in range(NB):
                        blk = t0 // 128 + b
                        pt, pn = self.next_ps()
                        for kc in range(KC):
                            P.op("pe", lambda e_, kc=kc, pt=pt, b=b, w=w: e_.matmul(pt[:, :], lhsT=hn[:, kc, b * 128:(b + 1) * 128], rhs=w[:, kc, :],
                                                                                  start=(kc == 0), stop=(kc == KC - 1)), reads=["p_hn", f"p_w{slot}"], writes=[pn])
                        P.op("act", lambda e_, pt=pt, blk=blk, cb=cb: e_.copy(out=p_own[:, blk, cb * 512:(cb + 1) * 512], in_=pt[:]),
                             reads=[pn], writes=[f"p_own_{blk}"])
            P.barrier()

    def odd_stage2(self, halo):
        P = self.P
        hg = self.halo_in
        for side in range(2):
            P.dma("sp", lambda e_, side=side: e_.dma_start(out=halo[side][:], in_=hg[side * 512:(side + 1) * 512, :].rearrange("(b p) c -> p b c", p=128)),
                  writes=[f"p_halo{side}"])

    def odd_stage3(self, L, o, src, dst, ntok, j, grid, p_own, halo):
        P = self.P
        nblk = ntok // 128
        with contextlib.ExitStack() as ph:
            deltas = (lambda k: W2D_DELTAS[k]) if grid else (lambda k: W1D_DELTAS)
            keys = [(k, dl) for k in POOL_WINDOWS for dl in deltas(k)]
            wmat = self.sb(ph, "p_wm", [128, len(keys), 128], BF16)
            c0 = C_W2D if grid else C_W1D
            P.dma("pool", lambda e_: e_.dma_start(out=wmat[:], in_=self.cmat[c0:c0 + len(keys)].rearrange("c p t -> p c t")), writes=["p_wm"])
            wpos = {kd: i for i, kd in enumerate(keys)}
            invt = self.sb(ph, "p_inv", [128, nblk, 4], F32)
            P.dma("sp", lambda e_: e_.dma_start(out=invt[:], in_=(self.inv2 if grid else self.inv1)), writes=["p_inv"])
            wg = self.sb(ph, "p_wg", [128, 4, 4, 512], BF16)
            P.dma("pool", lambda e_: e_.dma_start(out=wg[:], in_=self.w_grp.rearrange("g (cc p) d -> p g cc d", p=128)), writes=["p_wg"])
            bg = self.sb(ph, "p_bg", [128, 2, KC], F32)
            scp = self.sb(ph, "p_scp", [128, 2, KC], F32)
            sg1 = self.sb(ph, "p_sg1", [128, KC], F32)
            P.dma("sp", lambda e_: e_.dma_start(out=bg[:], in_=self.b_grp), writes=["p_bg"])
            P.dma("sp", lambda e_: e_.dma_start(out=scp[:], in_=self.sc_pool), writes=["p_scp"])
            P.op("dve", lambda e_: e_.tensor_tensor(out=sg1[:], in0=scp[:, o, :], in1=self.mods[:, L, 32:48, j], op=ALU.mult), reads=["p_scp", "mods"], writes=["p_sg1"])
            TT = min(512, ntok)
            NB = TT // 128
            ztm = [self.sb(ph, f"p_z{i}", [128, D], BF16) for i in range(2)]
            zT = self.sb(ph, "p_zT", [128, KC, TT], BF16)
            xt = self.sb(ph, "p_x3", [128, KC, TT], F32)
            tt_ = [self.sb(ph, f"p_t{i}", [128, TT], F32) for i in range(2)]
            nz = 0
            for t0 in range(0, ntok, TT):
                P.dma("sp", lambda e_, t0=t0: e_.dma_start(out=xt[:], in_=src[:, t0:t0 + TT].rearrange("(kc p) t -> p kc t", p=128)),
                      reads=["dram_x"], writes=["p_x3"])
                for b in range(NB):
                    blk = t0 // 128 + b
                    z_ = ztm[nz % 2]
                    zn = f"p_z{nz % 2}"
                    nz += 1
                    for gi, k in enumerate(POOL_WINDOWS):
                        cols = slice(gi * 512, (gi + 1) * 512)
                        srcs = []
                        for dl in deltas(k):
                            i = blk + dl
                            if 0 <= i < nblk:
                                srcs.append((dl, p_own[:, i, cols], f"p_own_{i}"))
                            elif grid and i < 0:
                                srcs.append((dl, halo[0][:, 4 + i, cols], "p_halo0"))
                            elif grid and i >= nblk:
                                srcs.append((dl, halo[1][:, i - nblk, cols], "p_halo1"))
                        pt, pn = self.next_ps()
                        for si, (dl, ap_, rn) in enumerate(srcs):
                            P.op("pe", lambda e_, pt=pt, ap_=ap_, k=k, dl=dl, si=si, ns=len(srcs): e_.matmul(pt[:, :], lhsT=wmat[:, wpos[(k, dl)], :], rhs=ap_,
                                                                                                     start=(si == 0), stop=(si == ns - 1)),
                                 reads=["p_wm", rn], writes=[pn])
                        P.op("dve", lambda e_, pt=pt, z_=z_, blk=blk, gi=gi, cols=cols: e_.scalar_tensor_tensor(out=z_[:, cols], in0=pt[:, :], scalar=invt[:, blk, gi:gi + 1],
                                                                                                       in1=p_own[:, blk, cols], op0=ALU.mult, op1=ALU.subtract),
                             reads=[pn, "p_inv", f"p_own_{blk}"], writes=[zn])
                    for half in range(2):
                        for c8 in range(8):
                            cc = half * 8 + c8
                            P.op("pe", lambda e_, c8=c8, cc=cc, z_=z_: e_.transpose(out=self.pst[:, c8 * 128:(c8 + 1) * 128], in_=z_[:, cc * 128:(cc + 1) * 128],
                                                                                   identity=self.ident[:]), reads=[zn, "ident"], writes=["pst"])
                        P.op("dve", lambda e_, half=half, b=b: e_.tensor_copy(out=zT[:, half * 8:(half + 1) * 8, b * 128:(b + 1) * 128],
                                                                            in_=self.pst[:, :].rearrange("p (c t) -> p c t", c=8)), reads=["pst"], writes=["p_zT"])
                for gi in range(4):
                    for dc in range(4):
                        m = gi * 4 + dc
                        pt, pn = self.next_ps()
                        for cc in range(4):
                            P.op("pe", lambda e_, pt=pt, gi=gi, cc=cc, dc=dc: e_.matmul(pt[:, :TT], lhsT=wg[:, gi, cc, dc * 128:(dc + 1) * 128], rhs=zT[:, gi * 4 + cc, :],
                                                                                      start=(cc == 0), stop=(cc == 3)), reads=["p_wg", "p_zT"], writes=[pn])
                        t_ = tt_[m % 2]
                        P.op("dve", lambda e_, pt=pt, t_=t_, m=m: e_.tensor_scalar(out=t_[:], in0=pt[:, :TT], scalar1=bg[:, o, m:m + 1], scalar2=sg1[:, m:m + 1],
                                                                                 op0=ALU.add, op1=ALU.mult), reads=[pn, "p_bg", "p_sg1"], writes=[f"p_t{m % 2}"])
                        P.op("pool", lambda e_, t_=t_, m=m: e_.tensor_tensor(out=xt[:, m, :], in0=xt[:, m, :], in1=t_[:], op=ALU.add), reads=[f"p_t{m % 2}", "p_x3"], writes=["p_x3"])
                P.dma("sp", lambda e_, t0=t0: e_.dma_start(out=dst[:, t0:t0 + TT].rearrange("(kc p) t -> p kc t", p=128), in_=xt[:]), reads=["p_x3"], writes=["dram_x"])
            P.barrier()


def _fm(v):
    v = np.asarray(v, np.float32)
    lead = v.shape[:-1]
    return np.ascontiguousarray(np.moveaxis(v.reshape(lead + (KC, 128)), -1, 0))


LAUNCHES = [
    ["adaln_s"],
    [("even_a", 0)],
    [("even_b", 0), ("ffn", 0), ("odd_a", 1)],
    [("odd_b", 1), ("ffn", 1), ("even_a", 2)],
    [("even_b", 2), ("ffn", 2), ("odd_a", 3)],
    [("odd_b", 3), ("ffn", 3), "final"],
]


def needed_inputs(stages):
    if list(stages) == ["adaln_s"]:
        return {"cvec3", "w_ada_sh", "b_ada_sh", "mods_part"}
    need = {"xT", "ctxT", "gmix", "gffn", "gfin", "cmat", "yT"}
    kinds = [t if isinstance(t, str) else t[0] for t in stages]
    if "adaln" in kinds:
        need |= {"cvec", "w_ada", "b_ada", "mods_out"}
    else:
        need |= {"mods_in"}
    if "final" not in kinds:
        need |= {"cres_out"}
    if "ffn" in kinds:
        need |= {"w_up", "w_down"}
    if "even_a" in kinds:
        need |= {"w_in_even", "w_out_even", "lb_logits", "g_hgrn", "wsT", "b_sp", "g_spv", "sc_out", "pack_out", "csc_out"}
    if "even_b" in kinds:
        need |= {"w_out_even", "g_hgrn", "sc_in", "pack_in", "coef", "csc_in"}
    if "odd_a" in kinds:
        need |= {"w_in_pool", "halo_out", "p_out"}
    if "odd_b" in kinds:
        need |= {"halo_in", "p_in", "w_grp", "b_grp", "sc_pool", "inv2", "inv1"}
        if stage_layer(stages, ("odd_b",)) == 1:
            need |= {"w_in_pool"}
    return need


def stage_layer(stages, kind):
    for t in stages:
        if not isinstance(t, str) and t[0] in kind:
            return t[1]
    return None


def make_in_maps(inputs, stages, state):
    f = lambda k: np.ascontiguousarray(np.asarray(inputs[k], np.float32))
    need = needed_inputs(stages)
    if list(stages) == ["adaln_s"]:
        c, c_ctx, w_ada, b_ada = f("c"), f("c_ctx"), np.asarray(inputs["w_ada"], np.float32), f("b_ada")
        cvec3 = np.ascontiguousarray(np.stack([c[0].reshape(KC, 128).T, c[1].reshape(KC, 128).T, c_ctx.reshape(KC, 128).T], axis=-1))
        maps = []
        for core in range(NCORES):
            cols = slice(core * 1536, (core + 1) * 1536)
            maps.append({"cvec3": cvec3, "w_ada_sh": np.ascontiguousarray(w_ada[:, :, cols]),
                         "b_ada_sh": np.ascontiguousarray(b_ada[:, cols].reshape(DEPTH, 12, 128).transpose(2, 0, 1))})
        return maps
    c, c_ctx = f("c"), f("c_ctx")
    shared = {"gmix": _fm(f("g_norm_mix")), "gffn": _fm(f("g_norm_ffn")), "gfin": _fm(f("g_norm_final")), "cmat": make_cmat()}
    if "w_ada" in need:
        shared["w_ada"] = f("w_ada")
        shared["b_ada"] = np.ascontiguousarray(f("b_ada").reshape(DEPTH, 96, 128).transpose(0, 2, 1))
    Lf = stage_layer(stages, ("ffn",))
    if Lf is not None:
        shared["w_up"] = f("w_ffn_up")[Lf]
        shared["w_down"] = f("w_ffn_down")[Lf]
    Le = stage_layer(stages, ("even_a", "even_b"))
    if Le is not None:
        e = Le // 2
        shared["w_out_even"] = f("w_out_even")[e]
        shared["g_hgrn"] = np.ascontiguousarray(f("g_hgrn_out").reshape(2, 8, 128).transpose(2, 0, 1))
        if "w_in_even" in need:
            shared["w_in_even"] = f("w_in_even")[e]
            shared["lb_logits"] = f("lb_logits")
            shared["wsT"] = np.ascontiguousarray(f("w_spatial")[e].transpose(0, 2, 1))
            shared["b_sp"] = f("b_spatial")[e]
            shared["g_spv"] = f("g_spatial_v")[e]
    Lo = stage_layer(stages, ("odd_a", "odd_b"))
    if Lo is not None:
        o = Lo // 2
        if "w_in_pool" in need:
            shared["w_in_pool"] = f("w_in_pool")[o]
        if "w_grp" in need:
            shared["w_grp"] = f("w_grp_pool")[o]
            shared["b_grp"] = _fm(f("b_grp_pool").reshape(2, D))
            shared["sc_pool"] = _fm(f("scale_pool"))
    if "pack_in" in need:
        shared["pack_in"] = state["pack"]
    maps = []
    for core in range(NCORES):
        b, seg = core // 4, core % 4
        m = dict(shared)
        m["xT"] = state["xT"][core]
        m["ctxT"] = state["ctxT"][core]
        if "cvec" in need:
            m["cvec"] = np.ascontiguousarray(np.stack([c[b].reshape(KC, 128).T, c_ctx.reshape(KC, 128).T], axis=-1))
        if "mods_in" in need:
            m["mods_in"] = state["mods"][core]
        if "sc_in" in need:
            m["sc_in"] = state["sc"][core]
            m["csc_in"] = state["csc"][core]
            for k in Builder.SCR_KEYS:
                m["scr_" + k] = state["scr"][core][k]
        if "coef" in need:
            coef = np.zeros((128, 4, NCORES), np.float32)
            for j in range(NCORES):
                if j // 4 == b:
                    coef[:, 0, j] = 1.0 if (j % 4) < seg else 0.0
                    coef[:, 1, j] = 1.0 if (j % 4) > seg else 0.0
                    coef[:, 2, j] = 1.0 if (j % 4) == seg - 1 else 0.0
                    coef[:, 3, j] = 1.0 if (j % 4) == seg + 1 else 0.0
            m["coef"] = coef
        if "p_in" in need:
            m["p_in"] = state["p"][core]
        if "halo_in" in need:
            hal = state["halo"]
            zero = np.zeros_like(hal[core][0:512])
            above = hal[core - 1][512:1024] if seg > 0 else zero
            below = hal[core + 1][0:512] if seg < 3 else zero
            m["halo_in"] = np.ascontiguousarray(np.concatenate([above, below], axis=0))
        if "inv2" in need:
            inv2, inv1 = make_invcnt(seg)
            m["inv2"] = np.ascontiguousarray(inv2.reshape(16, 128, 4).transpose(1, 0, 2))
            m["inv1"] = np.ascontiguousarray(inv1.reshape(2, 128, 4).transpose(1, 0, 2))
        maps.append({k: v for k, v in m.items() if k in need or k.startswith("scr_")})
    return maps


def init_state(inputs, x_override=None, ctx_override=None):
    x = np.asarray(inputs["x"], np.float32) if x_override is None else x_override
    ctx = np.asarray(inputs["ctx"], np.float32) if ctx_override is None else ctx_override
    st = {"xT": [], "ctxT": []}
    for core in range(NCORES):
        b, seg = core // 4, core % 4
        st["xT"].append(np.ascontiguousarray(x[b, seg * NT:(seg + 1) * NT, :].T))
        st["ctxT"].append(np.ascontiguousarray(ctx[b].T))
    return st


def run_launch(inputs, stages, state):
    import time
    t0 = time.time()
    nc = Builder(stages).build()
    maps = make_in_maps(inputs, stages, state)
    print("launch", stages, "build+maps s", round(time.time() - t0, 1), flush=True)
    res = run_bass_kernel_spmd(nc, maps, core_ids=list(range(NCORES)))
    print("launch done s", round(time.time() - t0, 1), flush=True)
    r = res.results
    if "mods_part" in r[0]:
        full = np.concatenate([np.asarray(r[c]["mods_part"]) for c in range(NCORES)], axis=2)
        state["mods"] = [np.ascontiguousarray(full[:, :, :, [core // 4, 2]]) for core in range(NCORES)]
        return res
    state["dbg"] = {k: np.asarray(v) for k, v in r[0].items() if k.startswith("dbg_")}
    state["xT"] = [np.asarray(r[c]["yT"]) for c in range(NCORES)]
    if "cres_out" in r[0]:
        state["ctxT"] = [np.asarray(r[c]["cres_out"]) for c in range(NCORES)]
    if "mods_out" in r[0]:
        state["mods"] = [np.asarray(r[c]["mods_out"]) for c in range(NCORES)]
    if "sc_out" in r[0]:
        state["sc"] = [np.asarray(r[c]["sc_out"]) for c in range(NCORES)]
        state["pack"] = np.concatenate([np.asarray(r[c]["pack_out"]) for c in range(NCORES)], axis=0)
        state["csc"] = [np.asarray(r[c]["csc_out"]) for c in range(NCORES)]
        state["scr"] = [{k: np.asarray(r[c]["scr_" + k]) for k in Builder.SCR_KEYS} for c in range(NCORES)]
    if "p_out" in r[0]:
        state["p"] = [np.asarray(r[c]["p_out"]) for c in range(NCORES)]
    if "halo_out" in r[0]:
        state["halo"] = [np.asarray(r[c]["halo_out"]) for c in range(NCORES)]
    return res


def assemble(state):
    out = np.zeros((2, 4 * NT, D), np.float32)
    for core in range(NCORES):
        b, seg = core // 4, core % 4
        out[b, seg * NT:(seg + 1) * NT, :] = state["xT"][core].T
    return out


def kernel(**inputs):
    state = init_state(inputs)
    for stages in LAUNCHES:
        run_launch(inputs, stages, state)
    return assemble(state)
```
